# Optimizing a Trainium2 kernel written in Bass

```python
import jax, jax.numpy as jnp
from jax import lax
import numpy as np

D_MODEL = 1024
BATCH = 2
SEQ = 16384
DEPTH = 2

CHUNK = 64
N_MIXERS = 4
GROUP_WIDTH = D_MODEL // N_MIXERS
HEAD_DIM = 64
N_HEADS = GROUP_WIDTH // HEAD_DIM
RWKV_W_LORA = 64
RWKV_A_LORA = 64
RWKV_GN_EPS = 64e-5
POOL_WINDOWS = (2, 4, 8, 16)
POOL_GROUP = GROUP_WIDTH // len(POOL_WINDOWS)
MLSTM_CONV = 4
ROPE_BASE = 10000.0
D_PLE = 256
LN_EPS = 1e-5
HEAD_NORM_EPS = 1e-6
DEEPNORM_ALPHA = (2.0 * DEPTH) ** 0.25
DEEPNORM_BETA = (8.0 * DEPTH) ** -0.25
RWKV_SHIFT_COLS = 3 * GROUP_WIDTH + RWKV_W_LORA + RWKV_A_LORA
COL_SIZES = (RWKV_SHIFT_COLS, GROUP_WIDTH,
             GROUP_WIDTH, GROUP_WIDTH,
             2 * GROUP_WIDTH, GROUP_WIDTH, N_HEADS, N_HEADS,
             GROUP_WIDTH, GROUP_WIDTH,
             GROUP_WIDTH, GROUP_WIDTH, GROUP_WIDTH, GROUP_WIDTH)
N_COLS = sum(COL_SIZES)

kernel_name = "hymba_style_rwkv7_pool_mlstm_retnet_deepnorm"


def _split_columns(z):
    idx = np.cumsum(COL_SIZES)[:-1].tolist()
    return jnp.split(z, idx, axis=-1)


def _layer_norm(x, g, b, eps):
    xf = x.astype(jnp.float32)
    mu = jnp.mean(xf, -1, keepdims=True)
    var = jnp.mean(jnp.square(xf - mu), -1, keepdims=True)
    return (xf - mu) * lax.rsqrt(var + eps) * g + b


def _head_norm(y, eps):
    y = y.astype(jnp.float32)
    mu = jnp.mean(y, -1, keepdims=True)
    var = jnp.mean(jnp.square(y - mu), -1, keepdims=True)
    return (y - mu) * lax.rsqrt(var + eps)


def _rotary(z, cos, sin):
    z1, z2 = z[..., ::2], z[..., 1::2]
    c, s = cos[:, None, :], sin[:, None, :]
    return jnp.stack([z1 * c - z2 * s, z1 * s + z2 * c], axis=-1).reshape(z.shape)


def _causal_dwconv(z, w):
    kw = w.shape[0]
    zp = jnp.pad(z, ((0, 0), (kw - 1, 0), (0, 0)))
    return lax.conv_general_dilated(zp, w[:, None, :].astype(z.dtype), (1,), 'VALID',
                                    dimension_numbers=('NWC', 'WIO', 'NWC'),
                                    feature_group_count=z.shape[-1])


def rwkv7_mix(z, mu, w0, w2, a0, a2, k_k, k_a, r_k, ln_w, ln_b):
    f32 = jnp.float32
    B, S, _ = z.shape
    G = GROUP_WIDTH
    prev = jnp.pad(z, ((0, 0), (1, 0), (0, 0)))[:, :-1]
    z = z + (prev - z) * mu
    r, k, v, wl, al = jnp.split(z, [G, 2 * G, 3 * G, 3 * G + RWKV_W_LORA], axis=-1)
    w_log = -jax.nn.softplus(-(w0 + jnp.tanh(wl) @ w2).astype(f32)) - 0.5
    decay = jnp.exp(-jnp.exp(w_log))
    a = jax.nn.sigmoid((a0 + al @ a2).astype(f32))
    heads = lambda t: t.astype(f32).reshape(B, S, N_HEADS, HEAD_DIM)
    kk = heads(k * k_k)
    kk = kk / jnp.maximum(jnp.sqrt(jnp.sum(kk * kk, -1, keepdims=True)), 1e-12)
    r, v, decay, a = heads(r), heads(v), heads(decay), heads(a)
    k = heads(k) * (1.0 + (a - 1.0) * k_a.astype(f32).reshape(N_HEADS, HEAD_DIM))

    def step(state, inp):
        w_t, r_t, k_t, v_t, kk_t, ab_t = inp
        sa = jnp.einsum('bhvk,bhk->bhv', state, -kk_t)
        state = (state * w_t[:, :, None, :] + sa[..., None] * ab_t[:, :, None, :]
                 + v_t[..., None] * k_t[:, :, None, :])
        return state, jnp.einsum('bhvk,bhk->bhv', state, r_t)

    s0 = jnp.zeros((B, N_HEADS, HEAD_DIM, HEAD_DIM), f32)
    xs = tuple(jnp.moveaxis(t, 1, 0) for t in (decay, r, k, v, kk, kk * a))
    _, y = lax.scan(step, s0, xs)
    y = jnp.moveaxis(y, 0, 1)
    y = (_head_norm(y, RWKV_GN_EPS) * ln_w.astype(f32).reshape(N_HEADS, HEAD_DIM)
         + ln_b.astype(f32).reshape(N_HEADS, HEAD_DIM))
    bonus = jnp.sum(r * k * r_k.astype(f32).reshape(N_HEADS, HEAD_DIM), -1, keepdims=True) * v
    return (y + bonus).reshape(B, S, GROUP_WIDTH)


def pool_mix(u, w_grp, scale):
    f32 = jnp.float32
    B, S, _ = u.shape
    uf = u.astype(f32)
    cs = jnp.pad(jnp.cumsum(uf, axis=1), ((0, 0), (1, 0), (0, 0)))
    hi = cs[:, 1:]
    t1 = jnp.arange(1, S + 1, dtype=f32)[None, :, None]
    means = []
    for gi, win in enumerate(POOL_WINDOWS):
        sl = slice(gi * POOL_GROUP, (gi + 1) * POOL_GROUP)
        lo = jnp.pad(cs[:, :S + 1 - win, sl], ((0, 0), (win - 1, 0), (0, 0)))
        means.append((hi[..., sl] - lo) / jnp.minimum(t1, float(win)))
    d = (jnp.concatenate(means, -1) - uf).reshape(B, S, len(POOL_WINDOWS), POOL_GROUP)
    y = jnp.einsum('bsgc,gcd->bsgd', d, w_grp.astype(f32)).reshape(B, S, GROUP_WIDTH)
    return y * scale.astype(f32)


def mlstm_mix(qk, v, ig_pre, fg_pre, o_pre, conv_w, i_bias, f_bias, ln_w):
    f32 = jnp.float32
    B, S, _ = v.shape
    nc = S // CHUNK
    qk = jax.nn.silu(_causal_dwconv(qk, conv_w).astype(f32))
    q, k = jnp.split(qk, 2, axis=-1)
    chunked = lambda t: t.astype(f32).reshape(B, nc, CHUNK, N_HEADS, HEAD_DIM).transpose(0, 3, 1, 2, 4)
    gate_chunked = lambda t: t.astype(f32).reshape(B, nc, CHUNK, N_HEADS).transpose(0, 3, 1, 2)
    q, k, v = chunked(q), chunked(k) * HEAD_DIM ** -0.5, chunked(v)
    log_i = gate_chunked(ig_pre + i_bias)
    log_f = jax.nn.log_sigmoid(gate_chunked(fg_pre + f_bias))
    b = jnp.cumsum(log_f, axis=-1)
    causal = jnp.tril(jnp.ones((CHUNK, CHUNK), bool))
    d_log = jnp.where(causal, b[..., :, None] - b[..., None, :] + log_i[..., None, :], -jnp.inf)
    m_intra = jnp.max(d_log, -1)
    g_log = b[..., -1:] - b + log_i
    m_chunk = jnp.max(g_log, -1)
    wgt = jnp.exp(g_log - m_chunk[..., None])[..., None]
    c_chunk = jnp.einsum('bhclv,bhclk->bhcvk', v * wgt, k)
    n_chunk = jnp.sum(k * wgt, axis=3)
    f_chunk = b[..., -1]

    def step(carry, inp):
        C, n, m = carry
        cc, ncn, mc, fc = inp
        m_new = jnp.maximum(fc + m, mc)
        s_old = jnp.exp(fc + m - m_new)
        s_new = jnp.exp(mc - m_new)
        carry_new = (s_old[..., None, None] * C + s_new[..., None, None] * cc,
                     s_old[..., None] * n + s_new[..., None] * ncn, m_new)
        return carry_new, (C, n, m)

    init = (jnp.zeros((B, N_HEADS, HEAD_DIM, HEAD_DIM), f32),
            jnp.zeros((B, N_HEADS, HEAD_DIM), f32), jnp.zeros((B, N_HEADS), f32))
    xs = tuple(jnp.moveaxis(t, 2, 0) for t in (c_chunk, n_chunk, m_chunk, f_chunk))
    _, (c_prev, n_prev, m_prev) = lax.scan(step, init, xs)
    c_prev, n_prev, m_prev = (jnp.moveaxis(t, 0, 2) for t in (c_prev, n_prev, m_prev))
    m_inter = b + m_prev[..., None]
    m_t = jnp.maximum(m_inter, m_intra)
    s_inter = jnp.exp(m_inter - m_t)
    scores = jnp.einsum('bhcld,bhcsd->bhcls', q, k) * jnp.exp(d_log - m_t[..., None])
    num = (jnp.einsum('bhcls,bhcsv->bhclv', scores, v)
           + s_inter[..., None] * jnp.einsum('bhcvk,bhclk->bhclv', c_prev, q))
    den = jnp.sum(scores, -1) + s_inter * jnp.einsum('bhck,bhclk->bhcl', n_prev, q)
    h = num / jnp.maximum(jnp.abs(den), jnp.exp(-m_t))[..., None]
    h = h.transpose(0, 2, 3, 1, 4).reshape(B, S, GROUP_WIDTH) * jax.nn.sigmoid(o_pre.astype(f32))
    h = _head_norm(h.reshape(B, S, N_HEADS, HEAD_DIM), HEAD_NORM_EPS) * ln_w.astype(f32).reshape(N_HEADS, HEAD_DIM)
    return h.reshape(B, S, GROUP_WIDTH)


def retention_mix(q, k, v, cos, sin):
    f32 = jnp.float32
    B, S, _ = q.shape
    nc = S // CHUNK
    to_heads = lambda t: t.astype(f32).reshape(B, S, N_HEADS, HEAD_DIM)
    q = _rotary(to_heads(q), cos, sin)
    k = _rotary(to_heads(k), cos, sin) * HEAD_DIM ** -0.5
    v = to_heads(v)
    chunked = lambda t: t.reshape(B, nc, CHUNK, N_HEADS, HEAD_DIM).transpose(0, 3, 1, 2, 4)
    q, k, v = chunked(q), chunked(k), chunked(v)
    log_g = jnp.log1p(-jnp.exp2(-5.0 - jnp.arange(N_HEADS, dtype=f32)))
    pos = jnp.arange(CHUNK, dtype=f32)
    rel = pos[:, None] - pos[None, :]
    dec = jnp.where(rel >= 0, jnp.exp(log_g[:, None, None] * jnp.maximum(rel, 0.0)), 0.0)
    scores = jnp.einsum('bhcld,bhcsd->bhcls', q, k) * dec[None, :, None]
    intra = jnp.einsum('bhcls,bhcsv->bhclv', scores, v)
    zeta = jnp.exp(log_g[:, None] * (CHUNK - 1.0 - pos))
    r_chunk = jnp.einsum('bhclk,bhclv->bhckv', k * zeta[None, :, None, :, None], v)
    g_chunk = jnp.exp(log_g * CHUNK)[None, :, None, None]

    def step(R, rc):
        return g_chunk * R + rc, R

    _, r_prev = lax.scan(step, jnp.zeros((B, N_HEADS, HEAD_DIM, HEAD_DIM), f32),
                         jnp.moveaxis(r_chunk, 2, 0))
    r_prev = jnp.moveaxis(r_prev, 0, 2)
    xi = jnp.exp(log_g[:, None] * (pos + 1.0))
    inter = jnp.einsum('bhcld,bhcdv->bhclv', q, r_prev) * xi[None, :, None, :, None]
    y = (intra + inter).transpose(0, 2, 3, 1, 4).reshape(B, S, N_HEADS, HEAD_DIM)
    return _head_norm(y, HEAD_NORM_EPS).reshape(B, S, GROUP_WIDTH)


def setup_inputs(seed: int = 0) -> dict:
    key = jax.random.key(seed)
    ks = jax.random.split(key, 32)
    f32 = jnp.float32
    G = GROUP_WIDTH
    nrm = lambda k, shape, s: s * jax.random.normal(k, shape, f32)
    return {
        "x": jax.random.normal(ks[0], (BATCH, SEQ, D_MODEL), f32),
        "p": jax.random.normal(ks[1], (DEPTH, BATCH, SEQ, D_PLE), f32),
        "w_in": nrm(ks[2], (DEPTH, D_MODEL, N_COLS), D_MODEL ** -0.5),
        "rw_mu": jax.random.uniform(ks[3], (DEPTH, RWKV_SHIFT_COLS), f32),
        "rw_w0": jax.random.uniform(ks[4], (DEPTH, G), f32, -5.0, 0.0),
        "rw_w2": nrm(ks[5], (DEPTH, RWKV_W_LORA, G), 0.1 * RWKV_W_LORA ** -0.5),
        "rw_a0": nrm(ks[6], (DEPTH, G), 0.1),
        "rw_a2": nrm(ks[7], (DEPTH, RWKV_A_LORA, G), RWKV_A_LORA ** -0.5),
        "rw_kk": 0.85 + nrm(ks[8], (DEPTH, G), 0.02),
        "rw_ka": 1.0 + nrm(ks[9], (DEPTH, G), 0.02),
        "rw_rk": nrm(ks[10], (DEPTH, G), 0.1),
        "rw_ln_w": 1.0 + nrm(ks[11], (DEPTH, G), 0.02),
        "rw_ln_b": nrm(ks[12], (DEPTH, G), 0.02),
        "pl_w": nrm(ks[13], (DEPTH, len(POOL_WINDOWS), POOL_GROUP, POOL_GROUP), POOL_GROUP ** -0.5),
        "pl_scale": 1.0 + nrm(ks[14], (DEPTH, G), 0.02),
        "ml_conv": nrm(ks[15], (DEPTH, MLSTM_CONV, 2 * G), MLSTM_CONV ** -0.5),
        "ml_ib": nrm(ks[16], (DEPTH, N_HEADS), 0.1),
        "ml_fb": jnp.linspace(3.0, 6.0, N_HEADS, dtype=f32)[None, :] + nrm(ks[17], (DEPTH, N_HEADS), 0.1),
        "ml_ln_w": 1.0 + nrm(ks[18], (DEPTH, G), 0.02),
        "w_out": nrm(ks[19], (DEPTH, D_MODEL, D_MODEL), D_MODEL ** -0.5 * DEEPNORM_BETA),
        "ple_w": nrm(ks[20], (DEPTH, D_PLE, D_MODEL), D_PLE ** -0.5 * DEEPNORM_BETA),
        "ple_gate": nrm(ks[21], (DEPTH, D_MODEL, D_MODEL), D_MODEL ** -0.5),
        "ln_g": 1.0 + nrm(ks[22], (DEPTH, D_MODEL), 0.02),
        "ln_b": nrm(ks[23], (DEPTH, D_MODEL), 0.02),
    }


def reference(x, p, w_in, rw_mu, rw_w0, rw_w2, rw_a0, rw_a2, rw_kk, rw_ka, rw_rk, rw_ln_w,
              rw_ln_b, pl_w, pl_scale, ml_conv, ml_ib, ml_fb, ml_ln_w, w_out, ple_w, ple_gate,
              ln_g, ln_b):
    dtype = x.dtype
    S = x.shape[1]
    pos = jnp.arange(S, dtype=jnp.float32)
    inv_freq = ROPE_BASE ** (-jnp.arange(0, HEAD_DIM, 2, dtype=jnp.float32) / HEAD_DIM)
    ang = pos[:, None] * inv_freq[None, :]
    cos, sin = jnp.cos(ang), jnp.sin(ang)
    for i in range(DEPTH):
        z = x @ w_in[i]
        (rw_in, rw_g, pl_u, pl_g, ml_qk, ml_v, ml_i, ml_f, ml_o, ml_g,
         rt_q, rt_k, rt_v, rt_g) = _split_columns(z)
        y_a = rwkv7_mix(rw_in, rw_mu[i], rw_w0[i], rw_w2[i], rw_a0[i], rw_a2[i], rw_kk[i],
                        rw_ka[i], rw_rk[i], rw_ln_w[i], rw_ln_b[i]) * jax.nn.silu(rw_g)
        y_b = pool_mix(pl_u, pl_w[i], pl_scale[i]) * jax.nn.silu(pl_g)
        y_c = mlstm_mix(ml_qk, ml_v, ml_i, ml_f, ml_o, ml_conv[i], ml_ib[i], ml_fb[i],
                        ml_ln_w[i]) * jax.nn.silu(ml_g)
        y_d = retention_mix(rt_q, rt_k, rt_v, cos, sin) * jax.nn.silu(rt_g)
        mixed = jnp.concatenate([y_a, y_b, y_c, y_d], axis=-1).astype(dtype)
        ple = jax.nn.sigmoid(x @ ple_gate[i]) * (p[i] @ ple_w[i])
        x = _layer_norm(DEEPNORM_ALPHA * x + mixed @ w_out[i] + ple, ln_g[i], ln_b[i], LN_EPS).astype(dtype)
    return x
```

```python
import contextlib
import math
import numpy as np
import concourse.bass as bass
import concourse.mybir as mybir
from concourse.bass_utils import run_bass_kernel_spmd

F32 = mybir.dt.float32
BF16 = mybir.dt.bfloat16
AF = mybir.ActivationFunctionType
ALU = mybir.AluOpType
AX = mybir.AxisListType

D = 1024
G = 256
NSEG = 4
SEG = 4096
HALO = 16
TT = 128
FMC = 1408
TMC = 2568
C0 = math.exp(-0.5)
ALPHA = (2.0 * 2) ** 0.25
RW_EPS = 64e-5
HN_EPS = 1e-6
LN_EPS = 1e-5
SAME_ENGINE_SYNC = ("act", "dve", "pool")
RDT = BF16


class Ref:
    __slots__ = ("ts", "ap")

    def __init__(self, ts, ap):
        self.ts = ts
        self.ap = ap

    def re(self, pat, **kw):
        return Ref(self.ts, self.ap.rearrange(pat, **kw))

    def bc(self, shape):
        return Ref(self.ts, self.ap.broadcast_to(list(shape)))

    def __getitem__(self, idx):
        return Ref(self.ts, self.ap[idx])


class T:
    __slots__ = ("h", "hb", "last_w", "readers", "name")

    def __init__(self, h, name="", hb=None):
        self.h = h
        self.hb = hb
        self.last_w = None
        self.readers = []
        self.name = name

    def __getitem__(self, idx):
        return Ref((self,), self.h[idx])

    def b16(self, idx):
        return Ref((self,), self.hb[idx])

    def view(self, ap):
        return View((self,), ap)


class View:
    __slots__ = ("ts", "ap")

    def __init__(self, ts, ap):
        self.ts = ts
        self.ap = ap

    def __getitem__(self, idx):
        return Ref(self.ts, self.ap[idx])


class Sched:
    ENG = ("pe", "act", "dve", "pool", "sp")

    def __init__(self, nc, stack):
        self.nc = nc
        self.base = stack
        self.stack = stack
        self.q = {e: [] for e in self.ENG}
        self.cnt = {}
        self.sem = {}
        self.waited = {e: {} for e in self.ENG}
        for e in ("pe", "act", "dve", "pool"):
            self.sem[e] = stack.enter_context(nc.semaphore("sem_" + e))
            self.cnt[e] = 0
        self.ndma_sem = 0
        self.nt = 0
        self.ninst = 0

    def sbuf(self, shape, dt, name=None):
        self.nt += 1
        name = f"{name or 't'}_{self.nt}"
        h = self.stack.enter_context(self.nc.sbuf_tensor(name, list(shape), dt))
        return T(h, name)

    def slot(self, name=None):
        self.nt += 1
        name = f"{name or 's'}_{self.nt}"
        h = self.stack.enter_context(self.nc.sbuf_tensor(name, [128, 512], F32))
        return T(h, name, hb=h.bitcast(BF16))

    def dma_sem(self):
        self.ndma_sem += 1
        key = f"dma{self.ndma_sem}"
        self.sem[key] = self.base.enter_context(self.nc.semaphore("sem_" + key))
        self.cnt[key] = 0
        return key

    def _need(self, eng, tok, waits):
        if tok is None:
            return
        key, val = tok
        if key == eng and eng not in SAME_ENGINE_SYNC:
            return
        if self.waited[eng].get(key, 0) >= val:
            return
        waits[key] = max(waits.get(key, 0), val)

    def _deps(self, eng, reads, writes):
        waits = {}
        for t in reads:
            self._need(eng, t.last_w, waits)
        for t in writes:
            self._need(eng, t.last_w, waits)
            for r in t.readers:
                self._need(eng, r, waits)
        for key, val in waits.items():
            self.waited[eng][key] = val
        return list(waits.items())

    def _record(self, tok, reads, writes):
        for t in writes:
            t.last_w = tok
            t.readers = []
        for t in reads:
            if t in writes:
                continue
            t.readers = [r for r in t.readers if r[0] != tok[0]] + [tok]

    def op(self, eng, fn, reads=(), writes=()):
        reads = list(dict.fromkeys(reads))
        writes = list(dict.fromkeys(writes))
        waits = self._deps(eng, reads, writes)
        self.cnt[eng] += 1
        self.ninst += 1
        tok = (eng, self.cnt[eng])
        sem = self.sem[eng]
        sems = self.sem

        def run(e, waits=waits, fn=fn, sem=sem):
            for key, val in waits:
                e.wait_ge(sems[key], val)
            fn(e).then_inc(sem, 1)
        self.q[eng].append(run)
        self._record(tok, reads, writes)
        return tok

    def dma(self, qeng, semkey, out_ap, in_ap, reads=(), writes=()):
        reads = list(dict.fromkeys(reads))
        writes = list(dict.fromkeys(writes))
        waits = self._deps(qeng, reads, writes)
        if semkey is None:
            if not hasattr(self, "pool_keys"):
                self.pool_keys = [self.dma_sem() for _ in range(16)]
                self.pool_i = 0
            semkey = self.pool_keys[self.pool_i % len(self.pool_keys)]
            self.pool_i += 1
            prev = self.cnt[semkey]
            if prev > 0 and self.waited[qeng].get(semkey, 0) < prev:
                waits = [w for w in waits if w[0] != semkey] + [(semkey, prev)]
                self.waited[qeng][semkey] = prev
        self.cnt[semkey] += 16
        tok = (semkey, self.cnt[semkey])
        sem = self.sem[semkey]
        sems = self.sem

        def run(e, waits=waits, sem=sem, out_ap=out_ap, in_ap=in_ap):
            for key, val in waits:
                e.wait_ge(sems[key], val)
            e.dma_start(out=out_ap, in_=in_ap).then_inc(sem, 16)
        self.q[qeng].append(run)
        self._record(tok, reads, writes)
        return tok

    def barrier(self):
        snap = dict(self.cnt)
        sems = self.sem
        for eng in self.ENG:
            waits = []
            for key, val in snap.items():
                if val > 0 and self.waited[eng].get(key, 0) < val:
                    waits.append((key, val))
                    self.waited[eng][key] = val

            def run(e, waits=waits):
                for key, val in waits:
                    e.wait_ge(sems[key], val)
            self.q[eng].append(run)

    def emit(self):
        nc = self.nc
        q = self.q
        self.q = {e: [] for e in self.ENG}
        with nc.Block() as block:
            @block.tensor
            def _(e):
                for f in q["pe"]:
                    f(e)

            @block.scalar
            def _(e):
                for f in q["act"]:
                    f(e)

            @block.vector
            def _(e):
                for f in q["dve"]:
                    f(e)

            @block.gpsimd
            def _(e):
                for f in q["pool"]:
                    f(e)

            @block.sync
            def _(e):
                for f in q["sp"]:
                    f(e)


def _ts(*refs):
    out = []
    for r in refs:
        if isinstance(r, Ref):
            out.extend(r.ts)
    return out


class Prog:
    def __init__(self, nc, stack):
        self.nc = nc
        self.S = Sched(nc, stack)
        ph = stack.enter_context(nc.psum_tensor("psum_all", [128, 8, 512], F32))
        phb = ph.bitcast(BF16)
        self.ph = ph
        self.banks = [T(ph, f"bank{j}", hb=phb) for j in range(8)]
        self.bank_i = 0

    def bank(self):
        j = self.bank_i % 8
        self.bank_i += 1
        return j, self.banks[j]

    def bank2(self):
        if self.bank_i % 2:
            self.bank_i += 1
        j = self.bank_i % 8
        self.bank_i += 2
        return j, View((self.banks[j], self.banks[j + 1]), self.ph[:, j:j + 2, :])

    def mm(self, out, lhsT, rhs, start=True, stop=True):
        self.S.op("pe", lambda e: e.matmul(out.ap, lhsT=lhsT.ap, rhs=rhs.ap, start=start, stop=stop),
                  reads=_ts(lhsT, rhs), writes=_ts(out))

    def tr(self, out, in_, ident):
        self.S.op("pe", lambda e: e.transpose(out=out.ap, in_=in_.ap, identity=ident.ap),
                  reads=_ts(in_, ident), writes=_ts(out))

    def act(self, out, in_, func, bias=None, scale=None):
        kw = {}
        if bias is not None:
            kw["bias"] = bias.ap if isinstance(bias, Ref) else bias
        if scale is not None:
            kw["scale"] = scale.ap if isinstance(scale, Ref) else scale
        self.S.op("act", lambda e: e.activation(out=out.ap, in_=in_.ap, func=func, **kw),
                  reads=_ts(in_, bias, scale), writes=_ts(out))

    def cp(self, eng, out, in_):
        if eng == "act":
            self.S.op("act", lambda e: e.copy(out=out.ap, in_=in_.ap), reads=_ts(in_), writes=_ts(out))
        else:
            self.S.op(eng, lambda e: e.tensor_copy(out=out.ap, in_=in_.ap), reads=_ts(in_), writes=_ts(out))

    def tt(self, eng, out, a, b, op):
        self.S.op(eng, lambda e: e.tensor_tensor(out=out.ap, in0=a.ap, in1=b.ap, op=op),
                  reads=_ts(a, b), writes=_ts(out))

    def ts(self, eng, out, a, s1, s2, op0, op1=None):
        v1 = s1.ap if isinstance(s1, Ref) else s1
        v2 = s2.ap if isinstance(s2, Ref) else s2
        if op1 is None:
            self.S.op(eng, lambda e: e.tensor_scalar(out=out.ap, in0=a.ap, scalar1=v1, scalar2=None, op0=op0),
                      reads=_ts(a, s1), writes=_ts(out))
        else:
            self.S.op(eng, lambda e: e.tensor_scalar(out=out.ap, in0=a.ap, scalar1=v1, scalar2=v2, op0=op0, op1=op1),
                      reads=_ts(a, s1, s2), writes=_ts(out))

    def stt(self, out, a, s, b, op0, op1):
        v = s.ap if isinstance(s, Ref) else s
        self.S.op("dve", lambda e: e.scalar_tensor_tensor(out=out.ap, in0=a.ap, scalar=v, in1=b.ap, op0=op0, op1=op1),
                  reads=_ts(a, s, b), writes=_ts(out))

    def red(self, out, in_, op=ALU.add):
        self.S.op("dve", lambda e: e.tensor_reduce(out=out.ap, in_=in_.ap, axis=AX.X, op=op),
                  reads=_ts(in_), writes=_ts(out))

    def scan_add(self, out, ones, data):
        self.S.op("dve", lambda e: e.tensor_tensor_scan(out=out.ap, data0=ones.ap, data1=data.ap, initial=0.0,
                                                        op0=ALU.mult, op1=ALU.add),
                  reads=_ts(ones, data), writes=_ts(out))

    def bn_stats(self, out, in_):
        self.S.op("dve", lambda e: e.bn_stats(out=out.ap, in_=in_.ap), reads=_ts(in_), writes=_ts(out))

    def bn_aggr(self, out, in_):
        self.S.op("dve", lambda e: e.bn_aggr(out=out.ap, in_=in_.ap), reads=_ts(in_), writes=_ts(out))

    def recip(self, out, in_):
        self.S.op("dve", lambda e: e.reciprocal(out=out.ap, in_=in_.ap), reads=_ts(in_), writes=_ts(out))

    def memset(self, eng, out, val):
        self.S.op(eng, lambda e: e.memset(out.ap, val), reads=[], writes=_ts(out))

    def load(self, semkey, out, src, q="sp"):
        if isinstance(src, Ref):
            return self.S.dma(q, semkey, out.ap, src.ap, reads=_ts(src), writes=_ts(out))
        return self.S.dma(q, semkey, out.ap, src, writes=_ts(out))

    def store(self, semkey, dst, in_, q="sp"):
        if isinstance(dst, Ref):
            return self.S.dma(q, semkey, dst.ap, in_.ap, reads=_ts(in_), writes=_ts(dst))
        return self.S.dma(q, semkey, dst, in_.ap, reads=_ts(in_))


def build_program(mode, NT):
    P2 = mode == "P2"
    nc = bass.Bass("TRN2", target_bir_lowering=False)
    NTOK = NT * TT

    def din(name, shape):
        return nc.dram_tensor(name, list(shape), F32, kind="ExternalInput").ap()

    def dout(name, shape):
        return nc.dram_tensor(name, list(shape), F32, kind="ExternalOutput").ap()

    d_xh = din("xh", [HALO + NTOK, D])
    d_wfm = din("wfm", [D, FMC])
    d_wtm = din("wtm", [D, TMC])
    d_rwp = din("rwp", [64, 30])
    d_w2 = din("w2", [64, G])
    d_a2 = din("a2", [64, G])
    d_row = din("rowp", [4360])
    d_conv = din("convw", [64, 2, 4, 4])
    d_plw = din("plw", [128, 2, 128])
    d_ident = din("ident", [128, 128])
    d_mle = din("mask_le", [128, 128])
    d_mgt = din("mask_gt", [128, 128])
    d_mG = din("mask_g", [64, 128])
    d_dec = din("dect", [128, 4, 128])
    d_zx = din("zetaxi", [128, 8])
    d_gb = din("gchunk", [64, 4, 64])
    d_cs = din("cs2", [NTOK, 128])
    d_cnt0 = din("cnt0", [128, 2, 128])
    if P2:
        d_ssg = din("sum_sg", [NSEG, 64, 4, 128])
        d_sml = din("sum_ml", [NSEG, 64, 4, 65])
        d_sbt = din("sum_bt", [NSEG, 64, 4])
        d_srt = din("sum_rt", [NSEG, 64, 4, 64])
        d_msk = din("seg_mask", [64, NSEG])
        d_gseg = din("gseg", [64, 4, 64])
        d_p = din("p", [NTOK, 256])
        d_wout = din("w_out", [D, D])
        d_plew = din("ple_w", [256, D])
        d_pleg = din("ple_gate", [D, D])
        d_out = dout("xout", [NTOK, D])
        h_mix = nc.dram_tensor("mix_scratch", [NT, 128, 1024], BF16, kind="Internal")
        h_xT = nc.dram_tensor("xT_scratch", [NT, 128, 1024], BF16, kind="Internal")
        t_mix = T(h_mix.ap(), "mix_scratch")
        t_xTs = T(h_xT.ap(), "xT_scratch")
    else:
        d_osg = dout("o_sg", [64, 4, 128])
        d_oml = dout("o_ml", [64, 4, 65])
        d_obt = dout("o_bt", [64, 4])
        d_ort = dout("o_rt", [64, 4, 64])

    R_MUV, R_LNW, R_LNB, R_PLS, R_MLLN, R_LNG, R_LNBB, R_IB, R_OMUV = 0, 256, 512, 768, 1024, 1280, 2304, 3328, 3336
    out_toks = []

    with contextlib.ExitStack() as st:
        Pg = Prog(nc, st)
        S = Pg.S
        mm, tr, act, cp, tt, ts, stt, red = Pg.mm, Pg.tr, Pg.act, Pg.cp, Pg.tt, Pg.ts, Pg.stt, Pg.red
        sb = S.sbuf
        ld_x = [S.dma_sem(), S.dma_sem()]
        ld_p = [S.dma_sem(), S.dma_sem()]
        ld_m = [S.dma_sem(), S.dma_sem()]
        ld_t = [S.dma_sem(), S.dma_sem()]
        ld_cs = [S.dma_sem(), S.dma_sem()]
        st_sx = S.dma_sem()
        st_sm = S.dma_sem()
        st_o = [S.dma_sem(), S.dma_sem()]

        ident = sb([128, 128], F32, "ident")
        identb = sb([128, 128], BF16, "identb")
        Pg.load(None, ident[:], d_ident)
        cp("dve", identb[:], ident[:])

        with contextlib.ExitStack() as stM:
            S.stack = stM
            mle = sb([128, 128], F32, "mle")
            mgt = sb([128, 128], F32, "mgt")
            mG = sb([64, 128], F32, "mG")
            dect = sb([128, 4, 128], F32, "dect")
            zx = sb([128, 8], F32, "zx")
            gb = sb([64, 4, 64], F32, "gb")
            cnt0 = sb([128, 2, 128], F32, "cnt0")
            rwp = sb([64, 48], F32, "rwp")
            w2 = sb([64, G], F32, "w2")
            a2 = sb([64, G], F32, "a2")
            rowp = sb([128, 1280 + 8 + 256 + 8], F32, "rowp")
            convw = sb([64, 2, 4, 4], F32, "convw")
            plwb = sb([128, 2, 128], BF16, "plwb")
            ones = sb([128, 128], F32, "ones")
            sm = sb([128, 96], F32, "sm")
            RM_IB, RM_OMUV = 1280, 1288
            Pg.load(None, mle[:], d_mle)
            Pg.load(None, mgt[:], d_mgt)
            Pg.load(None, mG[:], d_mG)
            Pg.load(None, dect[:], d_dec)
            Pg.load(None, zx[:], d_zx)
            Pg.load(None, gb[:], d_gb)
            Pg.load(None, cnt0[:], d_cnt0)
            Pg.load(None, rwp[:, 0:30], d_rwp)
            Pg.load(None, w2[:], d_w2)
            Pg.load(None, a2[:], d_a2)
            Pg.load(None, rowp[:, 0:1280], d_row[0:1280].partition_broadcast(128))
            Pg.load(None, rowp[:, RM_IB:RM_IB + 8], d_row[R_IB:R_IB + 8].partition_broadcast(128))
            Pg.load(None, convw[:], d_conv)
            Pg.memset("dve", ones[:], 1.0)
            ts("dve", rwp[:, 30:40], rwp[:, 0:10], -1.0, 1.0, ALU.mult, ALU.add)
            ts("dve", rwp[:, 40:44], rwp[:, 22:26], -1.0, 1.0, ALU.mult, ALU.add)
            ts("dve", rowp[:, RM_OMUV:RM_OMUV + 256], rowp[:, R_MUV:R_MUV + 256], -1.0, 1.0, ALU.mult, ALU.add)
            ts("dve", rowp[:, RM_IB:RM_IB + 4], rowp[:, RM_IB:RM_IB + 4], math.log(0.125), None, ALU.add)

            NSL = 36
            A = [S.slot(f"A{i}") for i in range(NSL)]

            def f64(sl, h=4):
                return sl.view(sl.h[0:64, :].rearrange("p (h t) -> p h t", h=h))

            def b64(sl, lo, n, c=8):
                hh = sl.hb if RDT == BF16 else sl.h
                return sl.view(hh[0:64, lo:lo + n].rearrange("p (c d) -> p c d", c=c))
            zs_r = f64(A[0]); rk = zs_r
            zs_k = f64(A[1]); kmod = zs_k
            sig = f64(A[2]); esc = sig; kq = sig; kk = sig
            aa = f64(A[3]); bbv = aa
            gsc = f64(A[4]); cumI = gsc; pinv = gsc
            tA = f64(A[5]); cE = tA; pprev = tA
            tB = f64(A[6]); dl = tB; pLp = tB
            pp = f64(A[7])
            tC = f64(A[8])
            zs_wa = A[9].view(A[9].h[0:64, 0:256].rearrange("p (h t) -> p h t", h=2))
            th = A[9].view(A[9].h[0:64, 256:384])
            Vc = A[10].view(A[10].h[0:64, :].rearrange("p (c n) -> p c n", c=2))
            tV = A[11].view(A[11].h[0:64, :].rearrange("p (c n) -> p c n", c=2)); grw = tV
            if RDT == BF16:
                AR = A[12].view(A[12].hb[0:64, :].rearrange("p (c w l) -> p c w l", c=8, w=2))
                Bt, Kt = b64(A[13], 0, 512), b64(A[13], 512, 512)
                Bpf, Kpf = b64(A[14], 0, 512), b64(A[14], 512, 512)
                Bptm, Kptm = b64(A[15], 0, 512), b64(A[15], 512, 512)
                Gb = b64(A[16], 0, 1024)
                Gk = b64(A[17], 0, 1024)
                Pn = [b64(A[18], 0, 512), b64(A[18], 512, 512)]
                Ptb = [b64(A[19], 0, 512), b64(A[19], 512, 512)]
                X = [b64(A[20], 0, 1024), b64(A[21], 0, 1024)]
                Phi, Rhat = b64(A[22], 0, 512), b64(A[22], 512, 512)
                Vcr = A[24].view(A[24].hb[0:64, 0:512].rearrange("p (c n) -> p c n", c=2))
            else:
                raise NotImplementedError
            Yrw = A[23].view(A[23].h[0:64, :].rearrange("p (c d) -> p c d", c=8))
            mixa = A[24].view(A[24].hb[0:64, 512:1024].rearrange("p (c n) -> p c n", c=2))
            cq, ck, ctmp = tA, tB, tC
            qT = A[25].view(A[25].hb[0:64, 0:512].rearrange("p (h t) -> p h t", h=4))
            kT = A[25].view(A[25].hb[0:64, 512:1024].rearrange("p (h t) -> p h t", h=4))
            f128 = lambda sl: sl.view(sl.h[:, :].rearrange("p (h t) -> p h t", h=4))
            Ah, Eh, Em = f128(A[2]), f128(A[3]), f128(A[4])
            PTm = A[26].view(A[26].hb[:, 0:512].rearrange("p (h t) -> p h t", h=4))
            ktw = A[26].view(A[26].hb[:, 512:768].rearrange("p (h d) -> p h d", h=4))
            Vaug = A[27].view(A[27].hb[:, 0:260].rearrange("p (h d) -> p h d", h=4))
            vrt = A[27].view(A[27].hb[:, 260:516].rearrange("p (h d) -> p h d", h=4))
            vz = A[27].view(A[27].hb[:, 516:772].rearrange("p (h d) -> p h d", h=4))
            numt = A[28].view(A[28].h[:, 0:260].rearrange("p (h d) -> p h d", h=4))
            numi = A[29].view(A[29].h[:, 0:260].rearrange("p (h d) -> p h d", h=4))
            osg = A[30].view(A[30].h[:, 0:256])
            Yml = A[30].view(A[30].h[:, 256:512].rearrange("p (h d) -> p h d", h=4))
            ctmp2 = A[1].view(A[1].h[0:64, 0:260].rearrange("p (h d) -> p h d", h=4))
            rtmp = A[5].view(A[5].h[0:64, 0:256].rearrange("p (h d) -> p h d", h=4))
            rA = A[13].view(A[13].h[:, :].rearrange("p (q w i) -> p q w i", q=8, w=2))
            rB = A[14].view(A[14].h[:, :].rearrange("p (q w i) -> p q w i", q=8, w=2))
            qkr = A[31].view(A[31].hb[:, 0:512].rearrange("p (q d) -> p q d", q=8))
            PTr = A[31].view(A[31].hb[:, 512:1024].rearrange("p (h t) -> p h t", h=4))
            qkT = A[32].view(A[32].hb[0:64, :].rearrange("p (q t) -> p q t", q=8))
            yint = A[33].view(A[33].h[:, 0:256].rearrange("p (h d) -> p h d", h=4))
            Yrt = A[33].view(A[33].h[:, 256:512].rearrange("p (h d) -> p h d", h=4))
            s2 = A[16].view(A[16].h[:, 0:288].rearrange("p (m t) -> p m t", m=2))
            s4 = A[17].view(A[17].h[:, 0:288].rearrange("p (m t) -> p m t", m=2))
            s8 = A[18].view(A[18].h[:, 0:144])
            s16 = A[18].view(A[18].h[:, 144:288])
            pm = A[19].view(A[19].h[:, 0:256].rearrange("p (m t) -> p m t", m=2))
            pd = A[19].view(A[19].hb[:, 512:768].rearrange("p (m t) -> p m t", m=2))
            g2 = A[20].view(A[20].h[:, :])
            g3 = A[21].view(A[21].h[:, 0:256])
            hbig = A[22].view(A[22].h[:, :].rearrange("p (g d) -> p g d", g=8))
            mixb = A[34].view(A[34].hb[:, 0:768])
            mixT = A[35].view(A[35].hb[:, :].rearrange("p (k t) -> p k t", k=8))
            smv = lambda lo, n, P_=128: sm.view(sm.h[0:P_, lo:lo + n])
            hsum, hsq, hm, hv = smv(0, 8), smv(8, 8), smv(16, 8), smv(24, 8)
            gat, nlf, ibs, ebl = smv(32, 8), smv(40, 4), smv(44, 4), smv(48, 4)
            etot, dden, bon = smv(52, 4, 64), smv(56, 4), smv(60, 8, 64)
            NBT = smv(68, 4, 64)

            wfm = sb([128, 8, FMC], BF16, "wfm")
            wtm = sb([128, 8, TMC], BF16, "wtm")
            cast_engs = ["dve", "pool", "act"]
            si = 0
            for dst, dram, ncols in ((wfm, d_wfm, FMC), (wtm, d_wtm, TMC)):
                for k in range(8):
                    for c0 in range(0, ncols, 512):
                        c1 = min(ncols, c0 + 512)
                        sl = A[si % 12 + 20] if si % 12 + 20 < NSL else A[si % 12]
                        Pg.load(None, sl[:, 0:c1 - c0], dram[k * 128:(k + 1) * 128, c0:c1])
                        cp(cast_engs[si % 3], dst[:, k, c0:c1], sl[:, 0:c1 - c0])
                        si += 1
            sl = A[0]
            Pg.load(None, sl[:, 0:256], d_plw.rearrange("p m n -> p (m n)"))
            cp("dve", plwb[:].re("p m n -> p (m n)"), sl[:, 0:256])

            xt = sb([128, D], F32, "xt")
            xTb = sb([128, 8, 144], BF16, "xTb")
            Zr = sb([64, 4, 144], F32, "Zr")
            Zk = sb([64, 4, 144], F32, "Zk")
            Zwa = sb([64, 2, 144], F32, "Zwa")
            Zu = sb([128, 2, 144], F32, "Zu")
            Zq = sb([64, 4, 144], F32, "Zq")
            Zkm = sb([64, 4, 144], F32, "Zkm")
            SW = 64 if P2 else 128
            S32 = sb([64, 4, SW], F32, "S32")
            Sr = sb([64, 4, SW], RDT, "Sr")
            C32 = sb([64, 4, 65], F32, "C32")
            Cb = sb([64, 4, 65], BF16, "Cb")
            R32 = sb([64, 4, 64], F32, "R32")
            Rb = sb([64, 4, 64], BF16, "Rb")
            cst = [sb([128, 128], F32, "cs0"), sb([128, 128], F32, "cs1")]
            Pg.memset("dve", S32[:], 0.0)
            Pg.memset("dve", C32[:], 0.0)
            Pg.memset("dve", R32[:], 0.0)
            if not P2:
                for h in range(4):
                    cp("pool", S32[:, h, 64:128], ident[0:64, 0:64])
            else:
                msk = sb([64, NSEG], F32, "msk")
                gseg = sb([64, 4, 64], F32, "gseg")
                Pg.load(None, msk[:], d_msk)
                Pg.load(None, gseg[:], d_gseg)
                c_sg = A[0].view(A[0].h[0:64, :].rearrange("p (h n) -> p h n", h=4))
                c_ml = A[1].view(A[1].h[0:64, 0:260].rearrange("p (h n) -> p h n", h=4))
                c_bt = A[1].view(A[1].h[0:64, 300:304])
                c_rt = A[2].view(A[2].h[0:64, 0:256].rearrange("p (h n) -> p h n", h=4))
                c_gt = A[3].view(A[3].h[0:64, 0:256].rearrange("p (h n) -> p h n", h=4))
                c_cd = A[4].view(A[4].h[0:64, 0:260].rearrange("p (h n) -> p h n", h=4))
                for js in range(NSEG - 1):
                    Pg.load(None, c_sg[:], d_ssg[js])
                    Pg.load(None, c_ml[:], d_sml[js])
                    Pg.load(None, c_bt[:], d_sbt[js])
                    Pg.load(None, c_rt[:], d_srt[js])
                    mj = msk[:, js:js + 1]
                    j, bk = Pg.bank()
                    for h in range(4):
                        tr(bk[0:64, j, h * 64:(h + 1) * 64], c_sg[:, h, 64:128], ident[0:64, 0:64])
                    cp("act", c_gt[:], bk[0:64, j, 0:256].re("p (h n) -> p h n", h=4))
                    j, bk = Pg.bank()
                    for h in range(4):
                        mm(bk[0:64, j, h * 64:(h + 1) * 64], c_gt[:, h, :], S32[:, h, :])
                    tt("dve", c_cd[:, :, 0:64], bk[0:64, j, 0:256].re("p (h n) -> p h n", h=4), c_sg[:, :, 0:64], ALU.add)
                    tt("dve", c_cd[:, :, 0:64], c_cd[:, :, 0:64], S32[:], ALU.subtract)
                    stt(S32[:], c_cd[:, :, 0:64], mj, S32[:], ALU.mult, ALU.add)
                    act(c_bt[:], c_bt[:], AF.Exp, scale=-1.0)
                    tt("dve", c_cd[:], C32[:], c_bt[:].re("p (h o) -> p h o", o=1).bc([64, 4, 65]), ALU.mult)
                    tt("dve", c_cd[:], c_cd[:], c_ml[:], ALU.add)
                    tt("dve", c_cd[:], c_cd[:], C32[:], ALU.subtract)
                    stt(C32[:], c_cd[:], mj, C32[:], ALU.mult, ALU.add)
                    tt("dve", c_cd[:, :, 0:64], R32[:], gseg[:], ALU.mult)
                    tt("dve", c_cd[:, :, 0:64], c_cd[:, :, 0:64], c_rt[:], ALU.add)
                    tt("dve", c_cd[:, :, 0:64], c_cd[:, :, 0:64], R32[:], ALU.subtract)
                    stt(R32[:], c_cd[:, :, 0:64], mj, R32[:], ALU.mult, ALU.add)
            cp("dve", Sr[:], S32[:])
            cp("dve", Cb[:], C32[:])
            cp("dve", Rb[:], R32[:])
            Pg.memset("dve", NBT[:], 0.0)

            FM_TILES = [("r", Zr, 0), ("k", Zk, 256), ("wa", Zwa, 512), ("u", Zu, 640), ("q", Zq, 896), ("km", Zkm, 1152)]
            if not P2:
                FM_TILES = [f for f in FM_TILES if f[0] in ("k", "wa", "km")]

            def proj_fm(c_lo, c_hi, d_lo):
                n = c_hi - c_lo
                for name, Z, off in FM_TILES:
                    j, bk = Pg.bank()
                    if name == "u":
                        for m in range(2):
                            for k in range(8):
                                mm(bk[:, j, m * 128:m * 128 + n], wfm[:, k, off + m * 128: off + (m + 1) * 128], xTb[:, k, c_lo:c_hi],
                                   start=(k == 0), stop=(k == 7))
                        cp("act", Z[:, :, d_lo:d_lo + n], bk[:, j, 0:256].re("p (m t) -> p m t", m=2)[:, :, 0:n])
                    else:
                        nm = 2 if name == "wa" else 4
                        for m in range(nm):
                            for k in range(8):
                                mm(bk[0:64, j, m * 128:m * 128 + n], wfm[:, k, off + m * 64: off + (m + 1) * 64], xTb[:, k, c_lo:c_hi],
                                   start=(k == 0), stop=(k == 7))
                        cp("act", Z[:, :, d_lo:d_lo + n], bk[0:64, j, 0:nm * 128].re("p (m t) -> p m t", m=nm)[:, :, 0:n])

            Pg.load(ld_x[0], xt[0:16, :], d_xh[0:HALO, :])
            j, bk = Pg.bank()
            for k in range(8):
                tr(bk[:, j, k * 16:(k + 1) * 16], xt[0:16, k * 128:(k + 1) * 128], ident[0:16, 0:16])
            cp("act", xTb[:, :, 128:144], bk[:, j, 0:128].re("p (k t) -> p k t", k=8))
            proj_fm(128, 144, 128)
            Pg.memset("dve", Vaug[:], 1.0)
            if not P2:
                Pg.memset("pool", AR[:, :, 1, :], 0.0)

            def b4(col, w=128):
                return rwp[:, col:col + 4].re("p (h o) -> p h o", o=1).bc([64, 4, w])

            def b2(col, w=128):
                return rwp[:, col:col + 2].re("p (h o) -> p h o", o=1).bc([64, 2, w])

            def v8(t_):
                return t_[:].re("p h (c l) -> p (h c) l", c=2)

            def tmproj(col_lo, ncols):
                j, bk = Pg.bank()
                for k in range(8):
                    mm(bk[:, j, 0:ncols], xTb[:, k, 16:144], wtm[:, k, col_lo:col_lo + ncols], start=(k == 0), stop=(k == 7))
                return j, bk

            def headnorm(Y, P_, ng, eps):
                red(hsum[0:P_, 0:ng], Y)
                tt("pool", hbig[0:P_, 0:ng, :], Y, Y, ALU.mult)
                red(hsq[0:P_, 0:ng], hbig[0:P_, 0:ng, :])
                ts("dve", hm[0:P_, 0:ng], hsum[0:P_, 0:ng], 1.0 / 64, None, ALU.mult)
                tt("dve", hv[0:P_, 0:ng], hm[0:P_, 0:ng], hm[0:P_, 0:ng], ALU.mult)
                stt(hv[0:P_, 0:ng], hsq[0:P_, 0:ng], 1.0 / 64, hv[0:P_, 0:ng], ALU.mult, ALU.subtract)
                ts("dve", hv[0:P_, 0:ng], hv[0:P_, 0:ng], eps, None, ALU.add)
                act(hv[0:P_, 0:ng], hv[0:P_, 0:ng], AF.Ln)
                act(hv[0:P_, 0:ng], hv[0:P_, 0:ng], AF.Exp, scale=-0.5)
                tt("dve", Y, Y, hm[0:P_, 0:ng].re("p (g o) -> p g o", o=1).bc([P_, ng, 64]), ALU.subtract)
                tt("dve", Y, Y, hv[0:P_, 0:ng].re("p (g o) -> p g o", o=1).bc([P_, ng, 64]), ALU.mult)

            for ti in range(NT):
                cp("pool", xTb[:, :, 0:16], xTb[:, :, 128:144])
                for name, Z, off in FM_TILES:
                    cp("pool", Z[:, :, 0:16], Z[:, :, 128:144])
                Pg.load(ld_x[0], xt[:], d_xh[HALO + ti * TT: HALO + (ti + 1) * TT, :])
                Pg.load(ld_cs[ti % 2], cst[ti % 2][:], d_cs[ti * TT:(ti + 1) * TT, :])
                j, b2v = Pg.bank2()
                for k in range(8):
                    tr(b2v[:, k // 4, (k % 4) * 128:(k % 4 + 1) * 128], xt[:, k * 128:(k + 1) * 128], ident[:])
                cp("act", xTb[:, 0:4, 16:144], b2v[:, 0, :].re("p (k t) -> p k t", k=4))
                cp("dve", xTb[:, 4:8, 16:144], b2v[:, 1, :].re("p (k t) -> p k t", k=4))
                if P2:
                    Pg.store(st_sx, t_xTs[ti].re("p (k t) -> p k t", k=8), xTb[:, :, 16:144])
                proj_fm(16, 144, 16)

                jv, bv = Pg.bank2()
                for c in range(2):
                    for w_, sh in ((0, 16), (1, 15)):
                        for k in range(8):
                            mm(bv[0:64, c, w_ * 256:(w_ + 1) * 256], xTb[:, k, sh + c * 64: sh + (c + 1) * 64], wtm[:, k, 0:256],
                               start=(k == 0), stop=(k == 7))
                bvv = bv[0:64, :, :].re("p c (w n) -> p c w n", w=2)
                tt("dve", tV[:], bvv[:, :, 1, :], rowp[0:64, R_MUV:R_MUV + 256].re("p (o n) -> p o n", o=1).bc([64, 2, 256]), ALU.mult)
                tt("dve", Vc[:], bvv[:, :, 0, :], rowp[0:64, RM_OMUV:RM_OMUV + 256].re("p (o n) -> p o n", o=1).bc([64, 2, 256]), ALU.mult)
                tt("dve", Vc[:], Vc[:], tV[:], ALU.add)
                cp("pool", Vcr[:], Vc[:])
                if P2:
                    jg, bg = Pg.bank()
                    for c in range(2):
                        for k in range(8):
                            mm(bg[0:64, jg, c * 256:(c + 1) * 256], xTb[:, k, 16 + c * 64: 16 + (c + 1) * 64], wtm[:, k, 256:512],
                               start=(k == 0), stop=(k == 7))
                    act(grw[:], bg[0:64, jg, :].re("p (c n) -> p c n", c=2), AF.Silu)

                if P2:
                    tt("dve", tA[:], Zr[:, :, 16:144], b4(30), ALU.mult)
                    tt("pool", tB[:], Zr[:, :, 15:143], b4(0), ALU.mult)
                    tt("dve", zs_r[:], tA[:], tB[:], ALU.add)
                tt("dve", tA[:], Zk[:, :, 16:144], b4(34), ALU.mult)
                tt("pool", tB[:], Zk[:, :, 15:143], b4(4), ALU.mult)
                tt("dve", zs_k[:], tA[:], tB[:], ALU.add)
                tt("dve", tA[:, 0:2, :], Zwa[:, :, 16:144], b2(38), ALU.mult)
                tt("pool", tB[:, 0:2, :], Zwa[:, :, 15:143], b2(8), ALU.mult)
                tt("dve", zs_wa[:], tA[:, 0:2, :], tB[:, 0:2, :], ALU.add)
                act(th[:], zs_wa[:, 0, :], AF.Tanh)
                j, bk = Pg.bank()
                for h in range(4):
                    mm(bk[0:64, j, h * 128:(h + 1) * 128], w2[:, h * 64:(h + 1) * 64], th[:])
                tt("dve", tA[:], bk[0:64, j, :].re("p (h t) -> p h t", h=4), b4(10), ALU.add)
                act(sig[:], tA[:], AF.Sigmoid)
                j, bk = Pg.bank()
                for h in range(4):
                    mm(bk[0:64, j, h * 128:(h + 1) * 128], a2[:, h * 64:(h + 1) * 64], zs_wa[:, 1, :])
                tt("dve", tB[:], bk[0:64, j, :].re("p (h t) -> p h t", h=4), b4(14), ALU.add)
                act(aa[:], tB[:], AF.Sigmoid)
                Pg.scan_add(gsc[:].re("p h t -> p (h t)"), ones[0:64, 0:1].bc([64, 512]), sig[:].re("p h t -> p (h t)"))
                tt("dve", esc[:], gsc[:], sig[:], ALU.subtract)
                base8 = v8(esc)[:, :, 0:1].bc([64, 8, 64])
                tt("dve", v8(cE), v8(esc), base8, ALU.subtract)
                tt("dve", v8(cumI), v8(gsc), base8, ALU.subtract)
                tt("dve", v8(dl), v8(cumI), v8(cumI)[:, :, 63:64].bc([64, 8, 64]), ALU.subtract)
                act(pp[:], cumI[:], AF.Exp, scale=-C0)
                act(pinv[:], cumI[:], AF.Exp, scale=C0)
                act(pprev[:], cE[:], AF.Exp, scale=-C0)
                act(pLp[:], dl[:], AF.Exp, scale=C0)
                tt("dve", kq[:], zs_k[:], b4(18), ALU.mult)
                tt("pool", tC[:], kq[:], kq[:], ALU.mult)
                j, bk = Pg.bank()
                mm(bk[0:64, j, :], ones[0:64, 0:64], tC[:].re("p h t -> p (h t)"))
                ts("dve", tC[:].re("p h t -> p (h t)"), bk[0:64, j, :], 1e-18, None, ALU.max)
                act(tC[:], tC[:], AF.Ln)
                act(tC[:], tC[:], AF.Exp, scale=-0.5)
                tt("dve", kk[:], kq[:], tC[:], ALU.mult)
                tt("dve", tC[:], aa[:], b4(22), ALU.mult)
                tt("dve", tC[:], tC[:], b4(40), ALU.add)
                tt("dve", kmod[:], zs_k[:], tC[:], ALU.mult)
                tt("pool", bbv[:], kk[:], aa[:], ALU.mult)
                stt(AR[:, :, 0, :], v8(kk), -1.0, v8(pprev), ALU.mult, ALU.mult)
                if P2:
                    tt("dve", AR[:, :, 1, :], v8(zs_r), v8(pp), ALU.mult)
                    tt("dve", rk[:], zs_r[:], b4(26), ALU.mult)
                    tt("dve", rk[:], rk[:], kmod[:], ALU.mult)
                tt("dve", Bt[:], v8(bbv), v8(pinv), ALU.mult)
                tt("dve", Kt[:], v8(kmod), v8(pinv), ALU.mult)
                tt("pool", Bpf[:], v8(bbv), v8(pLp), ALU.mult)
                tt("pool", Kpf[:], v8(kmod), v8(pLp), ALU.mult)
                Xc = X[0]
                for src, dst in ((AR, None), (Bpf, Bptm), (Kpf, Kptm)):
                    j, bk = Pg.bank()
                    for ch in range(8):
                        in_ = src[:, ch, 0, :] if src is AR else src[:, ch, :]
                        tr(bk.b16((slice(0, 64), j, slice(ch * 64, (ch + 1) * 64))), in_, identb[0:64, 0:64])
                    pview = bk.b16((slice(0, 64), j, slice(0, 512))).re("p (c d) -> p c d", c=8)
                    cp("act", Xc[:, :, 0:64] if dst is None else dst[:], pview)
                for lhs, dstG in ((Bt, Gb), (Kt, Gk)):
                    j, b2v = Pg.bank2()
                    for ch in range(8):
                        mm(b2v[0:64, ch // 4, (ch % 4) * 128:(ch % 4 + 1) * 128], lhs[:, ch, :], AR[:, ch, :, :].re("p w l -> p (w l)"))
                    tt("dve", dstG[:], b2v[0:64, :, :].re("p b (c n) -> p (b c) n", c=4),
                       mG[:].re("p (o n) -> p o n", o=1).bc([64, 8, 128]), ALU.mult)
                j, bk = Pg.bank()
                for ch in range(8):
                    mm(bk[0:64, j, ch * 64:(ch + 1) * 64], AR[:, ch, 0, :], Bt[:, ch, :])
                tt("dve", Pn[0][:], bk[0:64, j, :].re("p (c s) -> p c s", c=8),
                   mgt[0:64, 0:64].re("p (o s) -> p o s", o=1).bc([64, 8, 64]), ALU.mult)
                j, bk = Pg.bank()
                for ch in range(8):
                    h_, c_ = ch // 2, ch % 2
                    mm(bk[0:64, j, ch * 64:(ch + 1) * 64], Gk[:, ch, 0:64], Vcr[:, c_, h_ * 64:(h_ + 1) * 64])
                cp("act", Xc[:, :, 64:128], bk[0:64, j, :].re("p (c d) -> p c d", c=8))
                Ptc = None
                for lvl in range(6):
                    Xn = X[(lvl + 1) % 2]
                    j, b2v = Pg.bank2()
                    for ch in range(8):
                        lh = Gb[:, ch, 0:64] if lvl == 0 else Ptc[:, ch, :]
                        mm(b2v[0:64, ch // 4, (ch % 4) * 128:(ch % 4 + 1) * 128], lh, Xc[:, ch, :])
                    tt("dve", Xn[:], b2v[0:64, :, :].re("p b (c n) -> p (b c) n", c=4), Xc[:], ALU.add)
                    if lvl < 5:
                        Pnc, Pnn, Ptn = Pn[lvl % 2], Pn[(lvl + 1) % 2], Ptb[(lvl + 1) % 2]
                        j, bk = Pg.bank()
                        for ch in range(8):
                            lhT = Gb[:, ch, 0:64] if lvl == 0 else Ptc[:, ch, :]
                            mm(bk[0:64, j, ch * 64:(ch + 1) * 64], Pnc[:, ch, :], lhT)
                        cp("act", Ptn[:], bk[0:64, j, :].re("p (c d) -> p c d", c=8))
                        j, bk = Pg.bank()
                        for ch in range(8):
                            lhT = Gb[:, ch, 0:64] if lvl == 0 else Ptc[:, ch, :]
                            mm(bk[0:64, j, ch * 64:(ch + 1) * 64], lhT, Pnc[:, ch, :])
                        cp("act", Pnn[:], bk[0:64, j, :].re("p (c d) -> p c d", c=8))
                        Ptc = Ptn
                    Xc = Xn
                j, bk = Pg.bank()
                for ch in range(8):
                    mm(bk[0:64, j, ch * 64:(ch + 1) * 64], Xc[:, ch, 0:64], Bptm[:, ch, :])
                cp("act", Phi[:], bk[0:64, j, :].re("p (c d) -> p c d", c=8))
                if P2:
                    j, bk = Pg.bank()
                    for ch in range(8):
                        mm(bk[0:64, j, ch * 64:(ch + 1) * 64], Xc[:, ch, 0:64], Gb[:, ch, 64:128])
                    tt("dve", Rhat[:], bk[0:64, j, :].re("p (c d) -> p c d", c=8), AR[:, :, 1, :], ALU.add)
                    jy, by = Pg.bank()
                for c in range(2):
                    if P2:
                        for h in range(4):
                            ch = h * 2 + c
                            o = by[0:64, jy, ch * 64:(ch + 1) * 64]
                            mm(o, Rhat[:, ch, :], Sr[:, h, 0:64], start=True, stop=False)
                            mm(o, Gb[:, ch, 64:128], Xc[:, ch, 64:128], start=False, stop=False)
                            mm(o, Gk[:, ch, 64:128], Vcr[:, c, h * 64:(h + 1) * 64], start=False, stop=True)
                    js, bs = Pg.bank()
                    for h in range(4):
                        ch = h * 2 + c
                        mm(bs[0:64, js, h * 128:h * 128 + SW], Phi[:, ch, :], Sr[:, h, :], start=True, stop=False)
                        mm(bs[0:64, js, h * 128:h * 128 + 64], Bptm[:, ch, :], Xc[:, ch, 64:128], start=False, stop=False)
                        mm(bs[0:64, js, h * 128:h * 128 + 64], Kptm[:, ch, :], Vcr[:, c, h * 64:(h + 1) * 64], start=False, stop=True)
                    col = c * 64 + 63
                    tt("dve", S32[:], S32[:], pp[:, :, col:col + 1].bc([64, 4, SW]), ALU.mult)
                    tt("dve", S32[:], S32[:], bs[0:64, js, :].re("p (h n) -> p h n", h=4)[:, :, 0:SW], ALU.add)
                    cp("act", Sr[:], S32[:])
                if P2:
                    cp("act", Yrw[:], by[0:64, jy, :].re("p (c d) -> p c d", c=8))
                    j, bk = Pg.bank()
                    for ch in range(8):
                        mm(bk[0:64, j, ch * 2:ch * 2 + 1], v8(rk)[:, ch, :], ones[0:64, 0:1])
                    cp("act", bon[:], bk[0:64, j, 0:16].re("p (c o) -> p c o", o=2)[:, :, 0])

                j, bk = tmproj(2560, 8)
                tt("dve", gat[:], bk[:, j, 0:8], rowp[:, RM_IB:RM_IB + 8], ALU.add)
                cp("pool", ibs[:], gat[:, 0:4])
                act(nlf[:], gat[:, 4:8], AF.Exp, scale=-1.0)
                act(nlf[:], nlf[:], AF.Ln, bias=1.0)
                j, bk = Pg.bank()
                mm(bk[:, j, 0:4], mle[:], nlf[:])
                mm(bk[0:64, j, 8:12], ones[:, 0:64], nlf[:])
                act(ebl[:], bk[:, j, 0:4], AF.Exp, scale=-1.0)
                act(etot[:], bk[0:64, j, 8:12], AF.Exp, scale=-1.0)
                tt("dve", NBT[:], NBT[:], bk[0:64, j, 8:12], ALU.add)
                for h in range(4):
                    ts("dve", Ah[:, h, :], mgt[:], nlf[:, h:h + 1], None, ALU.mult)
                j, bk = Pg.bank()
                for h in range(4):
                    mm(bk[:, j, h * 128:(h + 1) * 128], Ah[:, h, :], mle[:])
                for h in range(4):
                    act(Eh[:, h, :], bk[:, j, h * 128:(h + 1) * 128], AF.Exp, scale=-1.0, bias=ibs[:, h:h + 1])
                j, bk = tmproj(2048, 512)
                cp("act", Vaug[:, :, 0:64], bk[:, j, 0:256].re("p (h d) -> p h d", h=4))
                if P2:
                    cp("act", vrt[:], bk[:, j, 256:512].re("p (h d) -> p h d", h=4))
                tt("dve", vz[:], bk[:, j, 256:512].re("p (h d) -> p h d", h=4), zx[:, 0:4].re("p (h o) -> p h o", o=1).bc([128, 4, 64]), ALU.mult)
                for Zs, cdst, qi, outT in ((Zq, cq, 0, qT), (Zkm, ck, 1, kT)):
                    if not P2 and qi == 0:
                        continue
                    for tap in range(4):
                        wv = convw[:, qi, :, tap:tap + 1].bc([64, 4, 128])
                        src = Zs[:, :, 13 + tap:141 + tap]
                        if tap == 0:
                            tt("dve", cdst[:], src, wv, ALU.mult)
                        else:
                            tt("pool", ctmp[:], src, wv, ALU.mult)
                            tt("dve", cdst[:], cdst[:], ctmp[:], ALU.add)
                    act(outT[:], cdst[:], AF.Silu)
                j, bk = Pg.bank()
                for h in range(4):
                    tr(bk.b16((slice(0, 128), j, slice(h * 64, (h + 1) * 64))), kT[:, h, :], identb[0:64, 0:64])
                tt("dve", ktw[:], bk.b16((slice(0, 128), j, slice(0, 256))).re("p (h d) -> p h d", h=4), Eh[:, :, 127:128].bc([128, 4, 64]), ALU.mult)
                if P2:
                    tt("pool", Em[:], Eh[:], mle[:].re("p (o n) -> p o n", o=1).bc([128, 4, 128]), ALU.mult)
                    j, bk = Pg.bank()
                    for h in range(4):
                        mm(bk[:, j, h * 128:(h + 1) * 128], kT[:, h, :], qT[:, h, :])
                    tt("dve", PTm[:], bk[:, j, :].re("p (h l) -> p h l", h=4), Em[:], ALU.mult)
                    ji, bki = Pg.bank()
                    jn, bkn = Pg.bank()
                    for h in range(4):
                        mm(bki[:, ji, h * 65:(h + 1) * 65], qT[:, h, :], Cb[:, h, :])
                    for h in range(4):
                        mm(bkn[:, jn, h * 65:(h + 1) * 65], PTm[:, h, :], Vaug[:, h, :])
                    tt("dve", numi[:], bki[:, ji, 0:260].re("p (h d) -> p h d", h=4), ebl[:].re("p (h o) -> p h o", o=1).bc([128, 4, 65]), ALU.mult)
                    tt("dve", numt[:], numi[:], bkn[:, jn, 0:260].re("p (h d) -> p h d", h=4), ALU.add)
                    act(dden[:], numt[:, :, 64], AF.Abs)
                    ts("dve", dden[:], dden[:], 1.0, None, ALU.max)
                    Pg.recip(dden[:], dden[:])
                    tt("dve", Yml[:], numt[:, :, 0:64], dden[:].re("p (h o) -> p h o", o=1).bc([128, 4, 64]), ALU.mult)
                j, bk = Pg.bank()
                for h in range(4):
                    mm(bk[0:64, j, h * 65:(h + 1) * 65], ktw[:, h, :], Vaug[:, h, :])
                tt("dve", ctmp2[:], C32[:], etot[:].re("p (h o) -> p h o", o=1).bc([64, 4, 65]), ALU.mult)
                tt("dve", C32[:], ctmp2[:], bk[0:64, j, 0:260].re("p (h d) -> p h d", h=4), ALU.add)
                cp("act", Cb[:], C32[:])

                j, bk = tmproj(1536, 512)
                zv = bk[:, j, :].re("p (q w i) -> p q w i", q=8, w=2)
                csv = cst[ti % 2]
                csA = csv[:, 0:64].re("p (o w i) -> p o w i", o=1, w=2).bc([128, 8, 2, 32])
                csB = csv[:, 64:128].re("p (o w i) -> p o w i", o=1, w=2).bc([128, 8, 2, 32])
                tt("dve", rA[:], zv[:, :, 0:1, :].bc([128, 8, 2, 32]), csA, ALU.mult)
                tt("dve", rB[:], zv[:, :, 1:2, :].bc([128, 8, 2, 32]), csB, ALU.mult)
                tt("dve", qkr[:].re("p q (w i) -> p q w i", w=2), rA[:], rB[:], ALU.add)
                if P2:
                    j, bk = Pg.bank()
                    for q_ in range(8):
                        tr(bk.b16((slice(0, 64), j, slice(q_ * 128, (q_ + 1) * 128))), qkr[:, q_, :], identb[:])
                    cp("act", qkT[:], bk.b16((slice(0, 64), j, slice(0, 1024))).re("p (q t) -> p q t", q=8))
                    j, bk = Pg.bank()
                    for h in range(4):
                        mm(bk[:, j, h * 128:(h + 1) * 128], qkT[:, 4 + h, :], qkT[:, h, :])
                    tt("dve", PTr[:], bk[:, j, :].re("p (h l) -> p h l", h=4), dect[:], ALU.mult)
                    ji, bki = Pg.bank()
                    jn, bkn = Pg.bank()
                    for h in range(4):
                        mm(bki[:, ji, h * 64:(h + 1) * 64], qkT[:, h, :], Rb[:, h, :])
                    for h in range(4):
                        mm(bkn[:, jn, h * 64:(h + 1) * 64], PTr[:, h, :], vrt[:, h, :])
                    tt("dve", yint[:], bki[:, ji, 0:256].re("p (h d) -> p h d", h=4), zx[:, 4:8].re("p (h o) -> p h o", o=1).bc([128, 4, 64]), ALU.mult)
                    tt("dve", Yrt[:], yint[:], bkn[:, jn, 0:256].re("p (h d) -> p h d", h=4), ALU.add)
                j, bk = Pg.bank()
                for h in range(4):
                    mm(bk[0:64, j, h * 64:(h + 1) * 64], qkr[:, 4 + h, :], vz[:, h, :])
                tt("dve", rtmp[:], R32[:], gb[:], ALU.mult)
                tt("dve", R32[:], rtmp[:], bk[0:64, j, 0:256].re("p (h d) -> p h d", h=4), ALU.add)
                cp("act", Rb[:], R32[:])

                if not P2:
                    continue

                tt("dve", s2[:, :, 1:144], Zu[:, :, 1:144], Zu[:, :, 0:143], ALU.add)
                tt("dve", s4[:, :, 3:144], s2[:, :, 3:144], s2[:, :, 1:142], ALU.add)
                tt("dve", s8[:, 7:144], s4[:, 1, 7:144], s4[:, 1, 3:140], ALU.add)
                tt("dve", s16[64:128, 15:144], s8[64:128, 15:144], s8[64:128, 7:136], ALU.add)
                sels = ((s2[0:64, 0, 16:144], 0, 0, 0.5), (s4[64:128, 0, 16:144], 64, 0, 0.25),
                        (s8[0:64, 16:144], 0, 1, 0.125), (s16[64:128, 16:144], 64, 1, 0.0625))
                for sw_, p0, m_, iw in sels:
                    if ti == 0:
                        tt("dve", pm[p0:p0 + 64, m_, :], sw_, cnt0[p0:p0 + 64, m_, :], ALU.mult)
                        tt("dve", pd[p0:p0 + 64, m_, :], pm[p0:p0 + 64, m_, :], Zu[p0:p0 + 64, m_, 16:144], ALU.subtract)
                    else:
                        stt(pd[p0:p0 + 64, m_, :], sw_, iw, Zu[p0:p0 + 64, m_, 16:144], ALU.mult, ALU.subtract)
                jpl, bpl = Pg.bank()
                for m_ in range(2):
                    mm(bpl[:, jpl, m_ * 128:(m_ + 1) * 128], pd[:, m_, :], plwb[:, m_, :])

                headnorm(Yrw[:], 64, 8, RW_EPS)
                lnw = rowp[0:64, R_LNW:R_LNW + 256].re("p (h o d) -> p h o d", h=4, o=1).bc([64, 4, 2, 64])
                lnb = rowp[0:64, R_LNB:R_LNB + 256].re("p (h o d) -> p h o d", h=4, o=1).bc([64, 4, 2, 64])
                Y4 = Yrw[:].re("p (h c) d -> p h c d", c=2)
                tt("dve", Y4, Y4, lnw, ALU.mult)
                tt("dve", Y4, Y4, lnb, ALU.add)
                Vv = Vc[:].re("p c (h d) -> p h c d", h=4)
                hb4 = hbig[0:64, :, :].re("p (h c) d -> p h c d", c=2)
                tt("dve", hb4, Vv, bon[:].re("p (h c o) -> p h c o", c=2, o=1).bc([64, 4, 2, 64]), ALU.mult)
                tt("dve", Yrw[:], Yrw[:], hbig[0:64, :, :], ALU.add)
                tt("dve", mixa[:].re("p c (h d) -> p h c d", h=4), Y4, grw[:].re("p c (h d) -> p h c d", h=4), ALU.mult)
                j, bk = tmproj(512, 512)
                act(g2[:], bk[:, j, :], AF.Silu)
                tt("dve", g2[:, 0:256], g2[:, 0:256], rowp[:, R_PLS:R_PLS + 256], ALU.mult)
                tt("dve", mixb[:, 0:256], bpl[:, jpl, 0:256], g2[:, 0:256], ALU.mult)
                j, bk = tmproj(1024, 512)
                act(g3[:], bk[:, j, 0:256], AF.Silu)
                act(osg[:], bk[:, j, 256:512], AF.Sigmoid)
                tt("dve", Yml[:], Yml[:], osg[:].re("p (h d) -> p h d", h=4), ALU.mult)
                headnorm(Yml[:], 128, 4, HN_EPS)
                tt("dve", g2[:, 256:512], g2[:, 256:512], rowp[:, R_MLLN:R_MLLN + 256], ALU.mult)
                tt("dve", mixb[:, 256:512], Yml[:].re("p h d -> p (h d)"), g2[:, 256:512], ALU.mult)
                headnorm(Yrt[:], 128, 4, HN_EPS)
                tt("dve", mixb[:, 512:768], Yrt[:].re("p h d -> p (h d)"), g3[:], ALU.mult)
                j, bk = Pg.bank()
                for c in range(2):
                    for k in range(2):
                        tr(bk.b16((slice(0, 128), j, slice(k * 128 + c * 64, k * 128 + (c + 1) * 64))), mixa[:, c, k * 128:(k + 1) * 128], identb[0:64, 0:64])
                for k in range(2, 8):
                    tr(bk.b16((slice(0, 128), j, slice(k * 128, (k + 1) * 128))), mixb[:, (k - 2) * 128:(k - 1) * 128], identb[:])
                cp("act", mixT[:], bk.b16((slice(0, 128), j, slice(0, 1024))).re("p (k t) -> p k t", k=8))
                Pg.store(st_sm, t_mix[ti].re("p (k t) -> p k t", k=8), mixT[:])

            if not P2:
                out_toks.append(Pg.store(None, d_osg, S32[:]))
                out_toks.append(Pg.store(None, d_oml, C32[:]))
                out_toks.append(Pg.store(None, d_obt, NBT[:]))
                out_toks.append(Pg.store(None, d_ort, R32[:]))
                S.barrier()
            S.emit()

        if P2:
            with contextlib.ExitStack() as stE:
                S.stack = stE
                S.barrier()
                wout = sb([128, 8, D], BF16, "wout")
                pleg = sb([128, 8, D], BF16, "pleg")
                plew = sb([128, 2, D], BF16, "plew")
                lng = sb([128, D], F32, "lng")
                lnb_ = sb([128, D], F32, "lnb")
                stg = [sb([128, D], F32, "stg0"), sb([128, D], F32, "stg1"), sb([128, D], F32, "stg2")]
                Pg.load(None, lng[:], d_row[R_LNG:R_LNG + D].partition_broadcast(128))
                Pg.load(None, lnb_[:], d_row[R_LNBB:R_LNBB + D].partition_broadcast(128))
                si = 0
                cast_engs = ["dve", "pool", "act"]
                for dst, dram, kc in ((wout, d_wout, 8), (pleg, d_pleg, 8), (plew, d_plew, 2)):
                    for k in range(kc):
                        sg = stg[si % 3]
                        Pg.load(None, sg[:], dram[k * 128:(k + 1) * 128, :])
                        cp(cast_engs[si % 3], dst[:, k, :], sg[:])
                        si += 1
                xte = [sb([128, D], F32, "xte0"), sb([128, D], F32, "xte1")]
                xTe = [sb([128, 8, 128], BF16, "xTe0"), sb([128, 8, 128], BF16, "xTe1")]
                mxe = [sb([128, 8, 128], BF16, "mxe0"), sb([128, 8, 128], BF16, "mxe1")]
                pte = [sb([128, 256], F32, "pte0"), sb([128, 256], F32, "pte1")]
                pT = sb([128, 2, 128], BF16, "pT")
                sgp = sb([128, D], F32, "sgp")
                acc = [sb([128, D], F32, "acc0"), sb([128, D], F32, "acc1")]
                bst = sb([128, 2, 6], F32, "bst")
                mv = sb([128, 2], F32, "mv")
                rstd = sb([128, 1], F32, "rstd")
                for ti in range(NT):
                    par = ti % 2
                    Pg.load(ld_x[par], xte[par][:], d_xh[HALO + ti * TT: HALO + (ti + 1) * TT, :])
                    Pg.load(ld_t[par], xTe[par][:], t_xTs[ti].re("p (k t) -> p k t", k=8))
                    Pg.load(ld_m[par], mxe[par][:], t_mix[ti].re("p (k t) -> p k t", k=8))
                    Pg.load(ld_p[par], pte[par][:], d_p[ti * TT:(ti + 1) * TT, :])
                    j, bk = Pg.bank()
                    for k in range(2):
                        tr(bk[:, j, k * 128:(k + 1) * 128], pte[par][:, k * 128:(k + 1) * 128], ident[:])
                    cp("act", pT[:], bk[:, j, 0:256].re("p (k t) -> p k t", k=2))
                    ac = acc[par]
                    for half in range(2):
                        cs_ = slice(half * 512, (half + 1) * 512)
                        jg, bg = Pg.bank()
                        for k in range(8):
                            mm(bg[:, jg, :], xTe[par][:, k, :], pleg[:, k, cs_], start=(k == 0), stop=(k == 7))
                        act(sgp[:, cs_], bg[:, jg, :], AF.Sigmoid)
                        jw, bw = Pg.bank()
                        for k in range(2):
                            mm(bw[:, jw, :], pT[:, k, :], plew[:, k, cs_], start=(k == 0), stop=(k == 1))
                        tt("dve", sgp[:, cs_], sgp[:, cs_], bw[:, jw, :], ALU.mult)
                        jo, bo = Pg.bank()
                        for k in range(8):
                            mm(bo[:, jo, :], mxe[par][:, k, :], wout[:, k, cs_], start=(k == 0), stop=(k == 7))
                        stt(ac[:, cs_], xte[par][:, cs_], ALPHA, bo[:, jo, :], ALU.mult, ALU.add)
                        tt("pool", ac[:, cs_], ac[:, cs_], sgp[:, cs_], ALU.add)
                        Pg.bn_stats(bst[:, half, :], ac[:, cs_])
                    Pg.bn_aggr(mv[:], bst[:].re("p a b -> p (a b)"))
                    ts("dve", rstd[:], mv[:, 1:2], LN_EPS, None, ALU.add)
                    act(rstd[:], rstd[:], AF.Ln)
                    act(rstd[:], rstd[:], AF.Exp, scale=-0.5)
                    ts("dve", ac[:], ac[:], mv[:, 0:1], rstd[:, 0:1], ALU.subtract, ALU.mult)
                    tt("pool", ac[:], ac[:], lng[:], ALU.mult)
                    tt("dve", ac[:], ac[:], lnb_[:], ALU.add)
                    out_toks.append(Pg.store(st_o[par], d_out[ti * TT:(ti + 1) * TT, :], ac[:]))
                S.barrier()
                S.emit()
    return nc


def _perm_rot(n_heads=4, hd=64):
    idx = []
    for h in range(n_heads):
        idx += [h * hd + 2 * i for i in range(hd // 2)] + [h * hd + 2 * i + 1 for i in range(hd // 2)]
    return np.array(idx)


def pack_layer(L, w_in, rw_mu, rw_w0, rw_w2, rw_a0, rw_a2, rw_kk, rw_ka, rw_rk, rw_ln_w, rw_ln_b, pl_w, pl_scale,
               ml_conv, ml_ib, ml_fb, ml_ln_w, ln_g, ln_b):
    W = np.asarray(w_in[L], np.float32)
    o = np.cumsum([0, 896, 256, 256, 256, 512, 256, 4, 4, 256, 256, 256, 256, 256, 256])
    rw_in, rw_g, pl_u, pl_g, ml_qk, ml_v, ml_i, ml_f, ml_o, ml_g, rt_q, rt_k, rt_v, rt_g = [W[:, o[i]:o[i + 1]] for i in range(14)]
    pr = _perm_rot()
    wfm = np.concatenate([rw_in[:, 0:256], rw_in[:, 256:512], rw_in[:, 768:896], pl_u, ml_qk[:, 0:256], ml_qk[:, 256:512]], axis=1)
    wtm = np.concatenate([rw_in[:, 512:768], rw_g, pl_g, ml_g, rt_g, ml_o, rt_q[:, pr], rt_k[:, pr], ml_v, rt_v, ml_i, ml_f], axis=1)
    assert wfm.shape[1] == FMC and wtm.shape[1] == TMC
    hT = lambda v: np.asarray(v, np.float32).reshape(4, 64).T
    mu = np.asarray(rw_mu[L], np.float32)
    rwp = np.concatenate([hT(mu[0:256]), hT(mu[256:512]), mu[768:832, None], mu[832:896, None],
                          hT(rw_w0[L]), hT(rw_a0[L]), hT(rw_kk[L]), hT(rw_ka[L]), hT(rw_rk[L])], axis=1)
    assert rwp.shape == (64, 30)
    rowp = np.zeros(4360, np.float32)
    rowp[0:256] = mu[512:768]
    rowp[256:512] = rw_ln_w[L]
    rowp[512:768] = rw_ln_b[L]
    rowp[768:1024] = pl_scale[L]
    rowp[1024:1280] = ml_ln_w[L]
    rowp[1280:2304] = ln_g[L]
    rowp[2304:3328] = ln_b[L]
    rowp[3328:3332] = ml_ib[L]
    rowp[3332:3336] = ml_fb[L]
    cw = np.asarray(ml_conv[L], np.float32)
    convw = cw.T.reshape(2, 4, 64, 4).transpose(2, 0, 1, 3)
    plw = np.zeros((128, 2, 128), np.float32)
    pw = np.asarray(pl_w[L], np.float32)
    for g in range(4):
        m, hh = g // 2, g % 2
        plw[hh * 64:(hh + 1) * 64, m, hh * 64:(hh + 1) * 64] = pw[g]
    return dict(wfm=np.ascontiguousarray(wfm), wtm=np.ascontiguousarray(wtm), rwp=np.ascontiguousarray(rwp),
                w2=np.asarray(rw_w2[L], np.float32), a2=np.asarray(rw_a2[L], np.float32), rowp=rowp,
                convw=np.ascontiguousarray(convw), plw=plw)


def const_tables(seg, NT):
    f = np.float32
    a = np.arange(128)
    ident = np.eye(128, dtype=f)
    mle = (a[:, None] <= a[None, :]).astype(f)
    mgt = (a[:, None] > a[None, :]).astype(f)
    a64 = np.arange(64)
    mG = np.concatenate([(a64[:, None] < a64[None, :]), (a64[:, None] <= a64[None, :])], axis=1).astype(f)
    gam = 1.0 - np.exp2(-5.0 - np.arange(4))
    lg = np.log(gam)
    rel = a[None, :] - a[:, None]
    dect = np.zeros((128, 4, 128), f)
    for h in range(4):
        dect[:, h, :] = np.where(rel >= 0, np.exp(lg[h] * np.maximum(rel, 0)), 0.0) * 0.125
    zx = np.zeros((128, 8), f)
    for h in range(4):
        zx[:, h] = np.exp(lg[h] * (127 - a)) * 0.125
        zx[:, 4 + h] = np.exp(lg[h] * (a + 1.0))
    gchunk = np.zeros((64, 4, 64), f)
    for h in range(4):
        gchunk[:, h, :] = np.exp(lg[h] * 128)
    pos = (seg * SEG + np.arange(NT * TT)).astype(np.float32)
    inv_freq = (np.float32(10000.0) ** (-np.arange(0, 64, 2, dtype=np.float32) / np.float32(64))).astype(np.float32)
    ang = pos[:, None] * inv_freq[None, :]
    c, s = np.cos(ang).astype(f), np.sin(ang).astype(f)
    cs2 = np.zeros((NT * TT, 2, 2, 32), f)
    cs2[:, 0, 0], cs2[:, 0, 1] = c, s
    cs2[:, 1, 0], cs2[:, 1, 1] = -s, c
    cnt0 = np.zeros((128, 2, 128), f)
    wins = (2, 4, 8, 16)
    for g in range(4):
        m, hh = g // 2, g % 2
        if seg == 0:
            cnt0[hh * 64:(hh + 1) * 64, m, :] = 1.0 / np.minimum(a + 1.0, float(wins[g]))[None, :]
        else:
            cnt0[hh * 64:(hh + 1) * 64, m, :] = 1.0 / wins[g]
    return dict(ident=ident, mask_le=mle, mask_gt=mgt, mask_g=mG, dect=dect, zetaxi=zx, gchunk=gchunk,
                cs2=np.ascontiguousarray(cs2.reshape(NT * TT, 128)), cnt0=cnt0)


def seg_consts(seg):
    f = np.float32
    gam = 1.0 - np.exp2(-5.0 - np.arange(4))
    gseg = np.zeros((64, 4, 64), f)
    for h in range(4):
        gseg[:, h, :] = np.exp(np.log(gam[h]) * SEG)
    msk = np.zeros((64, NSEG), f)
    msk[:, :seg] = 1.0
    return dict(gseg=gseg, seg_mask=msk)


_PROGS = {}


def _prog(mode, NT):
    key = (mode, NT)
    if key not in _PROGS:
        _PROGS[key] = build_program(mode, NT)
    return _PROGS[key]


def kernel(x, p, w_in, rw_mu, rw_w0, rw_w2, rw_a0, rw_a2, rw_kk, rw_ka, rw_rk, rw_ln_w, rw_ln_b, pl_w, pl_scale,
           ml_conv, ml_ib, ml_fb, ml_ln_w, w_out, ple_w, ple_gate, ln_g, ln_b):
    x = np.asarray(x, np.float32)
    p = np.asarray(p, np.float32)
    NT = SEG // TT
    ncores = 8
    cur = x
    tabs = [const_tables(c % NSEG, NT) for c in range(ncores)]
    segc = [seg_consts(c % NSEG) for c in range(ncores)]
    for L in range(2):
        pk = pack_layer(L, w_in, rw_mu, rw_w0, rw_w2, rw_a0, rw_a2, rw_kk, rw_ka, rw_rk, rw_ln_w, rw_ln_b, pl_w, pl_scale,
                        ml_conv, ml_ib, ml_fb, ml_ln_w, ln_g, ln_b)
        xhs = []
        for c in range(ncores):
            b, sg = c // NSEG, c % NSEG
            xh = np.zeros((HALO + SEG, D), np.float32)
            xh[HALO:] = cur[b, sg * SEG:(sg + 1) * SEG]
            if sg > 0:
                xh[:HALO] = cur[b, sg * SEG - HALO: sg * SEG]
            xhs.append(xh)
        ims = []
        for c in range(ncores):
            im = dict(pk)
            im.update(tabs[c])
            im["xh"] = xhs[c]
            ims.append(im)
        r1 = run_bass_kernel_spmd(_prog("P1", NT), ims, core_ids=list(range(ncores))).results
        ims = []
        for c in range(ncores):
            b = c // NSEG
            grp = range(b * NSEG, (b + 1) * NSEG)
            im = dict(pk)
            im.update(tabs[c])
            im.update(segc[c])
            im["xh"] = xhs[c]
            im["sum_sg"] = np.stack([r1[g]["o_sg"] for g in grp])
            im["sum_ml"] = np.stack([r1[g]["o_ml"] for g in grp])
            im["sum_bt"] = np.stack([r1[g]["o_bt"] for g in grp])
            im["sum_rt"] = np.stack([r1[g]["o_rt"] for g in grp])
            im["p"] = np.ascontiguousarray(p[L, b, (c % NSEG) * SEG:(c % NSEG + 1) * SEG])
            im["w_out"] = np.asarray(w_out[L], np.float32)
            im["ple_w"] = np.asarray(ple_w[L], np.float32)
            im["ple_gate"] = np.asarray(ple_gate[L], np.float32)
            ims.append(im)
        r2 = run_bass_kernel_spmd(_prog("P2", NT), ims, core_ids=list(range(ncores))).results
        nxt = np.empty_like(cur)
        for c in range(ncores):
            b, sg = c // NSEG, c % NSEG
            nxt[b, sg * SEG:(sg + 1) * SEG] = r2[c]["xout"]
        cur = nxt
    return cur
```

```python
import contextlib
import math
import numpy as np
import concourse.bass as bass
import concourse.mybir as mybir
from concourse.bass_utils import run_bass_kernel_spmd

F32 = mybir.dt.float32
BF16 = mybir.dt.bfloat16
AF = mybir.ActivationFunctionType
ALU = mybir.AluOpType
AX = mybir.AxisListType

D = 1024
G = 256
NSEG = 4
SEG = 4096
HALO = 16
TT = 128
FMC = 1408
TMC = 2568
C0 = math.exp(-0.5)
ALPHA = (2.0 * 2) ** 0.25
RW_EPS = 64e-5
HN_EPS = 1e-6
LN_EPS = 1e-5
SAME_ENGINE_SYNC = ("act", "dve", "pool")
PIPE = True
RDT = BF16


class Ref:
    __slots__ = ("ts", "ap")

    def __init__(self, ts, ap):
        self.ts = ts
        self.ap = ap

    def re(self, pat, **kw):
        return Ref(self.ts, self.ap.rearrange(pat, **kw))

    def bc(self, shape):
        return Ref(self.ts, self.ap.broadcast_to(list(shape)))

    def __getitem__(self, idx):
        return Ref(self.ts, self.ap[idx])


class T:
    __slots__ = ("h", "hb", "last_w", "readers", "name")

    def __init__(self, h, name="", hb=None):
        self.h = h
        self.hb = hb
        self.last_w = None
        self.readers = []
        self.name = name

    def __getitem__(self, idx):
        return Ref((self,), self.h[idx])

    def b16(self, idx):
        return Ref((self,), self.hb[idx])

    def view(self, ap):
        return View((self,), ap)


class View:
    __slots__ = ("ts", "ap")

    def __init__(self, ts, ap):
        self.ts = ts
        self.ap = ap

    def __getitem__(self, idx):
        return Ref(self.ts, self.ap[idx])


class Sched:
    ENG = ("pe", "act", "dve", "pool", "sp")

    def __init__(self, nc, stack):
        self.nc = nc
        self.base = stack
        self.stack = stack
        self.q = {e: [] for e in self.ENG}
        self.cnt = {}
        self.sem = {}
        self.waited = {e: {} for e in self.ENG}
        self.phase = 0
        self.ekey = {}
        for e in ("pe", "act", "dve", "pool"):
            k = f"{e}#0"
            self.ekey[e] = k
            self.sem[k] = stack.enter_context(nc.semaphore("sem_" + e + "_0"))
            self.cnt[k] = 0
        self.ndma_sem = 0
        self.nt = 0
        self.ninst = 0

    def sbuf(self, shape, dt, name=None):
        self.nt += 1
        name = f"{name or 't'}_{self.nt}"
        h = self.stack.enter_context(self.nc.sbuf_tensor(name, list(shape), dt))
        return T(h, name)

    def slot(self, name=None):
        self.nt += 1
        name = f"{name or 's'}_{self.nt}"
        h = self.stack.enter_context(self.nc.sbuf_tensor(name, [128, 512], F32))
        return T(h, name, hb=h.bitcast(BF16))

    def dma_sem(self):
        self.ndma_sem += 1
        key = f"dma{self.ndma_sem}"
        self.sem[key] = self.base.enter_context(self.nc.semaphore("sem_" + key))
        self.cnt[key] = 0
        return key

    def _need(self, eng, tok, waits):
        if tok is None:
            return
        key, val = tok
        if key.split("#")[0] == eng and eng not in SAME_ENGINE_SYNC:
            return
        if self.waited[eng].get(key, 0) >= val:
            return
        waits[key] = max(waits.get(key, 0), val)

    def _deps(self, eng, reads, writes):
        waits = {}
        for t in reads:
            self._need(eng, t.last_w, waits)
        for t in writes:
            self._need(eng, t.last_w, waits)
            for r in t.readers:
                self._need(eng, r, waits)
        for key, val in waits.items():
            self.waited[eng][key] = val
        return list(waits.items())

    def _record(self, tok, reads, writes):
        for t in writes:
            t.last_w = tok
            t.readers = []
        for t in reads:
            if t in writes:
                continue
            t.readers = [r for r in t.readers if r[0] != tok[0]] + [tok]

    def op(self, eng, fn, reads=(), writes=()):
        reads = list(dict.fromkeys(reads))
        writes = list(dict.fromkeys(writes))
        waits = self._deps(eng, reads, writes)
        ek = self.ekey[eng]
        self.cnt[ek] += 1
        self.ninst += 1
        tok = (ek, self.cnt[ek])
        sem = self.sem[ek]
        sems = self.sem

        def run(e, waits=waits, fn=fn, sem=sem):
            for key, val in waits:
                e.wait_ge(sems[key], val)
            fn(e).then_inc(sem, 1)
        self.q[eng].append(run)
        self._record(tok, reads, writes)
        return tok

    def dma(self, qeng, semkey, out_ap, in_ap, reads=(), writes=()):
        reads = list(dict.fromkeys(reads))
        writes = list(dict.fromkeys(writes))
        waits = self._deps(qeng, reads, writes)
        if semkey is None:
            if not hasattr(self, "pool_keys"):
                self.pool_keys = [self.dma_sem() for _ in range(16)]
                self.pool_i = 0
            semkey = self.pool_keys[self.pool_i % len(self.pool_keys)]
            self.pool_i += 1
            prev = self.cnt[semkey]
            if prev > 0 and self.waited[qeng].get(semkey, 0) < prev:
                waits = [w for w in waits if w[0] != semkey] + [(semkey, prev)]
                self.waited[qeng][semkey] = prev
        self.cnt[semkey] += 16
        tok = (semkey, self.cnt[semkey])
        sem = self.sem[semkey]
        sems = self.sem

        def run(e, waits=waits, sem=sem, out_ap=out_ap, in_ap=in_ap):
            for key, val in waits:
                e.wait_ge(sems[key], val)
            e.dma_start(out=out_ap, in_=in_ap).then_inc(sem, 16)
        self.q[qeng].append(run)
        self._record(tok, reads, writes)
        return tok

    def new_phase(self):
        self.barrier()
        self.phase += 1
        for e in ("pe", "act", "dve", "pool"):
            k = f"{e}#{self.phase}"
            self.ekey[e] = k
            self.sem[k] = self.base.enter_context(self.nc.semaphore(f"sem_{e}_{self.phase}"))
            self.cnt[k] = 0

    def collective_allgather(self, src_t, dst_t, src_ap, dst_ap, groups):
        key = self.dma_sem()
        waits = self._deps("pool", [src_t], [dst_t])
        self.cnt[key] += 1
        tok = (key, self.cnt[key])
        sems = self.sem

        def run(e, waits=waits, key=key):
            for k_, v_ in waits:
                e.wait_ge(sems[k_], v_)
            e.collective_compute("AllGather", mybir.AluOpType.bypass, replica_groups=groups,
                                 ins=[src_ap], outs=[dst_ap]).then_inc(sems[key])
        self.q["pool"].append(run)
        self._record(tok, [src_t], [dst_t])
        return tok

    def barrier(self):
        snap = dict(self.cnt)
        sems = self.sem
        for eng in self.ENG:
            waits = []
            for key, val in snap.items():
                if val > 0 and self.waited[eng].get(key, 0) < val:
                    waits.append((key, val))
                    self.waited[eng][key] = val

            def run(e, waits=waits):
                for key, val in waits:
                    e.wait_ge(sems[key], val)
            self.q[eng].append(run)

    def emit(self):
        nc = self.nc
        q = self.q
        self.q = {e: [] for e in self.ENG}
        with nc.Block() as block:
            @block.tensor
            def _(e):
                for f in q["pe"]:
                    f(e)

            @block.scalar
            def _(e):
                for f in q["act"]:
                    f(e)

            @block.vector
            def _(e):
                for f in q["dve"]:
                    f(e)

            @block.gpsimd
            def _(e):
                for f in q["pool"]:
                    f(e)

            @block.sync
            def _(e):
                for f in q["sp"]:
                    f(e)


def _ts(*refs):
    out = []
    for r in refs:
        if isinstance(r, Ref):
            out.extend(r.ts)
    return out


class Prog:
    def __init__(self, nc, stack):
        self.nc = nc
        self.S = Sched(nc, stack)
        ph = stack.enter_context(nc.psum_tensor("psum_all", [128, 8, 512], F32))
        phb = ph.bitcast(BF16)
        self.ph = ph
        self.banks = [T(ph, f"bank{j}", hb=phb) for j in range(8)]
        self.pool = list(range(8))
        self.pool_pos = {}

    def set_pool(self, pool):
        self.pool = pool

    def stream_sems(self):
        if not hasattr(self, "_ss"):
            S = self.S
            two = lambda: [S.dma_sem(), S.dma_sem()]
            self._ss = (two(), two(), two(), two(), two(), two(), S.dma_sem(), S.dma_sem())
        return self._ss

    def handoff_sems(self):
        if not hasattr(self, "_hs"):
            S = self.S
            self._hs = [[S.dma_sem(), S.dma_sem()] for _ in range(5)]
        return self._hs

    def bank(self):
        key = tuple(self.pool)
        i = self.pool_pos.get(key, 0)
        self.pool_pos[key] = i + 1
        j = self.pool[i % len(self.pool)]
        return j, self.banks[j]

    def bank2(self):
        key = tuple(self.pool)
        i = self.pool_pos.get(key, 0)
        if i % 2:
            i += 1
        self.pool_pos[key] = i + 2
        j = self.pool[i % len(self.pool)]
        return j, View((self.banks[j], self.banks[j + 1]), self.ph[:, j:j + 2, :])

    def mm(self, out, lhsT, rhs, start=True, stop=True):
        self.S.op("pe", lambda e: e.matmul(out.ap, lhsT=lhsT.ap, rhs=rhs.ap, start=start, stop=stop),
                  reads=_ts(lhsT, rhs), writes=_ts(out))

    def tr(self, out, in_, ident):
        self.S.op("pe", lambda e: e.transpose(out=out.ap, in_=in_.ap, identity=ident.ap),
                  reads=_ts(in_, ident), writes=_ts(out))

    def act(self, out, in_, func, bias=None, scale=None):
        kw = {}
        if bias is not None:
            kw["bias"] = bias.ap if isinstance(bias, Ref) else bias
        if scale is not None:
            kw["scale"] = scale.ap if isinstance(scale, Ref) else scale
        self.S.op("act", lambda e: e.activation(out=out.ap, in_=in_.ap, func=func, **kw),
                  reads=_ts(in_, bias, scale), writes=_ts(out))

    def cp(self, eng, out, in_):
        if eng == "act":
            self.S.op("act", lambda e: e.copy(out=out.ap, in_=in_.ap), reads=_ts(in_), writes=_ts(out))
        else:
            self.S.op(eng, lambda e: e.tensor_copy(out=out.ap, in_=in_.ap), reads=_ts(in_), writes=_ts(out))

    def tt(self, eng, out, a, b, op):
        self.S.op(eng, lambda e: e.tensor_tensor(out=out.ap, in0=a.ap, in1=b.ap, op=op),
                  reads=_ts(a, b), writes=_ts(out))

    def ts(self, eng, out, a, s1, s2, op0, op1=None):
        v1 = s1.ap if isinstance(s1, Ref) else s1
        v2 = s2.ap if isinstance(s2, Ref) else s2
        if op1 is None:
            self.S.op(eng, lambda e: e.tensor_scalar(out=out.ap, in0=a.ap, scalar1=v1, scalar2=None, op0=op0),
                      reads=_ts(a, s1), writes=_ts(out))
        else:
            self.S.op(eng, lambda e: e.tensor_scalar(out=out.ap, in0=a.ap, scalar1=v1, scalar2=v2, op0=op0, op1=op1),
                      reads=_ts(a, s1, s2), writes=_ts(out))

    def stt(self, out, a, s, b, op0, op1):
        v = s.ap if isinstance(s, Ref) else s
        self.S.op("dve", lambda e: e.scalar_tensor_tensor(out=out.ap, in0=a.ap, scalar=v, in1=b.ap, op0=op0, op1=op1),
                  reads=_ts(a, s, b), writes=_ts(out))

    def red(self, out, in_, op=ALU.add):
        self.S.op("dve", lambda e: e.tensor_reduce(out=out.ap, in_=in_.ap, axis=AX.X, op=op),
                  reads=_ts(in_), writes=_ts(out))

    def scan_add(self, out, ones, data):
        self.S.op("dve", lambda e: e.tensor_tensor_scan(out=out.ap, data0=ones.ap, data1=data.ap, initial=0.0,
                                                        op0=ALU.mult, op1=ALU.add),
                  reads=_ts(ones, data), writes=_ts(out))

    def bn_stats(self, out, in_):
        self.S.op("dve", lambda e: e.bn_stats(out=out.ap, in_=in_.ap), reads=_ts(in_), writes=_ts(out))

    def bn_aggr(self, out, in_):
        self.S.op("dve", lambda e: e.bn_aggr(out=out.ap, in_=in_.ap), reads=_ts(in_), writes=_ts(out))

    def recip(self, out, in_):
        self.S.op("dve", lambda e: e.reciprocal(out=out.ap, in_=in_.ap), reads=_ts(in_), writes=_ts(out))

    def memset(self, eng, out, val):
        self.S.op(eng, lambda e: e.memset(out.ap, val), reads=[], writes=_ts(out))

    def load(self, semkey, out, src, q="sp"):
        if isinstance(src, Ref):
            return self.S.dma(q, semkey, out.ap, src.ap, reads=_ts(src), writes=_ts(out))
        return self.S.dma(q, semkey, out.ap, src, writes=_ts(out))

    def store(self, semkey, dst, in_, q="sp"):
        if isinstance(dst, Ref):
            return self.S.dma(q, semkey, dst.ap, in_.ap, reads=_ts(in_), writes=_ts(dst))
        return self.S.dma(q, semkey, dst, in_.ap, reads=_ts(in_))


def emit_pass(Pg, ident, identb, mode, NT, dr, NR):
    P2 = mode == "P2"
    RW = not P2
    nc = Pg.nc
    t_rp, t_yh, t_bv, t_ps, t_pl = dr["t_rp"], dr["t_yh"], dr["t_bv"], dr["t_ps"], dr["t_pl"]
    NTOK = NT * TT
    d_xh = dr["xh"]
    d_wfm, d_wtm, d_rwp, d_w2, d_a2, d_row = dr["wfm"], dr["wtm"], dr["rwp"], dr["w2"], dr["a2"], dr["rowp"]
    d_conv, d_plw = dr["convw"], dr["plw"]
    d_mle, d_mgt, d_mG, d_dec, d_zx, d_gb = dr["mask_le"], dr["mask_gt"], dr["mask_g"], dr["dect"], dr["zetaxi"], dr["gchunk"]
    d_cs, d_cnt0 = dr["cs2"], dr["cnt0"]
    if P2:
        d_sall, d_msk, d_gseg = dr["sum_all"], dr["seg_mask"], dr["gseg"]
        d_p, d_wout, d_plew, d_pleg, d_out = dr["p"], dr["w_out"], dr["ple_w"], dr["ple_gate"], dr["xout"]
        t_mix, t_xTs = dr["t_mix"], dr["t_xTs"]
    else:
        d_osum = dr["o_sum"]
    R_MUV, R_LNW, R_LNB, R_PLS, R_MLLN, R_LNG, R_LNBB, R_IB, R_OMUV = 0, 256, 512, 768, 1024, 1280, 2304, 3328, 3336
    out_toks = []
    S = Pg.S
    mm, tr, act, cp, tt, ts, stt, red = Pg.mm, Pg.tr, Pg.act, Pg.cp, Pg.tt, Pg.ts, Pg.stt, Pg.red
    sb = S.sbuf
    ld_x, ld_p, ld_m, ld_t, ld_cs, st_o, st_sx, st_sm = Pg.stream_sems()
    ld_h = Pg.handoff_sems()
    if True:
        with contextlib.ExitStack() as stM:
            S.stack = stM
            mle = sb([128, 128], F32, "mle")
            mgt = sb([128, 128], F32, "mgt")
            mG = sb([64, 128], F32, "mG")
            dect = sb([128, 4, 128], F32, "dect")
            zx = sb([128, 8], F32, "zx")
            gb = sb([64, 4, 64], F32, "gb")
            cnt0 = sb([128, 2, 128], F32, "cnt0")
            rwp = sb([64, 48], F32, "rwp")
            w2 = sb([64, G], F32, "w2")
            a2 = sb([64, G], F32, "a2")
            rowp = sb([128, 1280 + 8 + 256 + 8], F32, "rowp")
            convw = sb([64, 2, 4, 4], F32, "convw")
            plwb = sb([128, 2, 128], BF16, "plwb")
            ones = sb([128, 128], F32, "ones")
            sm = sb([128, 96], F32, "sm")
            sm2 = sb([64, 16], F32, "sm2")
            RM_IB, RM_OMUV = 1280, 1288
            Pg.load(None, mle[:], d_mle)
            Pg.load(None, mgt[:], d_mgt)
            Pg.load(None, mG[:], d_mG)
            Pg.load(None, dect[:], d_dec)
            Pg.load(None, zx[:], d_zx)
            Pg.load(None, gb[:], d_gb)
            Pg.load(None, cnt0[:], d_cnt0)
            Pg.load(None, rwp[:, 0:30], d_rwp)
            Pg.load(None, w2[:], d_w2)
            Pg.load(None, a2[:], d_a2)
            Pg.load(None, rowp[:, 0:1280], d_row[0:1280].partition_broadcast(128))
            Pg.load(None, rowp[:, RM_IB:RM_IB + 8], d_row[R_IB:R_IB + 8].partition_broadcast(128))
            Pg.load(None, convw[:], d_conv)
            Pg.memset("dve", ones[:], 1.0)
            ts("dve", rwp[:, 30:40], rwp[:, 0:10], -1.0, 1.0, ALU.mult, ALU.add)
            ts("dve", rwp[:, 40:44], rwp[:, 22:26], -1.0, 1.0, ALU.mult, ALU.add)
            ts("dve", rowp[:, RM_OMUV:RM_OMUV + 256], rowp[:, R_MUV:R_MUV + 256], -1.0, 1.0, ALU.mult, ALU.add)
            ts("dve", rowp[:, RM_IB:RM_IB + 4], rowp[:, RM_IB:RM_IB + 4], math.log(0.125), None, ALU.add)

            NSL = 45 if RW else 38
            A = [S.slot(f"A{i}") for i in range(NSL)]

            def f64(sl, h=4):
                return sl.view(sl.h[0:64, :].rearrange("p (h t) -> p h t", h=h))

            def b64(sl, lo, n, c=8):
                hh = sl.hb if RDT == BF16 else sl.h
                return sl.view(hh[0:64, lo:lo + n].rearrange("p (c d) -> p c d", c=c))
            zs_r = f64(A[0]); rk = zs_r
            zs_k = f64(A[1]); kmod = zs_k
            sig = f64(A[2]); esc = sig; kq = sig; kk = sig
            aa = f64(A[3]); bbv = aa
            gsc = f64(A[4]); cumI = gsc; pinv = gsc
            tA = f64(A[5]); cE = tA; pprev = tA
            tB = f64(A[6]); dl = tB; pLp = tB
            pp = f64(A[7])
            tC = f64(A[8])
            zs_wa = A[9].view(A[9].h[0:64, 0:256].rearrange("p (h t) -> p h t", h=2))
            th = A[9].view(A[9].h[0:64, 256:384])
            Vc = A[10].view(A[10].h[0:64, :].rearrange("p (c n) -> p c n", c=2))
            tV = A[11].view(A[11].h[0:64, :].rearrange("p (c n) -> p c n", c=2)); grw = tV
            if RDT == BF16:
                AR = A[12].view(A[12].hb[0:64, :].rearrange("p (c w l) -> p c w l", c=8, w=2))
                Bt, Kt = b64(A[13], 0, 512), b64(A[13], 512, 512)
                Bpf, Kpf = b64(A[14], 0, 512), b64(A[14], 512, 512)
                Bptm, Kptm = b64(A[15], 0, 512), b64(A[15], 512, 512)
                Gb = b64(A[16], 0, 1024)
                Gk = b64(A[17], 0, 1024)
                Pn = [b64(A[18], 0, 512), b64(A[18], 512, 512)]
                Ptb = [b64(A[19], 0, 512), b64(A[19], 512, 512)]
                X = [b64(A[20], 0, 1024), b64(A[21], 0, 1024)]
                Phi, Rhat = b64(A[22], 0, 512), b64(A[22], 512, 512)
                Vcr = A[24].view(A[24].hb[0:64, 0:512].rearrange("p (c n) -> p c n", c=2))
            else:
                raise NotImplementedError
            Yrw = A[23].view(A[23].h[0:64, :].rearrange("p (c d) -> p c d", c=8))
            mixa = A[24].view(A[24].hb[0:64, 512:1024].rearrange("p (c n) -> p c n", c=2))
            BVt = A[28].view(A[28].h[0:64, :].rearrange("p (c d) -> p c d", c=8))
            psi = A[29].view(A[29].h[0:64, :].rearrange("p (c n) -> p c n", c=2))
            pLt = sm2.view(sm2.h[0:64, 8:16])
            RPl = [A[0].view(A[0].hb[0:64, :]), A[7].view(A[7].hb[0:64, :])]
            YHl = [A[9].view(A[9].h[0:64, :]), A[10].view(A[10].h[0:64, :])]
            BVl = [A[12].view(A[12].h[0:64, :]), A[15].view(A[15].h[0:64, :])]
            PSl = [A[36].view(A[36].h[0:64, :]), A[37].view(A[37].h[0:64, :])]
            PLl = [sm.view(sm.h[0:64, 80:88]), sm.view(sm.h[0:64, 88:96])]
            IFB = [(AR, Bt, Kt, Bpf, Kpf, Vc, Vcr, pp, zs_r)]
            if RW:
                IFB.append((A[38].view(A[38].hb[0:64, :].rearrange("p (c w l) -> p c w l", c=8, w=2)),
                            b64(A[39], 0, 512), b64(A[39], 512, 512), b64(A[40], 0, 512), b64(A[40], 512, 512),
                            A[41].view(A[41].h[0:64, :].rearrange("p (c n) -> p c n", c=2)),
                            A[42].view(A[42].hb[0:64, 0:512].rearrange("p (c n) -> p c n", c=2)),
                            f64(A[43]), f64(A[44])))
            else:
                IFB.append(IFB[0])
            cq, ck, ctmp = tA, tB, tC
            qT = A[25].view(A[25].hb[0:64, 0:512].rearrange("p (h t) -> p h t", h=4))
            kT = A[25].view(A[25].hb[0:64, 512:1024].rearrange("p (h t) -> p h t", h=4))
            f128 = lambda sl: sl.view(sl.h[:, :].rearrange("p (h t) -> p h t", h=4))
            Ah, Eh, Em = f128(A[2]), f128(A[3]), f128(A[4])
            PTm = A[26].view(A[26].hb[:, 0:512].rearrange("p (h t) -> p h t", h=4))
            ktw = A[26].view(A[26].hb[:, 512:768].rearrange("p (h d) -> p h d", h=4))
            Vaug = A[27].view(A[27].hb[:, 0:260].rearrange("p (h d) -> p h d", h=4))
            vrt = A[27].view(A[27].hb[:, 260:516].rearrange("p (h d) -> p h d", h=4))
            vz = A[27].view(A[27].hb[:, 516:772].rearrange("p (h d) -> p h d", h=4))
            numt = A[28].view(A[28].h[:, 0:260].rearrange("p (h d) -> p h d", h=4))
            numi = A[29].view(A[29].h[:, 0:260].rearrange("p (h d) -> p h d", h=4))
            osg = A[30].view(A[30].h[:, 0:256])
            Yml = A[30].view(A[30].h[:, 256:512].rearrange("p (h d) -> p h d", h=4))
            ctmp2 = A[1].view(A[1].h[0:64, 0:260].rearrange("p (h d) -> p h d", h=4))
            rtmp = A[5].view(A[5].h[0:64, 0:256].rearrange("p (h d) -> p h d", h=4))
            rAs, rBs = (A[33], A[30]) if RW else (A[13], A[14])
            rA = rAs.view(rAs.h[:, :].rearrange("p (q w i) -> p q w i", q=8, w=2))
            rB = rBs.view(rBs.h[:, :].rearrange("p (q w i) -> p q w i", q=8, w=2))
            qkr = A[31].view(A[31].hb[:, 0:512].rearrange("p (q d) -> p q d", q=8))
            PTr = A[31].view(A[31].hb[:, 512:1024].rearrange("p (h t) -> p h t", h=4))
            qkT = A[32].view(A[32].hb[0:64, :].rearrange("p (q t) -> p q t", q=8))
            yint = A[33].view(A[33].h[:, 0:256].rearrange("p (h d) -> p h d", h=4))
            Yrt = A[33].view(A[33].h[:, 256:512].rearrange("p (h d) -> p h d", h=4))
            s2 = A[16].view(A[16].h[:, 0:288].rearrange("p (m t) -> p m t", m=2))
            s4 = A[17].view(A[17].h[:, 0:288].rearrange("p (m t) -> p m t", m=2))
            s8 = A[18].view(A[18].h[:, 0:144])
            s16 = A[18].view(A[18].h[:, 144:288])
            pm = A[19].view(A[19].h[:, 0:256].rearrange("p (m t) -> p m t", m=2))
            pd = A[19].view(A[19].hb[:, 512:768].rearrange("p (m t) -> p m t", m=2))
            g2 = A[20].view(A[20].h[:, :])
            g3 = A[21].view(A[21].h[:, 0:256])
            hbig = A[22].view(A[22].h[:, :].rearrange("p (g d) -> p g d", g=8))
            mixb = A[34].view(A[34].hb[:, 0:768])
            mixT = A[35].view(A[35].hb[:, :].rearrange("p (k t) -> p k t", k=8))
            smv = lambda lo, n, P_=128: sm.view(sm.h[0:P_, lo:lo + n])
            hsum, hsq, hm, hv = smv(0, 8), smv(8, 8), smv(16, 8), smv(24, 8)
            gat, nlf, ibs, ebl = smv(32, 8), smv(40, 4), smv(44, 4), smv(48, 4)
            etot, dden, bon = smv(52, 4, 64), smv(56, 4), sm2.view(sm2.h[0:64, 0:8])
            NBT = smv(68, 4, 64)

            wfm = sb([128, 8, FMC], BF16, "wfm")
            wtm = sb([128, 8, TMC], BF16, "wtm")
            cast_engs = ["dve", "pool", "act"]
            si = 0
            for dst, dram, ncols in ((wfm, d_wfm, FMC), (wtm, d_wtm, TMC)):
                for k in range(8):
                    for c0 in range(0, ncols, 512):
                        c1 = min(ncols, c0 + 512)
                        sl = A[si % 12 + 20] if si % 12 + 20 < NSL else A[si % 12]
                        Pg.load(None, sl[:, 0:c1 - c0], dram[k * 128:(k + 1) * 128, c0:c1])
                        cp(cast_engs[si % 3], dst[:, k, c0:c1], sl[:, 0:c1 - c0])
                        si += 1
            sl = A[0]
            Pg.load(None, sl[:, 0:256], d_plw.rearrange("p m n -> p (m n)"))
            cp("dve", plwb[:].re("p m n -> p (m n)"), sl[:, 0:256])

            xt = sb([128, D], F32, "xt")
            xTb = sb([128, 8, 144], BF16, "xTb")
            Zr = sb([64, 4, 144], F32, "Zr")
            Zk = sb([64, 4, 144], F32, "Zk")
            Zwa = sb([64, 2, 144], F32, "Zwa")
            Zu = sb([128, 2, 144], F32, "Zu")
            Zq = sb([64, 4, 144], F32, "Zq")
            Zkm = sb([64, 4, 144], F32, "Zkm")
            SW = 64 if P2 else 128
            S32 = sb([64, 4, SW], F32, "S32")
            Sr = sb([64, 4, SW], RDT, "Sr")
            C32 = sb([64, 4, 65], F32, "C32")
            Cb = sb([64, 4, 65], BF16, "Cb")
            R32 = sb([64, 4, 64], F32, "R32")
            Rb = sb([64, 4, 64], BF16, "Rb")
            cst = [sb([128, 128], F32, "cs0"), sb([128, 128], F32, "cs1")]
            Pg.memset("dve", S32[:], 0.0)
            Pg.memset("dve", C32[:], 0.0)
            Pg.memset("dve", R32[:], 0.0)
            if not P2:
                for h in range(4):
                    cp("pool", S32[:, h, 64:128], ident[0:64, 0:64])
            else:
                msk = sb([64, NR], F32, "msk")
                gseg = sb([64, 4, 64], F32, "gseg")
                Pg.load(None, msk[:], d_msk)
                Pg.load(None, gseg[:], d_gseg)
                c_sg = A[0].view(A[0].h[0:64, :].rearrange("p (h n) -> p h n", h=4))
                c_ml = A[1].view(A[1].h[0:64, 0:260].rearrange("p (h n) -> p h n", h=4))
                c_bt = A[1].view(A[1].h[0:64, 300:304])
                c_rt = A[2].view(A[2].h[0:64, 0:256].rearrange("p (h n) -> p h n", h=4))
                c_gt = A[3].view(A[3].h[0:64, 0:256].rearrange("p (h n) -> p h n", h=4))
                c_cd = A[4].view(A[4].h[0:64, 0:260].rearrange("p (h n) -> p h n", h=4))
                for js in range(NR - 1):
                    rs = slice(js * 64, (js + 1) * 64)
                    Pg.load(None, c_sg[:], d_sall[rs, 0:512].rearrange("p (h n) -> p h n", h=4))
                    Pg.load(None, c_ml[:], d_sall[rs, 512:772].rearrange("p (h n) -> p h n", h=4))
                    Pg.load(None, c_bt[:], d_sall[rs, 772:776])
                    Pg.load(None, c_rt[:], d_sall[rs, 776:1032].rearrange("p (h n) -> p h n", h=4))
                    mj = msk[:, js:js + 1]
                    j, bk = Pg.bank()
                    for h in range(4):
                        tr(bk[0:64, j, h * 64:(h + 1) * 64], c_sg[:, h, 64:128], ident[0:64, 0:64])
                    cp("act", c_gt[:], bk[0:64, j, 0:256].re("p (h n) -> p h n", h=4))
                    j, bk = Pg.bank()
                    for h in range(4):
                        mm(bk[0:64, j, h * 64:(h + 1) * 64], c_gt[:, h, :], S32[:, h, :])
                    tt("dve", c_cd[:, :, 0:64], bk[0:64, j, 0:256].re("p (h n) -> p h n", h=4), c_sg[:, :, 0:64], ALU.add)
                    tt("dve", c_cd[:, :, 0:64], c_cd[:, :, 0:64], S32[:], ALU.subtract)
                    stt(S32[:], c_cd[:, :, 0:64], mj, S32[:], ALU.mult, ALU.add)
                    act(c_bt[:], c_bt[:], AF.Exp, scale=-1.0)
                    tt("dve", c_cd[:], C32[:], c_bt[:].re("p (h o) -> p h o", o=1).bc([64, 4, 65]), ALU.mult)
                    tt("dve", c_cd[:], c_cd[:], c_ml[:], ALU.add)
                    tt("dve", c_cd[:], c_cd[:], C32[:], ALU.subtract)
                    stt(C32[:], c_cd[:], mj, C32[:], ALU.mult, ALU.add)
                    tt("dve", c_cd[:, :, 0:64], R32[:], gseg[:], ALU.mult)
                    tt("dve", c_cd[:, :, 0:64], c_cd[:, :, 0:64], c_rt[:], ALU.add)
                    tt("dve", c_cd[:, :, 0:64], c_cd[:, :, 0:64], R32[:], ALU.subtract)
                    stt(R32[:], c_cd[:, :, 0:64], mj, R32[:], ALU.mult, ALU.add)
            cp("dve", Sr[:], S32[:])
            cp("dve", Cb[:], C32[:])
            cp("dve", Rb[:], R32[:])
            Pg.memset("dve", NBT[:], 0.0)

            FM_TILES = [("r", Zr, 0), ("k", Zk, 256), ("wa", Zwa, 512), ("u", Zu, 640), ("q", Zq, 896), ("km", Zkm, 1152)]
            if not P2:
                FM_TILES = [f for f in FM_TILES if f[0] in ("r", "k", "wa", "km")]
            else:
                FM_TILES = [f for f in FM_TILES if f[0] in ("u", "q", "km")]

            def proj_fm(c_lo, c_hi, d_lo):
                n = c_hi - c_lo
                for name, Z, off in FM_TILES:
                    j, bk = Pg.bank()
                    if name == "u":
                        for m in range(2):
                            for k in range(8):
                                mm(bk[:, j, m * 128:m * 128 + n], wfm[:, k, off + m * 128: off + (m + 1) * 128], xTb[:, k, c_lo:c_hi],
                                   start=(k == 0), stop=(k == 7))
                        cp("act", Z[:, :, d_lo:d_lo + n], bk[:, j, 0:256].re("p (m t) -> p m t", m=2)[:, :, 0:n])
                    else:
                        nm = 2 if name == "wa" else 4
                        for m in range(nm):
                            for k in range(8):
                                mm(bk[0:64, j, m * 128:m * 128 + n], wfm[:, k, off + m * 64: off + (m + 1) * 64], xTb[:, k, c_lo:c_hi],
                                   start=(k == 0), stop=(k == 7))
                        cp("act", Z[:, :, d_lo:d_lo + n], bk[0:64, j, 0:nm * 128].re("p (m t) -> p m t", m=nm)[:, :, 0:n])

            Pg.load(ld_x[0], xt[0:16, :], d_xh[0:HALO, :])
            j, bk = Pg.bank()
            for k in range(8):
                tr(bk[:, j, k * 16:(k + 1) * 16], xt[0:16, k * 128:(k + 1) * 128], ident[0:16, 0:16])
            cp("act", xTb[:, :, 128:144], bk[:, j, 0:128].re("p (k t) -> p k t", k=8))
            proj_fm(128, 144, 128)
            Pg.memset("dve", Vaug[:], 1.0)
            if not P2:
                Pg.memset("pool", AR[:, :, 1, :], 0.0)

            def b4(col, w=128):
                return rwp[:, col:col + 4].re("p (h o) -> p h o", o=1).bc([64, 4, w])

            def b2(col, w=128):
                return rwp[:, col:col + 2].re("p (h o) -> p h o", o=1).bc([64, 2, w])

            def v8(t_):
                return t_[:].re("p h (c l) -> p (h c) l", c=2)

            def tmproj(col_lo, ncols):
                j, bk = Pg.bank()
                for k in range(8):
                    mm(bk[:, j, 0:ncols], xTb[:, k, 16:144], wtm[:, k, col_lo:col_lo + ncols], start=(k == 0), stop=(k == 7))
                return j, bk

            def headnorm(Y, P_, ng, eps):
                red(hsum[0:P_, 0:ng], Y)
                tt("pool", hbig[0:P_, 0:ng, :], Y, Y, ALU.mult)
                red(hsq[0:P_, 0:ng], hbig[0:P_, 0:ng, :])
                ts("dve", hm[0:P_, 0:ng], hsum[0:P_, 0:ng], 1.0 / 64, None, ALU.mult)
                tt("dve", hv[0:P_, 0:ng], hm[0:P_, 0:ng], hm[0:P_, 0:ng], ALU.mult)
                stt(hv[0:P_, 0:ng], hsq[0:P_, 0:ng], 1.0 / 64, hv[0:P_, 0:ng], ALU.mult, ALU.subtract)
                ts("dve", hv[0:P_, 0:ng], hv[0:P_, 0:ng], eps, None, ALU.add)
                act(hv[0:P_, 0:ng], hv[0:P_, 0:ng], AF.Ln)
                act(hv[0:P_, 0:ng], hv[0:P_, 0:ng], AF.Exp, scale=-0.5)
                tt("dve", Y, Y, hm[0:P_, 0:ng].re("p (g o) -> p g o", o=1).bc([P_, ng, 64]), ALU.subtract)
                tt("dve", Y, Y, hv[0:P_, 0:ng].re("p (g o) -> p g o", o=1).bc([P_, ng, 64]), ALU.mult)

            def tile_body(ti, part):
                AR, Bt, Kt, Bpf, Kpf, Vc, Vcr, pp, zs_r = IFB[ti % 2]
                rk = zs_r
                if part != "B":
                    cp("pool", xTb[:, :, 0:16], xTb[:, :, 128:144])
                    for name, Z, off in FM_TILES:
                        cp("pool", Z[:, :, 0:16], Z[:, :, 128:144])
                    yield
                    Pg.load(ld_x[0], xt[:], d_xh[HALO + ti * TT: HALO + (ti + 1) * TT, :])
                    Pg.load(ld_cs[ti % 2], cst[ti % 2][:], d_cs[ti * TT:(ti + 1) * TT, :])
                    yield
                    j, b2v = Pg.bank2()
                    for k in range(8):
                        tr(b2v[:, k // 4, (k % 4) * 128:(k % 4 + 1) * 128], xt[:, k * 128:(k + 1) * 128], ident[:])
                    cp("act", xTb[:, 0:4, 16:144], b2v[:, 0, :].re("p (k t) -> p k t", k=4))
                    cp("dve", xTb[:, 4:8, 16:144], b2v[:, 1, :].re("p (k t) -> p k t", k=4))
                    if P2:
                        Pg.store(st_sx, t_xTs[ti].re("p (k t) -> p k t", k=8), xTb[:, :, 16:144])
                    proj_fm(16, 144, 16)

                    yield
                    if RW:
                        jv, bv = Pg.bank2()
                        for c in range(2):
                            for w_, sh in ((0, 16), (1, 15)):
                                for k in range(8):
                                    mm(bv[0:64, c, w_ * 256:(w_ + 1) * 256], xTb[:, k, sh + c * 64: sh + (c + 1) * 64], wtm[:, k, 0:256],
                                       start=(k == 0), stop=(k == 7))
                        bvv = bv[0:64, :, :].re("p c (w n) -> p c w n", w=2)
                        tt("dve", tV[:], bvv[:, :, 1, :], rowp[0:64, R_MUV:R_MUV + 256].re("p (o n) -> p o n", o=1).bc([64, 2, 256]), ALU.mult)
                        tt("dve", Vc[:], bvv[:, :, 0, :], rowp[0:64, RM_OMUV:RM_OMUV + 256].re("p (o n) -> p o n", o=1).bc([64, 2, 256]), ALU.mult)
                        tt("dve", Vc[:], Vc[:], tV[:], ALU.add)
                        cp("pool", Vcr[:], Vc[:])
                    if P2:
                        jg, bg = Pg.bank()
                        for c in range(2):
                            for k in range(8):
                                mm(bg[0:64, jg, c * 256:(c + 1) * 256], xTb[:, k, 16 + c * 64: 16 + (c + 1) * 64], wtm[:, k, 256:512],
                                   start=(k == 0), stop=(k == 7))
                        act(grw[:], bg[0:64, jg, :].re("p (c n) -> p c n", c=2), AF.Silu)

                    yield
                    if RW:
                        if True:
                            tt("dve", tA[:], Zr[:, :, 16:144], b4(30), ALU.mult)
                            tt("pool", tB[:], Zr[:, :, 15:143], b4(0), ALU.mult)
                            tt("dve", zs_r[:], tA[:], tB[:], ALU.add)
                        tt("dve", tA[:], Zk[:, :, 16:144], b4(34), ALU.mult)
                        tt("pool", tB[:], Zk[:, :, 15:143], b4(4), ALU.mult)
                        tt("dve", zs_k[:], tA[:], tB[:], ALU.add)
                        tt("dve", tA[:, 0:2, :], Zwa[:, :, 16:144], b2(38), ALU.mult)
                        tt("pool", tB[:, 0:2, :], Zwa[:, :, 15:143], b2(8), ALU.mult)
                        tt("dve", zs_wa[:], tA[:, 0:2, :], tB[:, 0:2, :], ALU.add)
                        act(th[:], zs_wa[:, 0, :], AF.Tanh)
                        yield
                        j, bk = Pg.bank()
                        for h in range(4):
                            mm(bk[0:64, j, h * 128:(h + 1) * 128], w2[:, h * 64:(h + 1) * 64], th[:])
                        tt("dve", tA[:], bk[0:64, j, :].re("p (h t) -> p h t", h=4), b4(10), ALU.add)
                        act(sig[:], tA[:], AF.Sigmoid)
                        yield
                        j, bk = Pg.bank()
                        for h in range(4):
                            mm(bk[0:64, j, h * 128:(h + 1) * 128], a2[:, h * 64:(h + 1) * 64], zs_wa[:, 1, :])
                        tt("dve", tB[:], bk[0:64, j, :].re("p (h t) -> p h t", h=4), b4(14), ALU.add)
                        act(aa[:], tB[:], AF.Sigmoid)
                        yield
                        Pg.scan_add(gsc[:].re("p h t -> p (h t)"), ones[0:64, 0:1].bc([64, 512]), sig[:].re("p h t -> p (h t)"))
                        tt("dve", esc[:], gsc[:], sig[:], ALU.subtract)
                        base8 = v8(esc)[:, :, 0:1].bc([64, 8, 64])
                        tt("dve", v8(cE), v8(esc), base8, ALU.subtract)
                        tt("dve", v8(cumI), v8(gsc), base8, ALU.subtract)
                        tt("dve", v8(dl), v8(cumI), v8(cumI)[:, :, 63:64].bc([64, 8, 64]), ALU.subtract)
                        act(pp[:], cumI[:], AF.Exp, scale=-C0)
                        act(pinv[:], cumI[:], AF.Exp, scale=C0)
                        act(pprev[:], cE[:], AF.Exp, scale=-C0)
                        act(pLp[:], dl[:], AF.Exp, scale=C0)
                        yield
                        tt("dve", kq[:], zs_k[:], b4(18), ALU.mult)
                        tt("pool", tC[:], kq[:], kq[:], ALU.mult)
                        yield
                        j, bk = Pg.bank()
                        mm(bk[0:64, j, :], ones[0:64, 0:64], tC[:].re("p h t -> p (h t)"))
                        ts("dve", tC[:].re("p h t -> p (h t)"), bk[0:64, j, :], 1e-18, None, ALU.max)
                        act(tC[:], tC[:], AF.Ln)
                        act(tC[:], tC[:], AF.Exp, scale=-0.5)
                        tt("dve", kk[:], kq[:], tC[:], ALU.mult)
                        yield
                        tt("dve", tC[:], aa[:], b4(22), ALU.mult)
                        tt("dve", tC[:], tC[:], b4(40), ALU.add)
                        tt("dve", kmod[:], zs_k[:], tC[:], ALU.mult)
                        tt("pool", bbv[:], kk[:], aa[:], ALU.mult)
                        yield
                        stt(AR[:, :, 0, :], v8(kk), -1.0, v8(pprev), ALU.mult, ALU.mult)
                        if True:
                            tt("dve", AR[:, :, 1, :], v8(zs_r), v8(pp), ALU.mult)
                            tt("dve", rk[:], zs_r[:], b4(26), ALU.mult)
                            tt("dve", rk[:], rk[:], kmod[:], ALU.mult)
                        tt("dve", Bt[:], v8(bbv), v8(pinv), ALU.mult)
                        tt("dve", Kt[:], v8(kmod), v8(pinv), ALU.mult)
                        tt("pool", Bpf[:], v8(bbv), v8(pLp), ALU.mult)
                        tt("pool", Kpf[:], v8(kmod), v8(pLp), ALU.mult)
                if part != "A":
                    if RW:
                        Xc = X[0]
                        for src, dst in ((AR, None), (Bpf, Bptm), (Kpf, Kptm)):
                            j, bk = Pg.bank()
                            for ch in range(8):
                                in_ = src[:, ch, 0, :] if src is AR else src[:, ch, :]
                                tr(bk.b16((slice(0, 64), j, slice(ch * 64, (ch + 1) * 64))), in_, identb[0:64, 0:64])
                            pview = bk.b16((slice(0, 64), j, slice(0, 512))).re("p (c d) -> p c d", c=8)
                            cp("act", Xc[:, :, 0:64] if dst is None else dst[:], pview)
                        yield
                        for lhs, dstG in ((Bt, Gb), (Kt, Gk)):
                            j, b2v = Pg.bank2()
                            for ch in range(8):
                                mm(b2v[0:64, ch // 4, (ch % 4) * 128:(ch % 4 + 1) * 128], lhs[:, ch, :], AR[:, ch, :, :].re("p w l -> p (w l)"))
                            tt("dve", dstG[:], b2v[0:64, :, :].re("p b (c n) -> p (b c) n", c=4),
                               mG[:].re("p (o n) -> p o n", o=1).bc([64, 8, 128]), ALU.mult)
                        yield
                        j, bk = Pg.bank()
                        for ch in range(8):
                            mm(bk[0:64, j, ch * 64:(ch + 1) * 64], AR[:, ch, 0, :], Bt[:, ch, :])
                        tt("dve", Pn[0][:], bk[0:64, j, :].re("p (c s) -> p c s", c=8),
                           mgt[0:64, 0:64].re("p (o s) -> p o s", o=1).bc([64, 8, 64]), ALU.mult)
                        yield
                        yield
                        j, bk = Pg.bank()
                        for ch in range(8):
                            h_, c_ = ch // 2, ch % 2
                            mm(bk[0:64, j, ch * 64:(ch + 1) * 64], Gk[:, ch, 0:64], Vcr[:, c_, h_ * 64:(h_ + 1) * 64])
                        cp("act", Xc[:, :, 64:128], bk[0:64, j, :].re("p (c d) -> p c d", c=8))
                        yield
                        Ptc = None
                        for lvl in range(6):
                            Xn = X[(lvl + 1) % 2]
                            yield
                            j, b2v = Pg.bank2()
                            for ch in range(8):
                                lh = Gb[:, ch, 0:64] if lvl == 0 else Ptc[:, ch, :]
                                mm(b2v[0:64, ch // 4, (ch % 4) * 128:(ch % 4 + 1) * 128], lh, Xc[:, ch, :])
                            tt("dve", Xn[:], b2v[0:64, :, :].re("p b (c n) -> p (b c) n", c=4), Xc[:], ALU.add)
                            if lvl < 5:
                                Pnc, Pnn, Ptn = Pn[lvl % 2], Pn[(lvl + 1) % 2], Ptb[(lvl + 1) % 2]
                                yield
                                j, bk = Pg.bank()
                                for ch in range(8):
                                    lhT = Gb[:, ch, 0:64] if lvl == 0 else Ptc[:, ch, :]
                                    mm(bk[0:64, j, ch * 64:(ch + 1) * 64], Pnc[:, ch, :], lhT)
                                cp("act", Ptn[:], bk[0:64, j, :].re("p (c d) -> p c d", c=8))
                                yield
                                j, bk = Pg.bank()
                                for ch in range(8):
                                    lhT = Gb[:, ch, 0:64] if lvl == 0 else Ptc[:, ch, :]
                                    mm(bk[0:64, j, ch * 64:(ch + 1) * 64], lhT, Pnc[:, ch, :])
                                cp("act", Pnn[:], bk[0:64, j, :].re("p (c d) -> p c d", c=8))
                                Ptc = Ptn
                            Xc = Xn
                        yield
                        yield
                        j, bk = Pg.bank()
                        for ch in range(8):
                            mm(bk[0:64, j, ch * 64:(ch + 1) * 64], Xc[:, ch, 0:64], Bptm[:, ch, :])
                        cp("act", Phi[:], bk[0:64, j, :].re("p (c d) -> p c d", c=8))
                        yield
                        j, bk = Pg.bank()
                        for ch in range(8):
                            mm(bk[0:64, j, ch * 64:(ch + 1) * 64], Xc[:, ch, 0:64], Gb[:, ch, 64:128])
                        tt("dve", Rhat[:], bk[0:64, j, :].re("p (c d) -> p c d", c=8), AR[:, :, 1, :], ALU.add)
                        Pg.store(None, t_rp[ti], A[22].b16((slice(0, 64), slice(0, 1024))))
                        yield
                        yield
                        j, bk = Pg.bank()
                        for ch in range(8):
                            h_, c_ = ch // 2, ch % 2
                            o = bk[0:64, j, ch * 64:(ch + 1) * 64]
                            mm(o, Gb[:, ch, 64:128], Xc[:, ch, 64:128], start=True, stop=False)
                            mm(o, Gk[:, ch, 64:128], Vcr[:, c_, h_ * 64:(h_ + 1) * 64], start=False, stop=True)
                        cp("act", Yrw[:], bk[0:64, j, :].re("p (c d) -> p c d", c=8))
                        Pg.store(None, t_yh[ti], Yrw[:].re("p c d -> p (c d)"))
                        yield
                        yield
                        j, bk = Pg.bank()
                        for ch in range(8):
                            mm(bk[0:64, j, ch * 2:ch * 2 + 1], v8(rk)[:, ch, :], ones[0:64, 0:1])
                        cp("act", bon[:], bk[0:64, j, 0:16].re("p (c o) -> p c o", o=2)[:, :, 0])
                        tt("dve", BVt[:].re("p (h c) d -> p h c d", c=2), Vc[:].re("p c (h d) -> p h c d", h=4),
                           bon[:].re("p (h c o) -> p h c o", c=2, o=1).bc([64, 4, 2, 64]), ALU.mult)
                        Pg.store(None, t_bv[ti], BVt[:].re("p c d -> p (c d)"))
                        yield
                        for c in range(2):
                            j, bk = Pg.bank()
                            for h in range(4):
                                ch = h * 2 + c
                                o = bk[0:64, j, h * 64:(h + 1) * 64]
                                mm(o, Bptm[:, ch, :], Xc[:, ch, 64:128], start=True, stop=False)
                                mm(o, Kptm[:, ch, :], Vcr[:, c, h * 64:(h + 1) * 64], start=False, stop=True)
                            cp("act", psi[:, c, :], bk[0:64, j, 0:256])
                            cp("pool", pLt[:, c * 4:(c + 1) * 4], pp[:, :, c * 64 + 63])
                        Pg.store(None, t_ps[ti], psi[:].re("p c n -> p (c n)"))
                        Pg.store(None, t_pl[ti], pLt[:])
                        Rhat_u, Phi_u, yhat_u, BV_u, psi_u, pL_u = Rhat, Phi, Yrw, BVt, psi, pLt
                    else:
                        par_ = ti % 2
                        Pg.load(ld_h[0][par_], RPl[par_][:], t_rp[ti])
                        Pg.load(ld_h[1][par_], YHl[par_][:], t_yh[ti])
                        Pg.load(ld_h[2][par_], BVl[par_][:], t_bv[ti])
                        Pg.load(ld_h[3][par_], PSl[par_][:], t_ps[ti])
                        Pg.load(ld_h[4][par_], PLl[par_][:], t_pl[ti])
                        Phi_u = View(RPl[par_].ts, RPl[par_].ap[:, 0:512].rearrange("p (c d) -> p c d", c=8))
                        Rhat_u = View(RPl[par_].ts, RPl[par_].ap[:, 512:1024].rearrange("p (c d) -> p c d", c=8))
                        yhat_u = View(YHl[par_].ts, YHl[par_].ap.rearrange("p (c d) -> p c d", c=8))
                        BV_u = View(BVl[par_].ts, BVl[par_].ap.rearrange("p (c d) -> p c d", c=8))
                        psi_u = View(PSl[par_].ts, PSl[par_].ap.rearrange("p (c n) -> p c n", c=2))
                        pL_u = PLl[par_]
                    yield
                    if P2:
                        jy, by = Pg.bank()
                    for c in range(2):
                        if P2:
                            for h in range(4):
                                ch = h * 2 + c
                                mm(by[0:64, jy, ch * 64:(ch + 1) * 64], Rhat_u[:, ch, :], Sr[:, h, 0:64])
                        yield
                        js, bs = Pg.bank()
                        for h in range(4):
                            ch = h * 2 + c
                            mm(bs[0:64, js, h * 128:h * 128 + SW], Phi_u[:, ch, :], Sr[:, h, :])
                        tt("dve", S32[:], S32[:], pL_u[:, c * 4:(c + 1) * 4].re("p (h o) -> p h o", o=1).bc([64, 4, SW]), ALU.mult)
                        tt("dve", S32[:], S32[:], bs[0:64, js, :].re("p (h n) -> p h n", h=4)[:, :, 0:SW], ALU.add)
                        tt("dve", S32[:, :, 0:64], S32[:, :, 0:64], psi_u[:, c, :].re("p (h d) -> p h d", h=4), ALU.add)
                        cp("act", Sr[:], S32[:])
                    if P2:
                        tt("dve", Yrw[:], by[0:64, jy, :].re("p (c d) -> p c d", c=8), yhat_u[:], ALU.add)

                if part != "B":
                    yield
                    j, bk = tmproj(2560, 8)
                    tt("dve", gat[:], bk[:, j, 0:8], rowp[:, RM_IB:RM_IB + 8], ALU.add)
                    cp("pool", ibs[:], gat[:, 0:4])
                    act(nlf[:], gat[:, 4:8], AF.Exp, scale=-1.0)
                    act(nlf[:], nlf[:], AF.Ln, bias=1.0)
                    yield
                    j, bk = Pg.bank()
                    mm(bk[:, j, 0:4], mle[:], nlf[:])
                    mm(bk[0:64, j, 8:12], ones[:, 0:64], nlf[:])
                    act(ebl[:], bk[:, j, 0:4], AF.Exp, scale=-1.0)
                    act(etot[:], bk[0:64, j, 8:12], AF.Exp, scale=-1.0)
                    tt("dve", NBT[:], NBT[:], bk[0:64, j, 8:12], ALU.add)
                    for h in range(4):
                        ts("dve", Ah[:, h, :], mgt[:], nlf[:, h:h + 1], None, ALU.mult)
                    yield
                    j, bk = Pg.bank()
                    for h in range(4):
                        mm(bk[:, j, h * 128:(h + 1) * 128], Ah[:, h, :], mle[:])
                    for h in range(4):
                        act(Eh[:, h, :], bk[:, j, h * 128:(h + 1) * 128], AF.Exp, scale=-1.0, bias=ibs[:, h:h + 1])
                    yield
                    j, bk = tmproj(2048, 512)
                    cp("act", Vaug[:, :, 0:64], bk[:, j, 0:256].re("p (h d) -> p h d", h=4))
                    if P2:
                        cp("act", vrt[:], bk[:, j, 256:512].re("p (h d) -> p h d", h=4))
                    tt("dve", vz[:], bk[:, j, 256:512].re("p (h d) -> p h d", h=4), zx[:, 0:4].re("p (h o) -> p h o", o=1).bc([128, 4, 64]), ALU.mult)
                    for Zs, cdst, qi, outT in ((Zq, cq, 0, qT), (Zkm, ck, 1, kT)):
                        if not P2 and qi == 0:
                            continue
                        for tap in range(4):
                            wv = convw[:, qi, :, tap:tap + 1].bc([64, 4, 128])
                            src = Zs[:, :, 13 + tap:141 + tap]
                            if tap == 0:
                                tt("dve", cdst[:], src, wv, ALU.mult)
                            else:
                                tt("pool", ctmp[:], src, wv, ALU.mult)
                                tt("dve", cdst[:], cdst[:], ctmp[:], ALU.add)
                        act(outT[:], cdst[:], AF.Silu)
                    yield
                    j, bk = Pg.bank()
                    for h in range(4):
                        tr(bk.b16((slice(0, 128), j, slice(h * 64, (h + 1) * 64))), kT[:, h, :], identb[0:64, 0:64])
                    tt("dve", ktw[:], bk.b16((slice(0, 128), j, slice(0, 256))).re("p (h d) -> p h d", h=4), Eh[:, :, 127:128].bc([128, 4, 64]), ALU.mult)
                    if P2:
                        tt("pool", Em[:], Eh[:], mle[:].re("p (o n) -> p o n", o=1).bc([128, 4, 128]), ALU.mult)
                        yield
                        j, bk = Pg.bank()
                        for h in range(4):
                            mm(bk[:, j, h * 128:(h + 1) * 128], kT[:, h, :], qT[:, h, :])
                        tt("dve", PTm[:], bk[:, j, :].re("p (h l) -> p h l", h=4), Em[:], ALU.mult)
                        yield
                        ji, bki = Pg.bank()
                        yield
                        jn, bkn = Pg.bank()
                        for h in range(4):
                            mm(bki[:, ji, h * 65:(h + 1) * 65], qT[:, h, :], Cb[:, h, :])
                        for h in range(4):
                            mm(bkn[:, jn, h * 65:(h + 1) * 65], PTm[:, h, :], Vaug[:, h, :])
                        tt("dve", numi[:], bki[:, ji, 0:260].re("p (h d) -> p h d", h=4), ebl[:].re("p (h o) -> p h o", o=1).bc([128, 4, 65]), ALU.mult)
                        tt("dve", numt[:], numi[:], bkn[:, jn, 0:260].re("p (h d) -> p h d", h=4), ALU.add)
                        act(dden[:], numt[:, :, 64], AF.Abs)
                        ts("dve", dden[:], dden[:], 1.0, None, ALU.max)
                        Pg.recip(dden[:], dden[:])
                        tt("dve", Yml[:], numt[:, :, 0:64], dden[:].re("p (h o) -> p h o", o=1).bc([128, 4, 64]), ALU.mult)
                    yield
                    j, bk = Pg.bank()
                    for h in range(4):
                        mm(bk[0:64, j, h * 65:(h + 1) * 65], ktw[:, h, :], Vaug[:, h, :])
                    tt("dve", ctmp2[:], C32[:], etot[:].re("p (h o) -> p h o", o=1).bc([64, 4, 65]), ALU.mult)
                    tt("dve", C32[:], ctmp2[:], bk[0:64, j, 0:260].re("p (h d) -> p h d", h=4), ALU.add)
                    cp("act", Cb[:], C32[:])

                    yield
                    yield
                    j, bk = tmproj(1536, 512)
                    zv = bk[:, j, :].re("p (q w i) -> p q w i", q=8, w=2)
                    csv = cst[ti % 2]
                    csA = csv[:, 0:64].re("p (o w i) -> p o w i", o=1, w=2).bc([128, 8, 2, 32])
                    csB = csv[:, 64:128].re("p (o w i) -> p o w i", o=1, w=2).bc([128, 8, 2, 32])
                    tt("dve", rA[:], zv[:, :, 0:1, :].bc([128, 8, 2, 32]), csA, ALU.mult)
                    tt("dve", rB[:], zv[:, :, 1:2, :].bc([128, 8, 2, 32]), csB, ALU.mult)
                    tt("dve", qkr[:].re("p q (w i) -> p q w i", w=2), rA[:], rB[:], ALU.add)
                    if P2:
                        j, bk = Pg.bank()
                        for q_ in range(8):
                            tr(bk.b16((slice(0, 64), j, slice(q_ * 128, (q_ + 1) * 128))), qkr[:, q_, :], identb[:])
                        cp("act", qkT[:], bk.b16((slice(0, 64), j, slice(0, 1024))).re("p (q t) -> p q t", q=8))
                        yield
                        j, bk = Pg.bank()
                        for h in range(4):
                            mm(bk[:, j, h * 128:(h + 1) * 128], qkT[:, 4 + h, :], qkT[:, h, :])
                        tt("dve", PTr[:], bk[:, j, :].re("p (h l) -> p h l", h=4), dect[:], ALU.mult)
                        yield
                        ji, bki = Pg.bank()
                        yield
                        jn, bkn = Pg.bank()
                        for h in range(4):
                            mm(bki[:, ji, h * 64:(h + 1) * 64], qkT[:, h, :], Rb[:, h, :])
                        for h in range(4):
                            mm(bkn[:, jn, h * 64:(h + 1) * 64], PTr[:, h, :], vrt[:, h, :])
                        tt("dve", yint[:], bki[:, ji, 0:256].re("p (h d) -> p h d", h=4), zx[:, 4:8].re("p (h o) -> p h o", o=1).bc([128, 4, 64]), ALU.mult)
                        tt("dve", Yrt[:], yint[:], bkn[:, jn, 0:256].re("p (h d) -> p h d", h=4), ALU.add)
                    yield
                    j, bk = Pg.bank()
                    for h in range(4):
                        mm(bk[0:64, j, h * 64:(h + 1) * 64], qkr[:, 4 + h, :], vz[:, h, :])
                    tt("dve", rtmp[:], R32[:], gb[:], ALU.mult)
                    tt("dve", R32[:], rtmp[:], bk[0:64, j, 0:256].re("p (h d) -> p h d", h=4), ALU.add)
                    cp("act", Rb[:], R32[:])

                    if not P2:
                        return

                    yield
                    tt("dve", s2[:, :, 1:144], Zu[:, :, 1:144], Zu[:, :, 0:143], ALU.add)
                    tt("dve", s4[:, :, 3:144], s2[:, :, 3:144], s2[:, :, 1:142], ALU.add)
                    tt("dve", s8[:, 7:144], s4[:, 1, 7:144], s4[:, 1, 3:140], ALU.add)
                    tt("dve", s16[64:128, 15:144], s8[64:128, 15:144], s8[64:128, 7:136], ALU.add)
                    sels = ((s2[0:64, 0, 16:144], 0, 0, 0.5), (s4[64:128, 0, 16:144], 64, 0, 0.25),
                            (s8[0:64, 16:144], 0, 1, 0.125), (s16[64:128, 16:144], 64, 1, 0.0625))
                    for sw_, p0, m_, iw in sels:
                        if ti == 0:
                            tt("dve", pm[p0:p0 + 64, m_, :], sw_, cnt0[p0:p0 + 64, m_, :], ALU.mult)
                            tt("dve", pd[p0:p0 + 64, m_, :], pm[p0:p0 + 64, m_, :], Zu[p0:p0 + 64, m_, 16:144], ALU.subtract)
                        else:
                            stt(pd[p0:p0 + 64, m_, :], sw_, iw, Zu[p0:p0 + 64, m_, 16:144], ALU.mult, ALU.subtract)
                    yield
                    jpl, bpl = Pg.bank()
                    for m_ in range(2):
                        mm(bpl[:, jpl, m_ * 128:(m_ + 1) * 128], pd[:, m_, :], plwb[:, m_, :])

                    yield
                    headnorm(Yrw[:], 64, 8, RW_EPS)
                    lnw = rowp[0:64, R_LNW:R_LNW + 256].re("p (h o d) -> p h o d", h=4, o=1).bc([64, 4, 2, 64])
                    lnb = rowp[0:64, R_LNB:R_LNB + 256].re("p (h o d) -> p h o d", h=4, o=1).bc([64, 4, 2, 64])
                    Y4 = Yrw[:].re("p (h c) d -> p h c d", c=2)
                    tt("dve", Y4, Y4, lnw, ALU.mult)
                    tt("dve", Y4, Y4, lnb, ALU.add)
                    tt("dve", Yrw[:], Yrw[:], BV_u[:], ALU.add)
                    tt("dve", mixa[:].re("p c (h d) -> p h c d", h=4), Y4, grw[:].re("p c (h d) -> p h c d", h=4), ALU.mult)
                    yield
                    j, bk = tmproj(512, 512)
                    act(g2[:], bk[:, j, :], AF.Silu)
                    tt("dve", g2[:, 0:256], g2[:, 0:256], rowp[:, R_PLS:R_PLS + 256], ALU.mult)
                    tt("dve", mixb[:, 0:256], bpl[:, jpl, 0:256], g2[:, 0:256], ALU.mult)
                    yield
                    j, bk = tmproj(1024, 512)
                    act(g3[:], bk[:, j, 0:256], AF.Silu)
                    act(osg[:], bk[:, j, 256:512], AF.Sigmoid)
                    tt("dve", Yml[:], Yml[:], osg[:].re("p (h d) -> p h d", h=4), ALU.mult)
                    headnorm(Yml[:], 128, 4, HN_EPS)
                    tt("dve", g2[:, 256:512], g2[:, 256:512], rowp[:, R_MLLN:R_MLLN + 256], ALU.mult)
                    tt("dve", mixb[:, 256:512], Yml[:].re("p h d -> p (h d)"), g2[:, 256:512], ALU.mult)
                    headnorm(Yrt[:], 128, 4, HN_EPS)
                    tt("dve", mixb[:, 512:768], Yrt[:].re("p h d -> p (h d)"), g3[:], ALU.mult)
                    yield
                    yield
                    j, bk = Pg.bank()
                    for c in range(2):
                        for k in range(2):
                            tr(bk.b16((slice(0, 128), j, slice(k * 128 + c * 64, k * 128 + (c + 1) * 64))), mixa[:, c, k * 128:(k + 1) * 128], identb[0:64, 0:64])
                    for k in range(2, 8):
                        tr(bk.b16((slice(0, 128), j, slice(k * 128, (k + 1) * 128))), mixb[:, (k - 2) * 128:(k - 1) * 128], identb[:])
                    cp("act", mixT[:], bk.b16((slice(0, 128), j, slice(0, 1024))).re("p (k t) -> p k t", k=8))
                    Pg.store(st_sm, t_mix[ti].re("p (k t) -> p k t", k=8), mixT[:])


            ALLB = list(range(8))

            def run_all(g):
                for _ in g:
                    pass
            if not (RW and PIPE):
                for ti in range(NT):
                    Pg.set_pool(ALLB)
                    run_all(tile_body(ti, "all"))
            else:
                PA, PB = [0, 1, 2, 3], [4, 5, 6, 7]
                Pg.set_pool(PA)
                run_all(tile_body(0, "A"))
                for ti in range(NT):
                    gens = [(PB, tile_body(ti, "B"))]
                    if ti + 1 < NT:
                        gens.append((PA, tile_body(ti + 1, "A")))
                    while gens:
                        for pg in list(gens):
                            Pg.set_pool(pg[0])
                            try:
                                next(pg[1])
                            except StopIteration:
                                gens.remove(pg)
                Pg.set_pool(ALLB)

            if not P2:
                Pg.store(None, d_osum[:, 0:512].rearrange("p (h n) -> p h n", h=4), S32[:])
                Pg.store(None, d_osum[:, 512:772].rearrange("p (h n) -> p h n", h=4), C32[:])
                Pg.store(None, d_osum[:, 772:776], NBT[:])
                Pg.store(None, d_osum[:, 776:1032].rearrange("p (h n) -> p h n", h=4), R32[:])
            S.barrier()
            S.emit()

        if P2:
            with contextlib.ExitStack() as stE:
                S.stack = stE
                S.barrier()
                wout = sb([128, 8, D], BF16, "wout")
                pleg = sb([128, 8, D], BF16, "pleg")
                plew = sb([128, 2, D], BF16, "plew")
                lng = sb([128, D], F32, "lng")
                lnb_ = sb([128, D], F32, "lnb")
                stg = [sb([128, D], F32, "stg0"), sb([128, D], F32, "stg1"), sb([128, D], F32, "stg2")]
                Pg.load(None, lng[:], d_row[R_LNG:R_LNG + D].partition_broadcast(128))
                Pg.load(None, lnb_[:], d_row[R_LNBB:R_LNBB + D].partition_broadcast(128))
                si = 0
                cast_engs = ["dve", "pool", "act"]
                for dst, dram, kc in ((wout, d_wout, 8), (pleg, d_pleg, 8), (plew, d_plew, 2)):
                    for k in range(kc):
                        sg = stg[si % 3]
                        Pg.load(None, sg[:], dram[k * 128:(k + 1) * 128, :])
                        cp(cast_engs[si % 3], dst[:, k, :], sg[:])
                        si += 1
                xte = [sb([128, D], F32, "xte0"), sb([128, D], F32, "xte1")]
                xTe = [sb([128, 8, 128], BF16, "xTe0"), sb([128, 8, 128], BF16, "xTe1")]
                mxe = [sb([128, 8, 128], BF16, "mxe0"), sb([128, 8, 128], BF16, "mxe1")]
                pte = [sb([128, 256], F32, "pte0"), sb([128, 256], F32, "pte1")]
                pT = sb([128, 2, 128], BF16, "pT")
                sgp = sb([128, D], F32, "sgp")
                acc = [sb([128, D], F32, "acc0"), sb([128, D], F32, "acc1")]
                bst = sb([128, 2, 6], F32, "bst")
                mv = sb([128, 2], F32, "mv")
                rstd = sb([128, 1], F32, "rstd")
                for ti in range(NT):
                    par = ti % 2
                    Pg.load(ld_x[par], xte[par][:], d_xh[HALO + ti * TT: HALO + (ti + 1) * TT, :])
                    Pg.load(ld_t[par], xTe[par][:], t_xTs[ti].re("p (k t) -> p k t", k=8), q="act")
                    Pg.load(ld_m[par], mxe[par][:], t_mix[ti].re("p (k t) -> p k t", k=8), q="pool")
                    Pg.load(ld_p[par], pte[par][:], d_p[ti * TT:(ti + 1) * TT, :])
                    j, bk = Pg.bank()
                    for k in range(2):
                        tr(bk[:, j, k * 128:(k + 1) * 128], pte[par][:, k * 128:(k + 1) * 128], ident[:])
                    cp("act", pT[:], bk[:, j, 0:256].re("p (k t) -> p k t", k=2))
                    ac = acc[par]
                    for half in range(2):
                        cs_ = slice(half * 512, (half + 1) * 512)
                        jg, bg = Pg.bank()
                        for k in range(8):
                            mm(bg[:, jg, :], xTe[par][:, k, :], pleg[:, k, cs_], start=(k == 0), stop=(k == 7))
                        act(sgp[:, cs_], bg[:, jg, :], AF.Sigmoid)
                        jw, bw = Pg.bank()
                        for k in range(2):
                            mm(bw[:, jw, :], pT[:, k, :], plew[:, k, cs_], start=(k == 0), stop=(k == 1))
                        tt("dve", sgp[:, cs_], sgp[:, cs_], bw[:, jw, :], ALU.mult)
                        jo, bo = Pg.bank()
                        for k in range(8):
                            mm(bo[:, jo, :], mxe[par][:, k, :], wout[:, k, cs_], start=(k == 0), stop=(k == 7))
                        stt(ac[:, cs_], xte[par][:, cs_], ALPHA, bo[:, jo, :], ALU.mult, ALU.add)
                        tt("pool", ac[:, cs_], ac[:, cs_], sgp[:, cs_], ALU.add)
                        Pg.bn_stats(bst[:, half, :], ac[:, cs_])
                    Pg.bn_aggr(mv[:], bst[:].re("p a b -> p (a b)"))
                    ts("dve", rstd[:], mv[:, 1:2], LN_EPS, None, ALU.add)
                    act(rstd[:], rstd[:], AF.Ln)
                    act(rstd[:], rstd[:], AF.Exp, scale=-0.5)
                    ts("dve", ac[:], ac[:], mv[:, 0:1], rstd[:, 0:1], ALU.subtract, ALU.mult)
                    tt("pool", ac[:], ac[:], lng[:], ALU.mult)
                    tt("dve", ac[:], ac[:], lnb_[:], ALU.add)
                    out_toks.append(Pg.store(st_o[par], d_out[ti * TT:(ti + 1) * TT, :], ac[:]))
                    if dr.get("tail") is not None and ti == NT - 1:
                        Pg.store(None, dr["tail"], ac[TT - HALO:TT, :])
                S.barrier()
                S.emit()
        S.stack = S.base


def _perm_rot(n_heads=4, hd=64):
    idx = []
    for h in range(n_heads):
        idx += [h * hd + 2 * i for i in range(hd // 2)] + [h * hd + 2 * i + 1 for i in range(hd // 2)]
    return np.array(idx)


def pack_layer(L, w_in, rw_mu, rw_w0, rw_w2, rw_a0, rw_a2, rw_kk, rw_ka, rw_rk, rw_ln_w, rw_ln_b, pl_w, pl_scale,
               ml_conv, ml_ib, ml_fb, ml_ln_w, ln_g, ln_b):
    W = np.asarray(w_in[L], np.float32)
    o = np.cumsum([0, 896, 256, 256, 256, 512, 256, 4, 4, 256, 256, 256, 256, 256, 256])
    rw_in, rw_g, pl_u, pl_g, ml_qk, ml_v, ml_i, ml_f, ml_o, ml_g, rt_q, rt_k, rt_v, rt_g = [W[:, o[i]:o[i + 1]] for i in range(14)]
    pr = _perm_rot()
    wfm = np.concatenate([rw_in[:, 0:256], rw_in[:, 256:512], rw_in[:, 768:896], pl_u, ml_qk[:, 0:256], ml_qk[:, 256:512]], axis=1)
    wtm = np.concatenate([rw_in[:, 512:768], rw_g, pl_g, ml_g, rt_g, ml_o, rt_q[:, pr], rt_k[:, pr], ml_v, rt_v, ml_i, ml_f], axis=1)
    assert wfm.shape[1] == FMC and wtm.shape[1] == TMC
    hT = lambda v: np.asarray(v, np.float32).reshape(4, 64).T
    mu = np.asarray(rw_mu[L], np.float32)
    rwp = np.concatenate([hT(mu[0:256]), hT(mu[256:512]), mu[768:832, None], mu[832:896, None],
                          hT(rw_w0[L]), hT(rw_a0[L]), hT(rw_kk[L]), hT(rw_ka[L]), hT(rw_rk[L])], axis=1)
    assert rwp.shape == (64, 30)
    rowp = np.zeros(4360, np.float32)
    rowp[0:256] = mu[512:768]
    rowp[256:512] = rw_ln_w[L]
    rowp[512:768] = rw_ln_b[L]
    rowp[768:1024] = pl_scale[L]
    rowp[1024:1280] = ml_ln_w[L]
    rowp[1280:2304] = ln_g[L]
    rowp[2304:3328] = ln_b[L]
    rowp[3328:3332] = ml_ib[L]
    rowp[3332:3336] = ml_fb[L]
    cw = np.asarray(ml_conv[L], np.float32)
    convw = cw.T.reshape(2, 4, 64, 4).transpose(2, 0, 1, 3)
    plw = np.zeros((128, 2, 128), np.float32)
    pw = np.asarray(pl_w[L], np.float32)
    for g in range(4):
        m, hh = g // 2, g % 2
        plw[hh * 64:(hh + 1) * 64, m, hh * 64:(hh + 1) * 64] = pw[g]
    return dict(wfm=np.ascontiguousarray(wfm), wtm=np.ascontiguousarray(wtm), rwp=np.ascontiguousarray(rwp),
                w2=np.asarray(rw_w2[L], np.float32), a2=np.asarray(rw_a2[L], np.float32), rowp=rowp,
                convw=np.ascontiguousarray(convw), plw=plw)


def const_tables(seg, NT):
    f = np.float32
    a = np.arange(128)
    ident = np.eye(128, dtype=f)
    mle = (a[:, None] <= a[None, :]).astype(f)
    mgt = (a[:, None] > a[None, :]).astype(f)
    a64 = np.arange(64)
    mG = np.concatenate([(a64[:, None] < a64[None, :]), (a64[:, None] <= a64[None, :])], axis=1).astype(f)
    gam = 1.0 - np.exp2(-5.0 - np.arange(4))
    lg = np.log(gam)
    rel = a[None, :] - a[:, None]
    dect = np.zeros((128, 4, 128), f)
    for h in range(4):
        dect[:, h, :] = np.where(rel >= 0, np.exp(lg[h] * np.maximum(rel, 0)), 0.0) * 0.125
    zx = np.zeros((128, 8), f)
    for h in range(4):
        zx[:, h] = np.exp(lg[h] * (127 - a)) * 0.125
        zx[:, 4 + h] = np.exp(lg[h] * (a + 1.0))
    gchunk = np.zeros((64, 4, 64), f)
    for h in range(4):
        gchunk[:, h, :] = np.exp(lg[h] * 128)
    pos = (seg * SEG + np.arange(NT * TT)).astype(np.float32)
    inv_freq = (np.float32(10000.0) ** (-np.arange(0, 64, 2, dtype=np.float32) / np.float32(64))).astype(np.float32)
    ang = pos[:, None] * inv_freq[None, :]
    c, s = np.cos(ang).astype(f), np.sin(ang).astype(f)
    cs2 = np.zeros((NT * TT, 2, 2, 32), f)
    cs2[:, 0, 0], cs2[:, 0, 1] = c, s
    cs2[:, 1, 0], cs2[:, 1, 1] = -s, c
    cnt0 = np.zeros((128, 2, 128), f)
    wins = (2, 4, 8, 16)
    for g in range(4):
        m, hh = g // 2, g % 2
        if seg == 0:
            cnt0[hh * 64:(hh + 1) * 64, m, :] = 1.0 / np.minimum(a + 1.0, float(wins[g]))[None, :]
        else:
            cnt0[hh * 64:(hh + 1) * 64, m, :] = 1.0 / wins[g]
    return dict(ident=ident, mask_le=mle, mask_gt=mgt, mask_g=mG, dect=dect, zetaxi=zx, gchunk=gchunk,
                cs2=np.ascontiguousarray(cs2.reshape(NT * TT, 128)), cnt0=cnt0)


def seg_consts(seg):
    f = np.float32
    gam = 1.0 - np.exp2(-5.0 - np.arange(4))
    gseg = np.zeros((64, 4, 64), f)
    for h in range(4):
        gseg[:, h, :] = np.exp(np.log(gam[h]) * SEG)
    msk = np.zeros((64, NSEG), f)
    msk[:, :seg] = 1.0
    return dict(gseg=gseg, seg_mask=msk)


SUMW = 1032
WNAMES = ("wfm", "wtm", "rwp", "w2", "a2", "rowp", "convw", "plw")
WSHAPES = dict(wfm=[D, FMC], wtm=[D, TMC], rwp=[64, 30], w2=[64, G], a2=[64, G], rowp=[4360], convw=[64, 2, 4, 4], plw=[128, 2, 128])
CSHAPES = lambda NT: dict(ident=[128, 128], mask_le=[128, 128], mask_gt=[128, 128], mask_g=[64, 128], dect=[128, 4, 128],
                          zetaxi=[128, 8], gchunk=[64, 4, 64], cs2=[NT * TT, 128], cnt0=[128, 2, 128])


def _declare(nc, NT, suffixes, NR, fused):
    def din(name, shape):
        return nc.dram_tensor(name, list(shape), F32, kind="ExternalInput").ap()
    dr = {}
    for k, shp in CSHAPES(NT).items():
        dr[k] = din(k, shp)
    dr["gseg"] = din("gseg", [64, 4, 64])
    dr["seg_mask"] = din("seg_mask", [64, NR])
    per = {}
    for sfx in suffixes:
        d = {}
        for k in WNAMES:
            d[k] = din(k + sfx, WSHAPES[k])
        d["w_out"] = din("w_out" + sfx, [D, D])
        d["ple_w"] = din("ple_w" + sfx, [256, D])
        d["ple_gate"] = din("ple_gate" + sfx, [D, D])
        d["p"] = din("p" + sfx, [NT * TT, 256])
        per[sfx] = d
    return dr, per


def build_fused(NT, NL=2):
    nc = bass.Bass("TRN2", target_bir_lowering=False)
    NR = 8
    groups = [list(range(NR))]
    dr, per = _declare(nc, NT, [f"_{L}" for L in range(NL)], NR, True)
    xh0 = nc.dram_tensor("xh", [HALO + NT * TT, D], F32, kind="ExternalInput").ap()
    pred_oh = nc.dram_tensor("pred_oh", [HALO, NR], F32, kind="ExternalInput").ap()
    xout = nc.dram_tensor("xout", [NT * TT, D], F32, kind="ExternalOutput").ap()
    x1 = nc.dram_tensor("x1_scratch", [HALO + NT * TT, D], F32, kind="Internal").ap()
    dr["t_mix"] = T(nc.dram_tensor("mix_scratch", [NT, 128, 1024], BF16, kind="Internal").ap(), "mix_scratch")
    dr["t_xTs"] = T(nc.dram_tensor("xT_scratch", [NT, 128, 1024], BF16, kind="Internal").ap(), "xT_scratch")
    dr["t_rp"] = T(nc.dram_tensor("h_rp", [NT, 64, 1024], BF16, kind="Internal").ap(), "h_rp")
    dr["t_yh"] = T(nc.dram_tensor("h_yh", [NT, 64, 512], F32, kind="Internal").ap(), "h_yh")
    dr["t_bv"] = T(nc.dram_tensor("h_bv", [NT, 64, 512], F32, kind="Internal").ap(), "h_bv")
    dr["t_ps"] = T(nc.dram_tensor("h_ps", [NT, 64, 512], F32, kind="Internal").ap(), "h_ps")
    dr["t_pl"] = T(nc.dram_tensor("h_pl", [NT, 64, 8], F32, kind="Internal").ap(), "h_pl")
    sum_src = [nc.dram_tensor(f"sum_src{L}", [64, SUMW], F32) for L in range(NL)]
    sum_all = [nc.dram_tensor(f"sum_all{L}", [NR * 64, SUMW], F32) for L in range(NL)]
    tail_src = nc.dram_tensor("tail_src", [HALO, D], F32)
    tail_all = nc.dram_tensor("tail_all", [NR * HALO, D], F32)
    with contextlib.ExitStack() as st:
        Pg = Prog(nc, st)
        S = Pg.S
        ident = S.sbuf([128, 128], F32, "ident")
        identb = S.sbuf([128, 128], BF16, "identb")
        Pg.load(None, ident[:], dr["ident"])
        Pg.cp("dve", identb[:], ident[:])
        for L in range(NL):
            d = dict(dr)
            d.update(per[f"_{L}"])
            d["xh"] = xh0 if L == 0 else x1
            d["o_sum"] = sum_src[L].ap()
            emit_pass(Pg, ident, identb, "P1", NT, d, NR)
            S.new_phase()
            tsrc, tdst = T(None, "ccsrc"), T(None, "ccdst")
            S.collective_allgather(tsrc, tdst, sum_src[L].ap().opt(), sum_all[L].ap().opt(), groups)
            S.barrier()
            S.emit()
            S.new_phase()
            d["sum_all"] = sum_all[L].ap()
            if L == 0 and NL > 1:
                d["xout"] = x1[HALO:, :]
                d["tail"] = tail_src.ap()
            else:
                d["xout"] = xout
                d["tail"] = None
            emit_pass(Pg, ident, identb, "P2", NT, d, NR)
            S.new_phase()
            if L == 0 and NL > 1:
                tsrc, tdst = T(None, "ccsrc"), T(None, "ccdst")
                S.collective_allgather(tsrc, tdst, tail_src.ap().opt(), tail_all.ap().opt(), groups)
                with contextlib.ExitStack() as stH:
                    S.stack = stH
                    S.barrier()
                    tl = S.sbuf([HALO, NR, D], F32, "tl")
                    oh = S.sbuf([HALO, NR], F32, "oh")
                    hl = S.sbuf([HALO, D], F32, "hl")
                    Pg.load(None, tl[:], tail_all.ap().rearrange("(j p) d -> p j d", p=HALO))
                    Pg.load(None, oh[:], pred_oh)
                    Pg.ts("dve", hl[:], tl[:, 0, :], oh[:, 0:1], None, ALU.mult)
                    for j in range(1, NR):
                        Pg.stt(hl[:], tl[:, j, :], oh[:, j:j + 1], hl[:], ALU.mult, ALU.add)
                    Pg.store(None, x1[0:HALO, :], hl[:])
                    S.barrier()
                    S.emit()
                S.stack = S.base
                S.new_phase()
    return nc


_PROGS = {}


def _prog(mode, NT):
    key = (mode, NT)
    if key not in _PROGS:
        _PROGS[key] = build_fused(NT, 2 if mode == "F" else 1)
    return _PROGS[key]


def _layer_inputs(L, args):
    (w_in, rw_mu, rw_w0, rw_w2, rw_a0, rw_a2, rw_kk, rw_ka, rw_rk, rw_ln_w, rw_ln_b, pl_w, pl_scale,
     ml_conv, ml_ib, ml_fb, ml_ln_w, w_out, ple_w, ple_gate, ln_g, ln_b) = args
    pk = pack_layer(L, w_in, rw_mu, rw_w0, rw_w2, rw_a0, rw_a2, rw_kk, rw_ka, rw_rk, rw_ln_w, rw_ln_b, pl_w, pl_scale,
                    ml_conv, ml_ib, ml_fb, ml_ln_w, ln_g, ln_b)
    pk["w_out"] = np.ascontiguousarray(np.asarray(w_out[L], np.float32))
    pk["ple_w"] = np.ascontiguousarray(np.asarray(ple_w[L], np.float32))
    pk["ple_gate"] = np.ascontiguousarray(np.asarray(ple_gate[L], np.float32))
    return pk


def _xh(cur, c):
    b, sg = c // NSEG, c % NSEG
    xh = np.zeros((HALO + SEG, D), np.float32)
    xh[HALO:] = cur[b, sg * SEG:(sg + 1) * SEG]
    if sg > 0:
        xh[:HALO] = cur[b, sg * SEG - HALO: sg * SEG]
    return xh


def _core_inputs(c, NT, pks, p, xcur, layers):
    b, sg = c // NSEG, c % NSEG
    ncores = 2 * NSEG
    im = {}
    im.update(const_tables(sg, NT))
    im["gseg"] = seg_consts(sg)["gseg"]
    msk = np.zeros((64, ncores), np.float32)
    msk[:, b * NSEG: c] = 1.0
    im["seg_mask"] = msk
    oh = np.zeros((HALO, ncores), np.float32)
    if sg > 0:
        oh[:, c - 1] = 1.0
    im["pred_oh"] = oh
    for i, L in enumerate(layers):
        for k, v in pks[L].items():
            im[f"{k}_{i}"] = v
        im[f"p_{i}"] = np.ascontiguousarray(p[L, b, sg * SEG:(sg + 1) * SEG])
    im["xh"] = _xh(xcur, c)
    return im


def kernel_two_launch(x, p, *args):
    x = np.asarray(x, np.float32)
    p = np.asarray(p, np.float32)
    NT = SEG // TT
    ncores = 2 * NSEG
    pks = [_layer_inputs(L, args) for L in range(2)]
    cur = x
    for L in range(2):
        ims = [_core_inputs(c, NT, pks, p, cur, [L]) for c in range(ncores)]
        res = run_bass_kernel_spmd(_prog("L", NT), ims, core_ids=list(range(ncores))).results
        nxt = np.empty_like(cur)
        for c in range(ncores):
            nxt[c // NSEG, (c % NSEG) * SEG:(c % NSEG + 1) * SEG] = res[c]["xout"]
        cur = nxt
    return cur


def kernel(x, p, w_in, rw_mu, rw_w0, rw_w2, rw_a0, rw_a2, rw_kk, rw_ka, rw_rk, rw_ln_w, rw_ln_b, pl_w, pl_scale,
           ml_conv, ml_ib, ml_fb, ml_ln_w, w_out, ple_w, ple_gate, ln_g, ln_b):
    args = (w_in, rw_mu, rw_w0, rw_w2, rw_a0, rw_a2, rw_kk, rw_ka, rw_rk, rw_ln_w, rw_ln_b, pl_w, pl_scale,
            ml_conv, ml_ib, ml_fb, ml_ln_w, w_out, ple_w, ple_gate, ln_g, ln_b)
    x = np.asarray(x, np.float32)
    p = np.asarray(p, np.float32)
    NT = SEG // TT
    ncores = 2 * NSEG
    pks = [_layer_inputs(L, args) for L in range(2)]
    ims = [_core_inputs(c, NT, pks, p, x, [0, 1]) for c in range(ncores)]
    res = run_bass_kernel_spmd(_prog("F", NT), ims, core_ids=list(range(ncores))).results
    out = np.empty_like(x)
    for c in range(ncores):
        out[c // NSEG, (c % NSEG) * SEG:(c % NSEG + 1) * SEG] = res[c]["xout"]
    return out
```

```python
import contextlib
import math
import numpy as np
import concourse.bass as bass
import concourse.mybir as mybir
from concourse.bass_utils import run_bass_kernel_spmd

F32 = mybir.dt.float32
BF16 = mybir.dt.bfloat16
AF = mybir.ActivationFunctionType
ALU = mybir.AluOpType
AX = mybir.AxisListType

D = 1024
G = 256
NSEG = 4
SEG = 4096
HALO = 16
TT = 128
FMC = 1408
TMC = 2568
C0 = math.exp(-0.5)
ALPHA = (2.0 * 2) ** 0.25
RW_EPS = 64e-5
HN_EPS = 1e-6
LN_EPS = 1e-5
SAME_ENGINE_SYNC = ("act", "dve", "pool")
PIPE = True
RDT = BF16


class Ref:
    __slots__ = ("ts", "ap")

    def __init__(self, ts, ap):
        self.ts = ts
        self.ap = ap

    def re(self, pat, **kw):
        return Ref(self.ts, self.ap.rearrange(pat, **kw))

    def bc(self, shape):
        return Ref(self.ts, self.ap.broadcast_to(list(shape)))

    def __getitem__(self, idx):
        return Ref(self.ts, self.ap[idx])


class T:
    __slots__ = ("h", "hb", "last_w", "readers", "name")

    def __init__(self, h, name="", hb=None):
        self.h = h
        self.hb = hb
        self.last_w = None
        self.readers = []
        self.name = name

    def __getitem__(self, idx):
        return Ref((self,), self.h[idx])

    def b16(self, idx):
        return Ref((self,), self.hb[idx])

    def view(self, ap):
        return View((self,), ap)


class View:
    __slots__ = ("ts", "ap")

    def __init__(self, ts, ap):
        self.ts = ts
        self.ap = ap

    def __getitem__(self, idx):
        return Ref(self.ts, self.ap[idx])


class Sched:
    ENG = ("pe", "act", "dve", "pool", "sp")

    def __init__(self, nc, stack):
        self.nc = nc
        self.base = stack
        self.stack = stack
        self.q = {e: [] for e in self.ENG}
        self.cnt = {}
        self.sem = {}
        self.waited = {e: {} for e in self.ENG}
        self.phase = 0
        self.ekey = {}
        for e in ("pe", "act", "dve", "pool"):
            k = f"{e}#0"
            self.ekey[e] = k
            self.sem[k] = stack.enter_context(nc.semaphore("sem_" + e + "_0"))
            self.cnt[k] = 0
        self.ndma_sem = 0
        self.nt = 0
        self.ninst = 0

    def sbuf(self, shape, dt, name=None):
        self.nt += 1
        name = f"{name or 't'}_{self.nt}"
        h = self.stack.enter_context(self.nc.sbuf_tensor(name, list(shape), dt))
        return T(h, name)

    def slot(self, name=None):
        self.nt += 1
        name = f"{name or 's'}_{self.nt}"
        h = self.stack.enter_context(self.nc.sbuf_tensor(name, [128, 512], F32))
        return T(h, name, hb=h.bitcast(BF16))

    def dma_sem(self):
        self.ndma_sem += 1
        key = f"dma{self.ndma_sem}"
        self.sem[key] = self.base.enter_context(self.nc.semaphore("sem_" + key))
        self.cnt[key] = 0
        return key

    def _need(self, eng, tok, waits):
        if tok is None:
            return
        key, val = tok
        if key.split("#")[0] == eng and eng not in SAME_ENGINE_SYNC:
            return
        if self.waited[eng].get(key, 0) >= val:
            return
        waits[key] = max(waits.get(key, 0), val)

    def _deps(self, eng, reads, writes):
        waits = {}
        for t in reads:
            self._need(eng, t.last_w, waits)
        for t in writes:
            self._need(eng, t.last_w, waits)
            for r in t.readers:
                self._need(eng, r, waits)
        for key, val in waits.items():
            self.waited[eng][key] = val
        return list(waits.items())

    def _record(self, tok, reads, writes):
        for t in writes:
            t.last_w = tok
            t.readers = []
        for t in reads:
            if t in writes:
                continue
            t.readers = [r for r in t.readers if r[0] != tok[0]] + [tok]

    def op(self, eng, fn, reads=(), writes=()):
        reads = list(dict.fromkeys(reads))
        writes = list(dict.fromkeys(writes))
        waits = self._deps(eng, reads, writes)
        ek = self.ekey[eng]
        self.cnt[ek] += 1
        self.ninst += 1
        tok = (ek, self.cnt[ek])
        sem = self.sem[ek]
        sems = self.sem

        def run(e, waits=waits, fn=fn, sem=sem):
            for key, val in waits:
                e.wait_ge(sems[key], val)
            fn(e).then_inc(sem, 1)
        self.q[eng].append(run)
        self._record(tok, reads, writes)
        return tok

    def dma(self, qeng, semkey, out_ap, in_ap, reads=(), writes=()):
        reads = list(dict.fromkeys(reads))
        writes = list(dict.fromkeys(writes))
        waits = self._deps(qeng, reads, writes)
        if semkey is None:
            if not hasattr(self, "pool_keys"):
                self.pool_keys = [self.dma_sem() for _ in range(16)]
                self.pool_i = 0
            semkey = self.pool_keys[self.pool_i % len(self.pool_keys)]
            self.pool_i += 1
            prev = self.cnt[semkey]
            if prev > 0 and self.waited[qeng].get(semkey, 0) < prev:
                waits = [w for w in waits if w[0] != semkey] + [(semkey, prev)]
                self.waited[qeng][semkey] = prev
        self.cnt[semkey] += 16
        tok = (semkey, self.cnt[semkey])
        sem = self.sem[semkey]
        sems = self.sem

        def run(e, waits=waits, sem=sem, out_ap=out_ap, in_ap=in_ap):
            for key, val in waits:
                e.wait_ge(sems[key], val)
            e.dma_start(out=out_ap, in_=in_ap).then_inc(sem, 16)
        self.q[qeng].append(run)
        self._record(tok, reads, writes)
        return tok

    def new_phase(self):
        self.barrier()
        self.phase += 1
        for e in ("pe", "act", "dve", "pool"):
            k = f"{e}#{self.phase}"
            self.ekey[e] = k
            self.sem[k] = self.base.enter_context(self.nc.semaphore(f"sem_{e}_{self.phase}"))
            self.cnt[k] = 0

    def collective_allgather(self, src_t, dst_t, src_ap, dst_ap, groups):
        key = self.dma_sem()
        waits = self._deps("pool", [src_t], [dst_t])
        self.cnt[key] += 1
        tok = (key, self.cnt[key])
        sems = self.sem

        def run(e, waits=waits, key=key):
            for k_, v_ in waits:
                e.wait_ge(sems[k_], v_)
            e.collective_compute("AllGather", mybir.AluOpType.bypass, replica_groups=groups,
                                 ins=[src_ap], outs=[dst_ap]).then_inc(sems[key])
        self.q["pool"].append(run)
        self._record(tok, [src_t], [dst_t])
        return tok

    def barrier(self):
        snap = dict(self.cnt)
        sems = self.sem
        for eng in self.ENG:
            waits = []
            for key, val in snap.items():
                if val > 0 and self.waited[eng].get(key, 0) < val:
                    waits.append((key, val))
                    self.waited[eng][key] = val

            def run(e, waits=waits):
                for key, val in waits:
                    e.wait_ge(sems[key], val)
            self.q[eng].append(run)

    def emit(self):
        nc = self.nc
        q = self.q
        self.q = {e: [] for e in self.ENG}
        with nc.Block() as block:
            @block.tensor
            def _(e):
                for f in q["pe"]:
                    f(e)

            @block.scalar
            def _(e):
                for f in q["act"]:
                    f(e)

            @block.vector
            def _(e):
                for f in q["dve"]:
                    f(e)

            @block.gpsimd
            def _(e):
                for f in q["pool"]:
                    f(e)

            @block.sync
            def _(e):
                for f in q["sp"]:
                    f(e)


def _ts(*refs):
    out = []
    for r in refs:
        if isinstance(r, Ref):
            out.extend(r.ts)
    return out


class Prog:
    def __init__(self, nc, stack):
        self.nc = nc
        self.S = Sched(nc, stack)
        ph = stack.enter_context(nc.psum_tensor("psum_all", [128, 8, 512], F32))
        phb = ph.bitcast(BF16)
        self.ph = ph
        self.banks = [T(ph, f"bank{j}", hb=phb) for j in range(8)]
        self.pool = list(range(8))
        self.pool_pos = {}

    def set_pool(self, pool):
        self.pool = pool

    def stream_sems(self):
        if not hasattr(self, "_ss"):
            S = self.S
            two = lambda: [S.dma_sem(), S.dma_sem()]
            self._ss = (two(), two(), two(), two(), two(), two(), S.dma_sem(), S.dma_sem())
        return self._ss

    def handoff_sems(self):
        if not hasattr(self, "_hs"):
            S = self.S
            self._hs = [[S.dma_sem(), S.dma_sem()] for _ in range(5)]
        return self._hs

    def bank(self):
        key = tuple(self.pool)
        i = self.pool_pos.get(key, 0)
        self.pool_pos[key] = i + 1
        j = self.pool[i % len(self.pool)]
        return j, self.banks[j]

    def bank2(self):
        key = tuple(self.pool)
        i = self.pool_pos.get(key, 0)
        if i % 2:
            i += 1
        self.pool_pos[key] = i + 2
        j = self.pool[i % len(self.pool)]
        return j, View((self.banks[j], self.banks[j + 1]), self.ph[:, j:j + 2, :])

    def mm(self, out, lhsT, rhs, start=True, stop=True):
        self.S.op("pe", lambda e: e.matmul(out.ap, lhsT=lhsT.ap, rhs=rhs.ap, start=start, stop=stop),
                  reads=_ts(lhsT, rhs), writes=_ts(out))

    def tr(self, out, in_, ident):
        self.S.op("pe", lambda e: e.transpose(out=out.ap, in_=in_.ap, identity=ident.ap),
                  reads=_ts(in_, ident), writes=_ts(out))

    def act(self, out, in_, func, bias=None, scale=None):
        kw = {}
        if bias is not None:
            kw["bias"] = bias.ap if isinstance(bias, Ref) else bias
        if scale is not None:
            kw["scale"] = scale.ap if isinstance(scale, Ref) else scale
        self.S.op("act", lambda e: e.activation(out=out.ap, in_=in_.ap, func=func, **kw),
                  reads=_ts(in_, bias, scale), writes=_ts(out))

    def cp(self, eng, out, in_):
        if eng == "act":
            self.S.op("act", lambda e: e.copy(out=out.ap, in_=in_.ap), reads=_ts(in_), writes=_ts(out))
        else:
            self.S.op(eng, lambda e: e.tensor_copy(out=out.ap, in_=in_.ap), reads=_ts(in_), writes=_ts(out))

    def tt(self, eng, out, a, b, op):
        self.S.op(eng, lambda e: e.tensor_tensor(out=out.ap, in0=a.ap, in1=b.ap, op=op),
                  reads=_ts(a, b), writes=_ts(out))

    def ts(self, eng, out, a, s1, s2, op0, op1=None):
        v1 = s1.ap if isinstance(s1, Ref) else s1
        v2 = s2.ap if isinstance(s2, Ref) else s2
        if op1 is None:
            self.S.op(eng, lambda e: e.tensor_scalar(out=out.ap, in0=a.ap, scalar1=v1, scalar2=None, op0=op0),
                      reads=_ts(a, s1), writes=_ts(out))
        else:
            self.S.op(eng, lambda e: e.tensor_scalar(out=out.ap, in0=a.ap, scalar1=v1, scalar2=v2, op0=op0, op1=op1),
                      reads=_ts(a, s1, s2), writes=_ts(out))

    def stt(self, out, a, s, b, op0, op1):
        v = s.ap if isinstance(s, Ref) else s
        self.S.op("dve", lambda e: e.scalar_tensor_tensor(out=out.ap, in0=a.ap, scalar=v, in1=b.ap, op0=op0, op1=op1),
                  reads=_ts(a, s, b), writes=_ts(out))

    def red(self, out, in_, op=ALU.add):
        self.S.op("dve", lambda e: e.tensor_reduce(out=out.ap, in_=in_.ap, axis=AX.X, op=op),
                  reads=_ts(in_), writes=_ts(out))

    def scan_add(self, out, ones, data):
        self.S.op("dve", lambda e: e.tensor_tensor_scan(out=out.ap, data0=ones.ap, data1=data.ap, initial=0.0,
                                                        op0=ALU.mult, op1=ALU.add),
                  reads=_ts(ones, data), writes=_ts(out))

    def bn_stats(self, out, in_):
        self.S.op("dve", lambda e: e.bn_stats(out=out.ap, in_=in_.ap), reads=_ts(in_), writes=_ts(out))

    def bn_aggr(self, out, in_):
        self.S.op("dve", lambda e: e.bn_aggr(out=out.ap, in_=in_.ap), reads=_ts(in_), writes=_ts(out))

    def recip(self, out, in_):
        self.S.op("dve", lambda e: e.reciprocal(out=out.ap, in_=in_.ap), reads=_ts(in_), writes=_ts(out))

    def memset(self, eng, out, val):
        self.S.op(eng, lambda e: e.memset(out.ap, val), reads=[], writes=_ts(out))

    def load(self, semkey, out, src, q="sp"):
        if isinstance(src, Ref):
            return self.S.dma(q, semkey, out.ap, src.ap, reads=_ts(src), writes=_ts(out))
        return self.S.dma(q, semkey, out.ap, src, writes=_ts(out))

    def store(self, semkey, dst, in_, q="sp"):
        if isinstance(dst, Ref):
            return self.S.dma(q, semkey, dst.ap, in_.ap, reads=_ts(in_), writes=_ts(dst))
        return self.S.dma(q, semkey, dst, in_.ap, reads=_ts(in_))


def emit_pass(Pg, ident, identb, mode, NT, dr, NR):
    P2 = mode == "P2"
    RW = not P2
    nc = Pg.nc
    t_rp, t_yh, t_bv, t_ps, t_pl = dr["t_rp"], dr["t_yh"], dr["t_bv"], dr["t_ps"], dr["t_pl"]
    NTOK = NT * TT
    d_xh = dr["xh"]
    d_wfm, d_wtm, d_rwp, d_w2, d_a2, d_row = dr["wfm"], dr["wtm"], dr["rwp"], dr["w2"], dr["a2"], dr["rowp"]
    d_conv, d_plw = dr["convw"], dr["plw"]
    d_mle, d_mgt, d_mG, d_dec, d_zx, d_gb = dr["mask_le"], dr["mask_gt"], dr["mask_g"], dr["dect"], dr["zetaxi"], dr["gchunk"]
    d_cs, d_cnt0 = dr["cs2"], dr["cnt0"]
    if P2:
        d_sall, d_msk, d_gseg = dr["sum_all"], dr["seg_mask"], dr["gseg"]
        d_p, d_wout, d_plew, d_pleg, d_out = dr["p"], dr["w_out"], dr["ple_w"], dr["ple_gate"], dr["xout"]
        t_mix, t_xTs = dr["t_mix"], dr["t_xTs"]
    else:
        d_osum = dr["o_sum"]
    R_MUV, R_LNW, R_LNB, R_PLS, R_MLLN, R_LNG, R_LNBB, R_IB, R_OMUV = 0, 256, 512, 768, 1024, 1280, 2304, 3328, 3336
    out_toks = []
    S = Pg.S
    mm, tr, act, cp, tt, ts, stt, red = Pg.mm, Pg.tr, Pg.act, Pg.cp, Pg.tt, Pg.ts, Pg.stt, Pg.red
    sb = S.sbuf
    ld_x, ld_p, ld_m, ld_t, ld_cs, st_o, st_sx, st_sm = Pg.stream_sems()
    ld_h = Pg.handoff_sems()
    if True:
        with contextlib.ExitStack() as stM:
            S.stack = stM
            mle = sb([128, 128], F32, "mle")
            mgt = sb([128, 128], F32, "mgt")
            mG = sb([64, 128], F32, "mG")
            dect = sb([128, 4, 128], F32, "dect")
            zx = sb([128, 8], F32, "zx")
            gb = sb([64, 4, 64], F32, "gb")
            cnt0 = sb([128, 2, 128], F32, "cnt0")
            rwp = sb([64, 48], F32, "rwp")
            w2 = sb([64, G], F32, "w2")
            a2 = sb([64, G], F32, "a2")
            rowp = sb([128, 1280 + 8 + 256 + 8], F32, "rowp")
            convw = sb([64, 2, 4, 4], F32, "convw")
            plwb = sb([128, 2, 128], BF16, "plwb")
            ones = sb([128, 128], F32, "ones")
            sm = sb([128, 96], F32, "sm")
            sm2 = sb([64, 16], F32, "sm2")
            RM_IB, RM_OMUV = 1280, 1288
            Pg.load(None, mle[:], d_mle)
            Pg.load(None, mgt[:], d_mgt)
            Pg.load(None, mG[:], d_mG)
            Pg.load(None, dect[:], d_dec)
            Pg.load(None, zx[:], d_zx)
            Pg.load(None, gb[:], d_gb)
            Pg.load(None, cnt0[:], d_cnt0)
            Pg.load(None, rwp[:, 0:30], d_rwp)
            Pg.load(None, w2[:], d_w2)
            Pg.load(None, a2[:], d_a2)
            Pg.load(None, rowp[:, 0:1280], d_row[0:1280].partition_broadcast(128))
            Pg.load(None, rowp[:, RM_IB:RM_IB + 8], d_row[R_IB:R_IB + 8].partition_broadcast(128))
            Pg.load(None, convw[:], d_conv)
            Pg.memset("dve", ones[:], 1.0)
            ts("dve", rwp[:, 30:40], rwp[:, 0:10], -1.0, 1.0, ALU.mult, ALU.add)
            ts("dve", rwp[:, 40:44], rwp[:, 22:26], -1.0, 1.0, ALU.mult, ALU.add)
            ts("dve", rowp[:, RM_OMUV:RM_OMUV + 256], rowp[:, R_MUV:R_MUV + 256], -1.0, 1.0, ALU.mult, ALU.add)
            ts("dve", rowp[:, RM_IB:RM_IB + 4], rowp[:, RM_IB:RM_IB + 4], math.log(0.125), None, ALU.add)

            NSL = 45 if RW else 38
            A = [S.slot(f"A{i}") for i in range(NSL)]

            def f64(sl, h=4):
                return sl.view(sl.h[0:64, :].rearrange("p (h t) -> p h t", h=h))

            def b64(sl, lo, n, c=8):
                hh = sl.hb if RDT == BF16 else sl.h
                return sl.view(hh[0:64, lo:lo + n].rearrange("p (c d) -> p c d", c=c))
            zs_r = f64(A[0]); rk = zs_r
            zs_k = f64(A[1]); kmod = zs_k
            sig = f64(A[2]); esc = sig; kq = sig; kk = sig
            aa = f64(A[3]); bbv = aa
            gsc = f64(A[4]); cumI = gsc; pinv = gsc
            tA = f64(A[5]); cE = tA; pprev = tA
            tB = f64(A[6]); dl = tB; pLp = tB
            pp = f64(A[7])
            tC = f64(A[8])
            zs_wa = A[9].view(A[9].h[0:64, 0:256].rearrange("p (h t) -> p h t", h=2))
            th = A[9].view(A[9].h[0:64, 256:384])
            Vc = A[10].view(A[10].h[0:64, :].rearrange("p (c n) -> p c n", c=2))
            tV = A[11].view(A[11].h[0:64, :].rearrange("p (c n) -> p c n", c=2)); grw = tV
            if RDT == BF16:
                AR = A[12].view(A[12].hb[0:64, :].rearrange("p (c w l) -> p c w l", c=8, w=2))
                Bt, Kt = b64(A[13], 0, 512), b64(A[13], 512, 512)
                Bpf, Kpf = b64(A[14], 0, 512), b64(A[14], 512, 512)
                Bptm, Kptm = b64(A[15], 0, 512), b64(A[15], 512, 512)
                Gb = b64(A[16], 0, 1024)
                Gk = b64(A[17], 0, 1024)
                Pn = [b64(A[18], 0, 512), b64(A[18], 512, 512)]
                Ptb = [b64(A[19], 0, 512), b64(A[19], 512, 512)]
                X = [b64(A[20], 0, 1024), b64(A[21], 0, 1024)]
                Phi, Rhat = b64(A[22], 0, 512), b64(A[22], 512, 512)
                Vcr = A[24].view(A[24].hb[0:64, 0:512].rearrange("p (c n) -> p c n", c=2))
            else:
                raise NotImplementedError
            Yrw = A[23].view(A[23].h[0:64, :].rearrange("p (c d) -> p c d", c=8))
            mixa = A[24].view(A[24].hb[0:64, 512:1024].rearrange("p (c n) -> p c n", c=2))
            BVt = A[28].view(A[28].h[0:64, :].rearrange("p (c d) -> p c d", c=8))
            psi = A[29].view(A[29].h[0:64, :].rearrange("p (c n) -> p c n", c=2))
            pLt = sm2.view(sm2.h[0:64, 8:16])
            RPl = [A[0].view(A[0].hb[0:64, :]), A[7].view(A[7].hb[0:64, :])]
            YHl = [A[9].view(A[9].h[0:64, :]), A[10].view(A[10].h[0:64, :])]
            BVl = [A[12].view(A[12].h[0:64, :]), A[15].view(A[15].h[0:64, :])]
            PSl = [A[36].view(A[36].h[0:64, :]), A[37].view(A[37].h[0:64, :])]
            PLl = [sm.view(sm.h[0:64, 80:88]), sm.view(sm.h[0:64, 88:96])]
            IFB = [(AR, Bt, Kt, Bpf, Kpf, Vc, Vcr, pp, zs_r)]
            if RW:
                IFB.append((A[38].view(A[38].hb[0:64, :].rearrange("p (c w l) -> p c w l", c=8, w=2)),
                            b64(A[39], 0, 512), b64(A[39], 512, 512), b64(A[40], 0, 512), b64(A[40], 512, 512),
                            A[41].view(A[41].h[0:64, :].rearrange("p (c n) -> p c n", c=2)),
                            A[42].view(A[42].hb[0:64, 0:512].rearrange("p (c n) -> p c n", c=2)),
                            f64(A[43]), f64(A[44])))
            else:
                IFB.append(IFB[0])
            cq, ck, ctmp = tA, tB, tC
            qT = A[25].view(A[25].hb[0:64, 0:512].rearrange("p (h t) -> p h t", h=4))
            kT = A[25].view(A[25].hb[0:64, 512:1024].rearrange("p (h t) -> p h t", h=4))
            f128 = lambda sl: sl.view(sl.h[:, :].rearrange("p (h t) -> p h t", h=4))
            Ah, Eh, Em = f128(A[2]), f128(A[3]), f128(A[4])
            PTm = A[26].view(A[26].hb[:, 0:512].rearrange("p (h t) -> p h t", h=4))
            ktw = A[26].view(A[26].hb[:, 512:768].rearrange("p (h d) -> p h d", h=4))
            Vaug = A[27].view(A[27].hb[:, 0:260].rearrange("p (h d) -> p h d", h=4))
            vrt = A[27].view(A[27].hb[:, 260:516].rearrange("p (h d) -> p h d", h=4))
            vz = A[27].view(A[27].hb[:, 516:772].rearrange("p (h d) -> p h d", h=4))
            numt = A[28].view(A[28].h[:, 0:260].rearrange("p (h d) -> p h d", h=4))
            numi = A[29].view(A[29].h[:, 0:260].rearrange("p (h d) -> p h d", h=4))
            osg = A[30].view(A[30].h[:, 0:256])
            Yml = A[30].view(A[30].h[:, 256:512].rearrange("p (h d) -> p h d", h=4))
            ctmp2 = A[1].view(A[1].h[0:64, 0:260].rearrange("p (h d) -> p h d", h=4))
            rtmp = A[5].view(A[5].h[0:64, 0:256].rearrange("p (h d) -> p h d", h=4))
            rAs, rBs = (A[33], A[30]) if RW else (A[13], A[14])
            rA = rAs.view(rAs.h[:, :].rearrange("p (q w i) -> p q w i", q=8, w=2))
            rB = rBs.view(rBs.h[:, :].rearrange("p (q w i) -> p q w i", q=8, w=2))
            qkr = A[31].view(A[31].hb[:, 0:512].rearrange("p (q d) -> p q d", q=8))
            PTr = A[31].view(A[31].hb[:, 512:1024].rearrange("p (h t) -> p h t", h=4))
            qkT = A[32].view(A[32].hb[0:64, :].rearrange("p (q t) -> p q t", q=8))
            yint = A[33].view(A[33].h[:, 0:256].rearrange("p (h d) -> p h d", h=4))
            Yrt = A[33].view(A[33].h[:, 256:512].rearrange("p (h d) -> p h d", h=4))
            s2 = A[16].view(A[16].h[:, 0:288].rearrange("p (m t) -> p m t", m=2))
            s4 = A[17].view(A[17].h[:, 0:288].rearrange("p (m t) -> p m t", m=2))
            s8 = A[18].view(A[18].h[:, 0:144])
            s16 = A[18].view(A[18].h[:, 144:288])
            pm = A[19].view(A[19].h[:, 0:256].rearrange("p (m t) -> p m t", m=2))
            pd = A[19].view(A[19].hb[:, 512:768].rearrange("p (m t) -> p m t", m=2))
            g2 = A[20].view(A[20].h[:, :])
            g3 = A[21].view(A[21].h[:, 0:256])
            hbig = A[22].view(A[22].h[:, :].rearrange("p (g d) -> p g d", g=8))
            mixb = A[34].view(A[34].hb[:, 0:768])
            mixT = A[35].view(A[35].hb[:, :].rearrange("p (k t) -> p k t", k=8))
            smv = lambda lo, n, P_=128: sm.view(sm.h[0:P_, lo:lo + n])
            hsum, hsq, hm, hv = smv(0, 8), smv(8, 8), smv(16, 8), smv(24, 8)
            gat, nlf, ibs, ebl = smv(32, 8), smv(40, 4), smv(44, 4), smv(48, 4)
            etot, dden, bon = smv(52, 4, 64), smv(56, 4), sm2.view(sm2.h[0:64, 0:8])
            NBT = smv(68, 4, 64)

            wfm = sb([128, 8, FMC], BF16, "wfm")
            wtm = sb([128, 8, TMC], BF16, "wtm")
            cast_engs = ["dve", "pool", "act"]
            si = 0
            for dst, dram, ncols in ((wfm, d_wfm, FMC), (wtm, d_wtm, TMC)):
                for k in range(8):
                    for c0 in range(0, ncols, 512):
                        c1 = min(ncols, c0 + 512)
                        sl = A[si % 12 + 20] if si % 12 + 20 < NSL else A[si % 12]
                        Pg.load(None, sl[:, 0:c1 - c0], dram[k * 128:(k + 1) * 128, c0:c1])
                        cp(cast_engs[si % 3], dst[:, k, c0:c1], sl[:, 0:c1 - c0])
                        si += 1
            sl = A[0]
            Pg.load(None, sl[:, 0:256], d_plw.rearrange("p m n -> p (m n)"))
            cp("dve", plwb[:].re("p m n -> p (m n)"), sl[:, 0:256])

            xt = sb([128, D], F32, "xt")
            xTb = sb([128, 8, 144], BF16, "xTb")
            Zr = sb([64, 4, 144], F32, "Zr")
            Zk = sb([64, 4, 144], F32, "Zk")
            Zwa = sb([64, 2, 144], F32, "Zwa")
            Zu = sb([128, 2, 144], F32, "Zu")
            Zq = sb([64, 4, 144], F32, "Zq")
            Zkm = sb([64, 4, 144], F32, "Zkm")
            SW = 64 if P2 else 128
            S32 = sb([64, 4, SW], F32, "S32")
            Sr = sb([64, 4, SW], RDT, "Sr")
            C32 = sb([64, 4, 65], F32, "C32")
            Cb = sb([64, 4, 65], BF16, "Cb")
            R32 = sb([64, 4, 64], F32, "R32")
            Rb = sb([64, 4, 64], BF16, "Rb")
            cst = [sb([128, 128], F32, "cs0"), sb([128, 128], F32, "cs1")]
            Pg.memset("dve", S32[:], 0.0)
            Pg.memset("dve", C32[:], 0.0)
            Pg.memset("dve", R32[:], 0.0)
            if not P2:
                for h in range(4):
                    cp("pool", S32[:, h, 64:128], ident[0:64, 0:64])
            else:
                msk = sb([64, NR], F32, "msk")
                gseg = sb([64, 4, 64], F32, "gseg")
                Pg.load(None, msk[:], d_msk)
                Pg.load(None, gseg[:], d_gseg)
                c_sg = A[0].view(A[0].h[0:64, :].rearrange("p (h n) -> p h n", h=4))
                c_ml = A[1].view(A[1].h[0:64, 0:260].rearrange("p (h n) -> p h n", h=4))
                c_bt = A[1].view(A[1].h[0:64, 300:304])
                c_rt = A[2].view(A[2].h[0:64, 0:256].rearrange("p (h n) -> p h n", h=4))
                c_gt = A[3].view(A[3].h[0:64, 0:256].rearrange("p (h n) -> p h n", h=4))
                c_cd = A[4].view(A[4].h[0:64, 0:260].rearrange("p (h n) -> p h n", h=4))
                for js in range(NR - 1):
                    rs = slice(js * 64, (js + 1) * 64)
                    Pg.load(None, c_sg[:], d_sall[rs, 0:512].rearrange("p (h n) -> p h n", h=4))
                    Pg.load(None, c_ml[:], d_sall[rs, 512:772].rearrange("p (h n) -> p h n", h=4))
                    Pg.load(None, c_bt[:], d_sall[rs, 772:776])
                    Pg.load(None, c_rt[:], d_sall[rs, 776:1032].rearrange("p (h n) -> p h n", h=4))
                    mj = msk[:, js:js + 1]
                    j, bk = Pg.bank()
                    for h in range(4):
                        tr(bk[0:64, j, h * 64:(h + 1) * 64], c_sg[:, h, 64:128], ident[0:64, 0:64])
                    cp("act", c_gt[:], bk[0:64, j, 0:256].re("p (h n) -> p h n", h=4))
                    j, bk = Pg.bank()
                    for h in range(4):
                        mm(bk[0:64, j, h * 64:(h + 1) * 64], c_gt[:, h, :], S32[:, h, :])
                    tt("dve", c_cd[:, :, 0:64], bk[0:64, j, 0:256].re("p (h n) -> p h n", h=4), c_sg[:, :, 0:64], ALU.add)
                    tt("dve", c_cd[:, :, 0:64], c_cd[:, :, 0:64], S32[:], ALU.subtract)
                    stt(S32[:], c_cd[:, :, 0:64], mj, S32[:], ALU.mult, ALU.add)
                    act(c_bt[:], c_bt[:], AF.Exp, scale=-1.0)
                    tt("dve", c_cd[:], C32[:], c_bt[:].re("p (h o) -> p h o", o=1).bc([64, 4, 65]), ALU.mult)
                    tt("dve", c_cd[:], c_cd[:], c_ml[:], ALU.add)
                    tt("dve", c_cd[:], c_cd[:], C32[:], ALU.subtract)
                    stt(C32[:], c_cd[:], mj, C32[:], ALU.mult, ALU.add)
                    tt("dve", c_cd[:, :, 0:64], R32[:], gseg[:], ALU.mult)
                    tt("dve", c_cd[:, :, 0:64], c_cd[:, :, 0:64], c_rt[:], ALU.add)
                    tt("dve", c_cd[:, :, 0:64], c_cd[:, :, 0:64], R32[:], ALU.subtract)
                    stt(R32[:], c_cd[:, :, 0:64], mj, R32[:], ALU.mult, ALU.add)
            cp("dve", Sr[:], S32[:])
            cp("dve", Cb[:], C32[:])
            cp("dve", Rb[:], R32[:])
            Pg.memset("dve", NBT[:], 0.0)

            FM_TILES = [("r", Zr, 0), ("k", Zk, 256), ("wa", Zwa, 512), ("u", Zu, 640), ("q", Zq, 896), ("km", Zkm, 1152)]
            if not P2:
                FM_TILES = [f for f in FM_TILES if f[0] in ("r", "k", "wa", "km")]
            else:
                FM_TILES = [f for f in FM_TILES if f[0] in ("u", "q", "km")]

            def proj_fm(c_lo, c_hi, d_lo):
                n = c_hi - c_lo
                for name, Z, off in FM_TILES:
                    j, bk = Pg.bank()
                    if name == "u":
                        for m in range(2):
                            for k in range(8):
                                mm(bk[:, j, m * 128:m * 128 + n], wfm[:, k, off + m * 128: off + (m + 1) * 128], xTb[:, k, c_lo:c_hi],
                                   start=(k == 0), stop=(k == 7))
                        cp("act", Z[:, :, d_lo:d_lo + n], bk[:, j, 0:256].re("p (m t) -> p m t", m=2)[:, :, 0:n])
                    else:
                        nm = 2 if name == "wa" else 4
                        for m in range(nm):
                            for k in range(8):
                                mm(bk[0:64, j, m * 128:m * 128 + n], wfm[:, k, off + m * 64: off + (m + 1) * 64], xTb[:, k, c_lo:c_hi],
                                   start=(k == 0), stop=(k == 7))
                        cp("act", Z[:, :, d_lo:d_lo + n], bk[0:64, j, 0:nm * 128].re("p (m t) -> p m t", m=nm)[:, :, 0:n])

            Pg.load(ld_x[0], xt[0:16, :], d_xh[0:HALO, :])
            j, bk = Pg.bank()
            for k in range(8):
                tr(bk[:, j, k * 16:(k + 1) * 16], xt[0:16, k * 128:(k + 1) * 128], ident[0:16, 0:16])
            cp("act", xTb[:, :, 128:144], bk[:, j, 0:128].re("p (k t) -> p k t", k=8))
            proj_fm(128, 144, 128)
            Pg.memset("dve", Vaug[:], 1.0)
            if not P2:
                Pg.memset("pool", AR[:, :, 1, :], 0.0)

            def b4(col, w=128):
                return rwp[:, col:col + 4].re("p (h o) -> p h o", o=1).bc([64, 4, w])

            def b2(col, w=128):
                return rwp[:, col:col + 2].re("p (h o) -> p h o", o=1).bc([64, 2, w])

            def v8(t_):
                return t_[:].re("p h (c l) -> p (h c) l", c=2)

            def tmproj(col_lo, ncols):
                j, bk = Pg.bank()
                for k in range(8):
                    mm(bk[:, j, 0:ncols], xTb[:, k, 16:144], wtm[:, k, col_lo:col_lo + ncols], start=(k == 0), stop=(k == 7))
                return j, bk

            def headnorm(Y, P_, ng, eps):
                red(hsum[0:P_, 0:ng], Y)
                tt("pool", hbig[0:P_, 0:ng, :], Y, Y, ALU.mult)
                red(hsq[0:P_, 0:ng], hbig[0:P_, 0:ng, :])
                ts("dve", hm[0:P_, 0:ng], hsum[0:P_, 0:ng], 1.0 / 64, None, ALU.mult)
                tt("dve", hv[0:P_, 0:ng], hm[0:P_, 0:ng], hm[0:P_, 0:ng], ALU.mult)
                stt(hv[0:P_, 0:ng], hsq[0:P_, 0:ng], 1.0 / 64, hv[0:P_, 0:ng], ALU.mult, ALU.subtract)
                ts("dve", hv[0:P_, 0:ng], hv[0:P_, 0:ng], eps, None, ALU.add)
                act(hv[0:P_, 0:ng], hv[0:P_, 0:ng], AF.Ln)
                act(hv[0:P_, 0:ng], hv[0:P_, 0:ng], AF.Exp, scale=-0.5)
                tt("dve", Y, Y, hm[0:P_, 0:ng].re("p (g o) -> p g o", o=1).bc([P_, ng, 64]), ALU.subtract)
                tt("dve", Y, Y, hv[0:P_, 0:ng].re("p (g o) -> p g o", o=1).bc([P_, ng, 64]), ALU.mult)

            def tile_body(ti, part):
                AR, Bt, Kt, Bpf, Kpf, Vc, Vcr, pp, zs_r = IFB[ti % 2]
                rk = zs_r
                if part != "B":
                    cp("pool", xTb[:, :, 0:16], xTb[:, :, 128:144])
                    yield
                    for name, Z, off in FM_TILES:
                        cp("pool", Z[:, :, 0:16], Z[:, :, 128:144])
                    yield
                    Pg.load(ld_x[0], xt[:], d_xh[HALO + ti * TT: HALO + (ti + 1) * TT, :])
                    Pg.load(ld_cs[ti % 2], cst[ti % 2][:], d_cs[ti * TT:(ti + 1) * TT, :])
                    yield
                    j, b2v = Pg.bank2()
                    for k in range(8):
                        tr(b2v[:, k // 4, (k % 4) * 128:(k % 4 + 1) * 128], xt[:, k * 128:(k + 1) * 128], ident[:])
                    cp("act", xTb[:, 0:4, 16:144], b2v[:, 0, :].re("p (k t) -> p k t", k=4))
                    yield
                    cp("dve", xTb[:, 4:8, 16:144], b2v[:, 1, :].re("p (k t) -> p k t", k=4))
                    yield
                    if P2:
                        Pg.store(st_sx, t_xTs[ti].re("p (k t) -> p k t", k=8), xTb[:, :, 16:144])
                    proj_fm(16, 144, 16)

                    yield
                    if RW:
                        jv, bv = Pg.bank2()
                        for c in range(2):
                            for w_, sh in ((0, 16), (1, 15)):
                                for k in range(8):
                                    mm(bv[0:64, c, w_ * 256:(w_ + 1) * 256], xTb[:, k, sh + c * 64: sh + (c + 1) * 64], wtm[:, k, 0:256],
                                       start=(k == 0), stop=(k == 7))
                        bvv = bv[0:64, :, :].re("p c (w n) -> p c w n", w=2)
                        tt("dve", tV[:], bvv[:, :, 1, :], rowp[0:64, R_MUV:R_MUV + 256].re("p (o n) -> p o n", o=1).bc([64, 2, 256]), ALU.mult)
                        yield
                        tt("dve", Vc[:], bvv[:, :, 0, :], rowp[0:64, RM_OMUV:RM_OMUV + 256].re("p (o n) -> p o n", o=1).bc([64, 2, 256]), ALU.mult)
                        yield
                        tt("dve", Vc[:], Vc[:], tV[:], ALU.add)
                        yield
                        cp("pool", Vcr[:], Vc[:])
                        yield
                    if P2:
                        jg, bg = Pg.bank()
                        for c in range(2):
                            for k in range(8):
                                mm(bg[0:64, jg, c * 256:(c + 1) * 256], xTb[:, k, 16 + c * 64: 16 + (c + 1) * 64], wtm[:, k, 256:512],
                                   start=(k == 0), stop=(k == 7))
                        act(grw[:], bg[0:64, jg, :].re("p (c n) -> p c n", c=2), AF.Silu)
                        yield

                    yield
                    if RW:
                        if True:
                            tt("dve", tA[:], Zr[:, :, 16:144], b4(30), ALU.mult)
                            yield
                            tt("pool", tB[:], Zr[:, :, 15:143], b4(0), ALU.mult)
                            yield
                            tt("dve", zs_r[:], tA[:], tB[:], ALU.add)
                            yield
                        tt("dve", tA[:], Zk[:, :, 16:144], b4(34), ALU.mult)
                        yield
                        tt("pool", tB[:], Zk[:, :, 15:143], b4(4), ALU.mult)
                        yield
                        tt("dve", zs_k[:], tA[:], tB[:], ALU.add)
                        yield
                        tt("dve", tA[:, 0:2, :], Zwa[:, :, 16:144], b2(38), ALU.mult)
                        yield
                        tt("pool", tB[:, 0:2, :], Zwa[:, :, 15:143], b2(8), ALU.mult)
                        yield
                        tt("dve", zs_wa[:], tA[:, 0:2, :], tB[:, 0:2, :], ALU.add)
                        yield
                        act(th[:], zs_wa[:, 0, :], AF.Tanh)
                        yield
                        j, bk = Pg.bank()
                        for h in range(4):
                            mm(bk[0:64, j, h * 128:(h + 1) * 128], w2[:, h * 64:(h + 1) * 64], th[:])
                        tt("dve", tA[:], bk[0:64, j, :].re("p (h t) -> p h t", h=4), b4(10), ALU.add)
                        yield
                        act(sig[:], tA[:], AF.Sigmoid)
                        yield
                        j, bk = Pg.bank()
                        for h in range(4):
                            mm(bk[0:64, j, h * 128:(h + 1) * 128], a2[:, h * 64:(h + 1) * 64], zs_wa[:, 1, :])
                        tt("dve", tB[:], bk[0:64, j, :].re("p (h t) -> p h t", h=4), b4(14), ALU.add)
                        yield
                        act(aa[:], tB[:], AF.Sigmoid)
                        yield
                        Pg.scan_add(gsc[:].re("p h t -> p (h t)"), ones[0:64, 0:1].bc([64, 512]), sig[:].re("p h t -> p (h t)"))
                        yield
                        tt("dve", esc[:], gsc[:], sig[:], ALU.subtract)
                        yield
                        base8 = v8(esc)[:, :, 0:1].bc([64, 8, 64])
                        tt("dve", v8(cE), v8(esc), base8, ALU.subtract)
                        yield
                        tt("dve", v8(cumI), v8(gsc), base8, ALU.subtract)
                        yield
                        tt("dve", v8(dl), v8(cumI), v8(cumI)[:, :, 63:64].bc([64, 8, 64]), ALU.subtract)
                        yield
                        act(pp[:], cumI[:], AF.Exp, scale=-C0)
                        yield
                        act(pinv[:], cumI[:], AF.Exp, scale=C0)
                        yield
                        act(pprev[:], cE[:], AF.Exp, scale=-C0)
                        yield
                        act(pLp[:], dl[:], AF.Exp, scale=C0)
                        yield
                        tt("dve", kq[:], zs_k[:], b4(18), ALU.mult)
                        yield
                        tt("pool", tC[:], kq[:], kq[:], ALU.mult)
                        yield
                        j, bk = Pg.bank()
                        mm(bk[0:64, j, :], ones[0:64, 0:64], tC[:].re("p h t -> p (h t)"))
                        ts("dve", tC[:].re("p h t -> p (h t)"), bk[0:64, j, :], 1e-18, None, ALU.max)
                        yield
                        act(tC[:], tC[:], AF.Ln)
                        yield
                        act(tC[:], tC[:], AF.Exp, scale=-0.5)
                        yield
                        tt("dve", kk[:], kq[:], tC[:], ALU.mult)
                        yield
                        tt("dve", tC[:], aa[:], b4(22), ALU.mult)
                        yield
                        tt("dve", tC[:], tC[:], b4(40), ALU.add)
                        yield
                        tt("dve", kmod[:], zs_k[:], tC[:], ALU.mult)
                        yield
                        tt("pool", bbv[:], kk[:], aa[:], ALU.mult)
                        yield
                        stt(AR[:, :, 0, :], v8(kk), -1.0, v8(pprev), ALU.mult, ALU.mult)
                        yield
                        if True:
                            tt("dve", AR[:, :, 1, :], v8(zs_r), v8(pp), ALU.mult)
                            yield
                            tt("dve", rk[:], zs_r[:], b4(26), ALU.mult)
                            yield
                            tt("dve", rk[:], rk[:], kmod[:], ALU.mult)
                            yield
                        tt("dve", Bt[:], v8(bbv), v8(pinv), ALU.mult)
                        yield
                        tt("dve", Kt[:], v8(kmod), v8(pinv), ALU.mult)
                        yield
                        tt("pool", Bpf[:], v8(bbv), v8(pLp), ALU.mult)
                        yield
                        tt("pool", Kpf[:], v8(kmod), v8(pLp), ALU.mult)
                        yield
                if part != "A":
                    if RW:
                        Xc = X[0]
                        for src, dst in ((AR, None), (Bpf, Bptm), (Kpf, Kptm)):
                            j, bk = Pg.bank()
                            for ch in range(8):
                                in_ = src[:, ch, 0, :] if src is AR else src[:, ch, :]
                                tr(bk.b16((slice(0, 64), j, slice(ch * 64, (ch + 1) * 64))), in_, identb[0:64, 0:64])
                            pview = bk.b16((slice(0, 64), j, slice(0, 512))).re("p (c d) -> p c d", c=8)
                            cp("act", Xc[:, :, 0:64] if dst is None else dst[:], pview)
                        yield
                        for lhs, dstG in ((Bt, Gb), (Kt, Gk)):
                            j, b2v = Pg.bank2()
                            for ch in range(8):
                                mm(b2v[0:64, ch // 4, (ch % 4) * 128:(ch % 4 + 1) * 128], lhs[:, ch, :], AR[:, ch, :, :].re("p w l -> p (w l)"))
                            tt("dve", dstG[:], b2v[0:64, :, :].re("p b (c n) -> p (b c) n", c=4),
                               mG[:].re("p (o n) -> p o n", o=1).bc([64, 8, 128]), ALU.mult)
                        yield
                        j, bk = Pg.bank()
                        for ch in range(8):
                            mm(bk[0:64, j, ch * 64:(ch + 1) * 64], AR[:, ch, 0, :], Bt[:, ch, :])
                        tt("dve", Pn[0][:], bk[0:64, j, :].re("p (c s) -> p c s", c=8),
                           mgt[0:64, 0:64].re("p (o s) -> p o s", o=1).bc([64, 8, 64]), ALU.mult)
                        yield
                        yield
                        j, bk = Pg.bank()
                        for ch in range(8):
                            h_, c_ = ch // 2, ch % 2
                            mm(bk[0:64, j, ch * 64:(ch + 1) * 64], Gk[:, ch, 0:64], Vcr[:, c_, h_ * 64:(h_ + 1) * 64])
                        cp("act", Xc[:, :, 64:128], bk[0:64, j, :].re("p (c d) -> p c d", c=8))
                        yield
                        Ptc = None
                        for lvl in range(6):
                            Xn = X[(lvl + 1) % 2]
                            yield
                            j, b2v = Pg.bank2()
                            for ch in range(8):
                                lh = Gb[:, ch, 0:64] if lvl == 0 else Ptc[:, ch, :]
                                o_ = b2v[0:64, ch // 4, (ch % 4) * 128:(ch % 4 + 1) * 128]
                                mm(o_, lh, Xc[:, ch, :], start=True, stop=False)
                                mm(o_, identb[0:64, 0:64], Xc[:, ch, :], start=False, stop=True)
                            cp("act", Xn[:], b2v[0:64, :, :].re("p b (c n) -> p (b c) n", c=4))
                            if lvl < 5:
                                Pnc, Pnn, Ptn = Pn[lvl % 2], Pn[(lvl + 1) % 2], Ptb[(lvl + 1) % 2]
                                yield
                                j, bk = Pg.bank()
                                for ch in range(8):
                                    lhT = Gb[:, ch, 0:64] if lvl == 0 else Ptc[:, ch, :]
                                    mm(bk[0:64, j, ch * 64:(ch + 1) * 64], Pnc[:, ch, :], lhT)
                                cp("act", Ptn[:], bk[0:64, j, :].re("p (c d) -> p c d", c=8))
                                yield
                                j, bk = Pg.bank()
                                for ch in range(8):
                                    lhT = Gb[:, ch, 0:64] if lvl == 0 else Ptc[:, ch, :]
                                    mm(bk[0:64, j, ch * 64:(ch + 1) * 64], lhT, Pnc[:, ch, :])
                                cp("act", Pnn[:], bk[0:64, j, :].re("p (c d) -> p c d", c=8))
                                Ptc = Ptn
                            Xc = Xn
                        yield
                        yield
                        j, bk = Pg.bank()
                        for ch in range(8):
                            mm(bk[0:64, j, ch * 64:(ch + 1) * 64], Xc[:, ch, 0:64], Bptm[:, ch, :])
                        cp("act", Phi[:], bk[0:64, j, :].re("p (c d) -> p c d", c=8))
                        yield
                        j, bk = Pg.bank()
                        for ch in range(8):
                            mm(bk[0:64, j, ch * 64:(ch + 1) * 64], Xc[:, ch, 0:64], Gb[:, ch, 64:128])
                        tt("dve", Rhat[:], bk[0:64, j, :].re("p (c d) -> p c d", c=8), AR[:, :, 1, :], ALU.add)
                        Pg.store(None, t_rp[ti], A[22].b16((slice(0, 64), slice(0, 1024))))
                        yield
                        yield
                        j, bk = Pg.bank()
                        for ch in range(8):
                            h_, c_ = ch // 2, ch % 2
                            o = bk[0:64, j, ch * 64:(ch + 1) * 64]
                            mm(o, Gb[:, ch, 64:128], Xc[:, ch, 64:128], start=True, stop=False)
                            mm(o, Gk[:, ch, 64:128], Vcr[:, c_, h_ * 64:(h_ + 1) * 64], start=False, stop=True)
                        cp("act", Yrw[:], bk[0:64, j, :].re("p (c d) -> p c d", c=8))
                        Pg.store(None, t_yh[ti], Yrw[:].re("p c d -> p (c d)"))
                        yield
                        yield
                        j, bk = Pg.bank()
                        for ch in range(8):
                            mm(bk[0:64, j, ch * 2:ch * 2 + 1], v8(rk)[:, ch, :], ones[0:64, 0:1])
                        cp("act", bon[:], bk[0:64, j, 0:16].re("p (c o) -> p c o", o=2)[:, :, 0])
                        tt("dve", BVt[:].re("p (h c) d -> p h c d", c=2), Vc[:].re("p c (h d) -> p h c d", h=4),
                           bon[:].re("p (h c o) -> p h c o", c=2, o=1).bc([64, 4, 2, 64]), ALU.mult)
                        Pg.store(None, t_bv[ti], BVt[:].re("p c d -> p (c d)"))
                        yield
                        for c in range(2):
                            j, bk = Pg.bank()
                            for h in range(4):
                                ch = h * 2 + c
                                o = bk[0:64, j, h * 64:(h + 1) * 64]
                                mm(o, Bptm[:, ch, :], Xc[:, ch, 64:128], start=True, stop=False)
                                mm(o, Kptm[:, ch, :], Vcr[:, c, h * 64:(h + 1) * 64], start=False, stop=True)
                            cp("act", psi[:, c, :], bk[0:64, j, 0:256])
                            cp("pool", pLt[:, c * 4:(c + 1) * 4], pp[:, :, c * 64 + 63])
                        Pg.store(None, t_ps[ti], psi[:].re("p c n -> p (c n)"))
                        Pg.store(None, t_pl[ti], pLt[:])
                        Rhat_u, Phi_u, yhat_u, BV_u, psi_u, pL_u = Rhat, Phi, Yrw, BVt, psi, pLt
                    else:
                        par_ = ti % 2
                        Pg.load(ld_h[0][par_], RPl[par_][:], t_rp[ti])
                        Pg.load(ld_h[1][par_], YHl[par_][:], t_yh[ti])
                        Pg.load(ld_h[2][par_], BVl[par_][:], t_bv[ti])
                        Pg.load(ld_h[3][par_], PSl[par_][:], t_ps[ti])
                        Pg.load(ld_h[4][par_], PLl[par_][:], t_pl[ti])
                        Phi_u = View(RPl[par_].ts, RPl[par_].ap[:, 0:512].rearrange("p (c d) -> p c d", c=8))
                        Rhat_u = View(RPl[par_].ts, RPl[par_].ap[:, 512:1024].rearrange("p (c d) -> p c d", c=8))
                        yhat_u = View(YHl[par_].ts, YHl[par_].ap.rearrange("p (c d) -> p c d", c=8))
                        BV_u = View(BVl[par_].ts, BVl[par_].ap.rearrange("p (c d) -> p c d", c=8))
                        psi_u = View(PSl[par_].ts, PSl[par_].ap.rearrange("p (c n) -> p c n", c=2))
                        pL_u = PLl[par_]
                    yield
                    if P2:
                        jy, by = Pg.bank()
                    for c in range(2):
                        if P2:
                            for h in range(4):
                                ch = h * 2 + c
                                mm(by[0:64, jy, ch * 64:(ch + 1) * 64], Rhat_u[:, ch, :], Sr[:, h, 0:64])
                        yield
                        js, bs = Pg.bank()
                        for h in range(4):
                            ch = h * 2 + c
                            mm(bs[0:64, js, h * 128:h * 128 + SW], Phi_u[:, ch, :], Sr[:, h, :])
                        tt("dve", S32[:], S32[:], pL_u[:, c * 4:(c + 1) * 4].re("p (h o) -> p h o", o=1).bc([64, 4, SW]), ALU.mult)
                        tt("dve", S32[:], S32[:], bs[0:64, js, :].re("p (h n) -> p h n", h=4)[:, :, 0:SW], ALU.add)
                        tt("dve", S32[:, :, 0:64], S32[:, :, 0:64], psi_u[:, c, :].re("p (h d) -> p h d", h=4), ALU.add)
                        cp("act", Sr[:], S32[:])
                    if P2:
                        tt("dve", Yrw[:], by[0:64, jy, :].re("p (c d) -> p c d", c=8), yhat_u[:], ALU.add)

                if part != "B":
                    yield
                    j, bk = tmproj(2560, 8)
                    tt("dve", gat[:], bk[:, j, 0:8], rowp[:, RM_IB:RM_IB + 8], ALU.add)
                    yield
                    cp("pool", ibs[:], gat[:, 0:4])
                    yield
                    act(nlf[:], gat[:, 4:8], AF.Exp, scale=-1.0)
                    yield
                    act(nlf[:], nlf[:], AF.Ln, bias=1.0)
                    yield
                    j, bk = Pg.bank()
                    mm(bk[:, j, 0:4], mle[:], nlf[:])
                    mm(bk[0:64, j, 8:12], ones[:, 0:64], nlf[:])
                    act(ebl[:], bk[:, j, 0:4], AF.Exp, scale=-1.0)
                    yield
                    act(etot[:], bk[0:64, j, 8:12], AF.Exp, scale=-1.0)
                    yield
                    tt("dve", NBT[:], NBT[:], bk[0:64, j, 8:12], ALU.add)
                    yield
                    for h in range(4):
                        ts("dve", Ah[:, h, :], mgt[:], nlf[:, h:h + 1], None, ALU.mult)
                    yield
                    j, bk = Pg.bank()
                    for h in range(4):
                        mm(bk[:, j, h * 128:(h + 1) * 128], Ah[:, h, :], mle[:])
                    for h in range(4):
                        act(Eh[:, h, :], bk[:, j, h * 128:(h + 1) * 128], AF.Exp, scale=-1.0, bias=ibs[:, h:h + 1])
                    yield
                    j, bk = tmproj(2048, 512)
                    cp("act", Vaug[:, :, 0:64], bk[:, j, 0:256].re("p (h d) -> p h d", h=4))
                    yield
                    if P2:
                        cp("act", vrt[:], bk[:, j, 256:512].re("p (h d) -> p h d", h=4))
                        yield
                    tt("dve", vz[:], bk[:, j, 256:512].re("p (h d) -> p h d", h=4), zx[:, 0:4].re("p (h o) -> p h o", o=1).bc([128, 4, 64]), ALU.mult)
                    yield
                    for Zs, cdst, qi, outT in ((Zq, cq, 0, qT), (Zkm, ck, 1, kT)):
                        if not P2 and qi == 0:
                            continue
                        for tap in range(4):
                            wv = convw[:, qi, :, tap:tap + 1].bc([64, 4, 128])
                            src = Zs[:, :, 13 + tap:141 + tap]
                            if tap == 0:
                                tt("dve", cdst[:], src, wv, ALU.mult)
                                yield
                            else:
                                tt("pool", ctmp[:], src, wv, ALU.mult)
                                yield
                                tt("dve", cdst[:], cdst[:], ctmp[:], ALU.add)
                                yield
                        act(outT[:], cdst[:], AF.Silu)
                    yield
                    j, bk = Pg.bank()
                    for h in range(4):
                        tr(bk.b16((slice(0, 128), j, slice(h * 64, (h + 1) * 64))), kT[:, h, :], identb[0:64, 0:64])
                    tt("dve", ktw[:], bk.b16((slice(0, 128), j, slice(0, 256))).re("p (h d) -> p h d", h=4), Eh[:, :, 127:128].bc([128, 4, 64]), ALU.mult)
                    yield
                    if P2:
                        tt("pool", Em[:], Eh[:], mle[:].re("p (o n) -> p o n", o=1).bc([128, 4, 128]), ALU.mult)
                        yield
                        j, bk = Pg.bank()
                        for h in range(4):
                            mm(bk[:, j, h * 128:(h + 1) * 128], kT[:, h, :], qT[:, h, :])
                        tt("dve", PTm[:], bk[:, j, :].re("p (h l) -> p h l", h=4), Em[:], ALU.mult)
                        yield
                        ji, bki = Pg.bank()
                        yield
                        jn, bkn = Pg.bank()
                        for h in range(4):
                            mm(bki[:, ji, h * 65:(h + 1) * 65], qT[:, h, :], Cb[:, h, :])
                        for h in range(4):
                            mm(bkn[:, jn, h * 65:(h + 1) * 65], PTm[:, h, :], Vaug[:, h, :])
                        tt("dve", numi[:], bki[:, ji, 0:260].re("p (h d) -> p h d", h=4), ebl[:].re("p (h o) -> p h o", o=1).bc([128, 4, 65]), ALU.mult)
                        yield
                        tt("dve", numt[:], numi[:], bkn[:, jn, 0:260].re("p (h d) -> p h d", h=4), ALU.add)
                        yield
                        act(dden[:], numt[:, :, 64], AF.Abs)
                        yield
                        ts("dve", dden[:], dden[:], 1.0, None, ALU.max)
                        yield
                        Pg.recip(dden[:], dden[:])
                        yield
                        tt("dve", Yml[:], numt[:, :, 0:64], dden[:].re("p (h o) -> p h o", o=1).bc([128, 4, 64]), ALU.mult)
                    yield
                    j, bk = Pg.bank()
                    for h in range(4):
                        mm(bk[0:64, j, h * 65:(h + 1) * 65], ktw[:, h, :], Vaug[:, h, :])
                    tt("dve", ctmp2[:], C32[:], etot[:].re("p (h o) -> p h o", o=1).bc([64, 4, 65]), ALU.mult)
                    yield
                    tt("dve", C32[:], ctmp2[:], bk[0:64, j, 0:260].re("p (h d) -> p h d", h=4), ALU.add)
                    yield
                    cp("act", Cb[:], C32[:])
                    yield

                    yield
                    yield
                    j, bk = tmproj(1536, 512)
                    zv = bk[:, j, :].re("p (q w i) -> p q w i", q=8, w=2)
                    csv = cst[ti % 2]
                    csA = csv[:, 0:64].re("p (o w i) -> p o w i", o=1, w=2).bc([128, 8, 2, 32])
                    csB = csv[:, 64:128].re("p (o w i) -> p o w i", o=1, w=2).bc([128, 8, 2, 32])
                    tt("dve", rA[:], zv[:, :, 0:1, :].bc([128, 8, 2, 32]), csA, ALU.mult)
                    yield
                    tt("dve", rB[:], zv[:, :, 1:2, :].bc([128, 8, 2, 32]), csB, ALU.mult)
                    yield
                    tt("dve", qkr[:].re("p q (w i) -> p q w i", w=2), rA[:], rB[:], ALU.add)
                    yield
                    if P2:
                        j, bk = Pg.bank()
                        for q_ in range(8):
                            tr(bk.b16((slice(0, 64), j, slice(q_ * 128, (q_ + 1) * 128))), qkr[:, q_, :], identb[:])
                        cp("act", qkT[:], bk.b16((slice(0, 64), j, slice(0, 1024))).re("p (q t) -> p q t", q=8))
                        yield
                        j, bk = Pg.bank()
                        for h in range(4):
                            mm(bk[:, j, h * 128:(h + 1) * 128], qkT[:, 4 + h, :], qkT[:, h, :])
                        tt("dve", PTr[:], bk[:, j, :].re("p (h l) -> p h l", h=4), dect[:], ALU.mult)
                        yield
                        ji, bki = Pg.bank()
                        yield
                        jn, bkn = Pg.bank()
                        for h in range(4):
                            mm(bki[:, ji, h * 64:(h + 1) * 64], qkT[:, h, :], Rb[:, h, :])
                        for h in range(4):
                            mm(bkn[:, jn, h * 64:(h + 1) * 64], PTr[:, h, :], vrt[:, h, :])
                        tt("dve", yint[:], bki[:, ji, 0:256].re("p (h d) -> p h d", h=4), zx[:, 4:8].re("p (h o) -> p h o", o=1).bc([128, 4, 64]), ALU.mult)
                        yield
                        tt("dve", Yrt[:], yint[:], bkn[:, jn, 0:256].re("p (h d) -> p h d", h=4), ALU.add)
                    yield
                    j, bk = Pg.bank()
                    for h in range(4):
                        mm(bk[0:64, j, h * 64:(h + 1) * 64], qkr[:, 4 + h, :], vz[:, h, :])
                    tt("dve", rtmp[:], R32[:], gb[:], ALU.mult)
                    yield
                    tt("dve", R32[:], rtmp[:], bk[0:64, j, 0:256].re("p (h d) -> p h d", h=4), ALU.add)
                    yield
                    cp("act", Rb[:], R32[:])
                    yield

                    if not P2:
                        return

                    yield
                    tt("dve", s2[:, :, 1:144], Zu[:, :, 1:144], Zu[:, :, 0:143], ALU.add)
                    yield
                    tt("dve", s4[:, :, 3:144], s2[:, :, 3:144], s2[:, :, 1:142], ALU.add)
                    yield
                    tt("dve", s8[:, 7:144], s4[:, 1, 7:144], s4[:, 1, 3:140], ALU.add)
                    yield
                    tt("dve", s16[64:128, 15:144], s8[64:128, 15:144], s8[64:128, 7:136], ALU.add)
                    yield
                    sels = ((s2[0:64, 0, 16:144], 0, 0, 0.5), (s4[64:128, 0, 16:144], 64, 0, 0.25),
                            (s8[0:64, 16:144], 0, 1, 0.125), (s16[64:128, 16:144], 64, 1, 0.0625))
                    for sw_, p0, m_, iw in sels:
                        if ti == 0:
                            tt("dve", pm[p0:p0 + 64, m_, :], sw_, cnt0[p0:p0 + 64, m_, :], ALU.mult)
                            yield
                            tt("dve", pd[p0:p0 + 64, m_, :], pm[p0:p0 + 64, m_, :], Zu[p0:p0 + 64, m_, 16:144], ALU.subtract)
                            yield
                        else:
                            stt(pd[p0:p0 + 64, m_, :], sw_, iw, Zu[p0:p0 + 64, m_, 16:144], ALU.mult, ALU.subtract)
                    yield
                    jpl, bpl = Pg.bank()
                    for m_ in range(2):
                        mm(bpl[:, jpl, m_ * 128:(m_ + 1) * 128], pd[:, m_, :], plwb[:, m_, :])

                    yield
                    headnorm(Yrw[:], 64, 8, RW_EPS)
                    lnw = rowp[0:64, R_LNW:R_LNW + 256].re("p (h o d) -> p h o d", h=4, o=1).bc([64, 4, 2, 64])
                    lnb = rowp[0:64, R_LNB:R_LNB + 256].re("p (h o d) -> p h o d", h=4, o=1).bc([64, 4, 2, 64])
                    Y4 = Yrw[:].re("p (h c) d -> p h c d", c=2)
                    tt("dve", Y4, Y4, lnw, ALU.mult)
                    yield
                    tt("dve", Y4, Y4, lnb, ALU.add)
                    yield
                    tt("dve", Yrw[:], Yrw[:], BV_u[:], ALU.add)
                    yield
                    tt("dve", mixa[:].re("p c (h d) -> p h c d", h=4), Y4, grw[:].re("p c (h d) -> p h c d", h=4), ALU.mult)
                    yield
                    j, bk = tmproj(512, 512)
                    act(g2[:], bk[:, j, :], AF.Silu)
                    yield
                    tt("dve", g2[:, 0:256], g2[:, 0:256], rowp[:, R_PLS:R_PLS + 256], ALU.mult)
                    yield
                    tt("dve", mixb[:, 0:256], bpl[:, jpl, 0:256], g2[:, 0:256], ALU.mult)
                    yield
                    j, bk = tmproj(1024, 512)
                    act(g3[:], bk[:, j, 0:256], AF.Silu)
                    yield
                    act(osg[:], bk[:, j, 256:512], AF.Sigmoid)
                    yield
                    tt("dve", Yml[:], Yml[:], osg[:].re("p (h d) -> p h d", h=4), ALU.mult)
                    yield
                    headnorm(Yml[:], 128, 4, HN_EPS)
                    tt("dve", g2[:, 256:512], g2[:, 256:512], rowp[:, R_MLLN:R_MLLN + 256], ALU.mult)
                    yield
                    tt("dve", mixb[:, 256:512], Yml[:].re("p h d -> p (h d)"), g2[:, 256:512], ALU.mult)
                    yield
                    headnorm(Yrt[:], 128, 4, HN_EPS)
                    tt("dve", mixb[:, 512:768], Yrt[:].re("p h d -> p (h d)"), g3[:], ALU.mult)
                    yield
                    yield
                    j, bk = Pg.bank()
                    for c in range(2):
                        for k in range(2):
                            tr(bk.b16((slice(0, 128), j, slice(k * 128 + c * 64, k * 128 + (c + 1) * 64))), mixa[:, c, k * 128:(k + 1) * 128], identb[0:64, 0:64])
                    for k in range(2, 8):
                        tr(bk.b16((slice(0, 128), j, slice(k * 128, (k + 1) * 128))), mixb[:, (k - 2) * 128:(k - 1) * 128], identb[:])
                    cp("act", mixT[:], bk.b16((slice(0, 128), j, slice(0, 1024))).re("p (k t) -> p k t", k=8))
                    yield
                    Pg.store(st_sm, t_mix[ti].re("p (k t) -> p k t", k=8), mixT[:])


            ALLB = list(range(8))

            def run_all(g):
                for _ in g:
                    pass
            if not (RW and PIPE):
                for ti in range(NT):
                    Pg.set_pool(ALLB)
                    run_all(tile_body(ti, "all"))
            else:
                PA, PB = [0, 1, 2, 3], [4, 5, 6, 7]
                KA = 5
                Pg.set_pool(PA)
                run_all(tile_body(0, "A"))
                for ti in range(NT):
                    gens = [(PB, tile_body(ti, "B"))]
                    if ti + 1 < NT:
                        gens.append((PA, tile_body(ti + 1, "A")))
                    while gens:
                        for pg in list(gens):
                            Pg.set_pool(pg[0])
                            try:
                                for _ in range(KA if pg[0] is PA else 1):
                                    next(pg[1])
                            except StopIteration:
                                gens.remove(pg)
                Pg.set_pool(ALLB)

            if not P2:
                Pg.store(None, d_osum[:, 0:512].rearrange("p (h n) -> p h n", h=4), S32[:])
                Pg.store(None, d_osum[:, 512:772].rearrange("p (h n) -> p h n", h=4), C32[:])
                Pg.store(None, d_osum[:, 772:776], NBT[:])
                Pg.store(None, d_osum[:, 776:1032].rearrange("p (h n) -> p h n", h=4), R32[:])
            S.barrier()
            S.emit()

        if P2:
            with contextlib.ExitStack() as stE:
                S.stack = stE
                S.barrier()
                wout = sb([128, 8, D], BF16, "wout")
                pleg = sb([128, 8, D], BF16, "pleg")
                plew = sb([128, 2, D], BF16, "plew")
                lng = sb([128, D], F32, "lng")
                lnb_ = sb([128, D], F32, "lnb")
                stg = [sb([128, D], F32, "stg0"), sb([128, D], F32, "stg1"), sb([128, D], F32, "stg2")]
                Pg.load(None, lng[:], d_row[R_LNG:R_LNG + D].partition_broadcast(128))
                Pg.load(None, lnb_[:], d_row[R_LNBB:R_LNBB + D].partition_broadcast(128))
                si = 0
                cast_engs = ["dve", "pool", "act"]
                for dst, dram, kc in ((wout, d_wout, 8), (pleg, d_pleg, 8), (plew, d_plew, 2)):
                    for k in range(kc):
                        sg = stg[si % 3]
                        Pg.load(None, sg[:], dram[k * 128:(k + 1) * 128, :])
                        cp(cast_engs[si % 3], dst[:, k, :], sg[:])
                        si += 1
                xte = [sb([128, D], F32, "xte0"), sb([128, D], F32, "xte1")]
                xTe = [sb([128, 8, 128], BF16, "xTe0"), sb([128, 8, 128], BF16, "xTe1")]
                mxe = [sb([128, 8, 128], BF16, "mxe0"), sb([128, 8, 128], BF16, "mxe1")]
                pte = [sb([128, 256], F32, "pte0"), sb([128, 256], F32, "pte1")]
                pT = sb([128, 2, 128], BF16, "pT")
                sgp = sb([128, D], F32, "sgp")
                acc = [sb([128, D], F32, "acc0"), sb([128, D], F32, "acc1")]
                bst = sb([128, 2, 6], F32, "bst")
                mv = sb([128, 2], F32, "mv")
                rstd = sb([128, 1], F32, "rstd")
                for ti in range(NT):
                    par = ti % 2
                    Pg.load(ld_x[par], xte[par][:], d_xh[HALO + ti * TT: HALO + (ti + 1) * TT, :])
                    Pg.load(ld_t[par], xTe[par][:], t_xTs[ti].re("p (k t) -> p k t", k=8), q="act")
                    Pg.load(ld_m[par], mxe[par][:], t_mix[ti].re("p (k t) -> p k t", k=8), q="pool")
                    Pg.load(ld_p[par], pte[par][:], d_p[ti * TT:(ti + 1) * TT, :])
                    j, bk = Pg.bank()
                    for k in range(2):
                        tr(bk[:, j, k * 128:(k + 1) * 128], pte[par][:, k * 128:(k + 1) * 128], ident[:])
                    cp("act", pT[:], bk[:, j, 0:256].re("p (k t) -> p k t", k=2))
                    ac = acc[par]
                    for half in range(2):
                        cs_ = slice(half * 512, (half + 1) * 512)
                        jg, bg = Pg.bank()
                        for k in range(8):
                            mm(bg[:, jg, :], xTe[par][:, k, :], pleg[:, k, cs_], start=(k == 0), stop=(k == 7))
                        act(sgp[:, cs_], bg[:, jg, :], AF.Sigmoid)
                        jw, bw = Pg.bank()
                        for k in range(2):
                            mm(bw[:, jw, :], pT[:, k, :], plew[:, k, cs_], start=(k == 0), stop=(k == 1))
                        tt("dve", sgp[:, cs_], sgp[:, cs_], bw[:, jw, :], ALU.mult)
                        jo, bo = Pg.bank()
                        for k in range(8):
                            mm(bo[:, jo, :], mxe[par][:, k, :], wout[:, k, cs_], start=(k == 0), stop=(k == 7))
                        stt(ac[:, cs_], xte[par][:, cs_], ALPHA, bo[:, jo, :], ALU.mult, ALU.add)
                        tt("pool", ac[:, cs_], ac[:, cs_], sgp[:, cs_], ALU.add)
                        Pg.bn_stats(bst[:, half, :], ac[:, cs_])
                    Pg.bn_aggr(mv[:], bst[:].re("p a b -> p (a b)"))
                    ts("dve", rstd[:], mv[:, 1:2], LN_EPS, None, ALU.add)
                    act(rstd[:], rstd[:], AF.Ln)
                    act(rstd[:], rstd[:], AF.Exp, scale=-0.5)
                    ts("dve", ac[:], ac[:], mv[:, 0:1], rstd[:, 0:1], ALU.subtract, ALU.mult)
                    tt("pool", ac[:], ac[:], lng[:], ALU.mult)
                    tt("dve", ac[:], ac[:], lnb_[:], ALU.add)
                    out_toks.append(Pg.store(st_o[par], d_out[ti * TT:(ti + 1) * TT, :], ac[:]))
                    if dr.get("tail") is not None and ti == NT - 1:
                        Pg.store(None, dr["tail"], ac[TT - HALO:TT, :])
                S.barrier()
                S.emit()
        S.stack = S.base


def _perm_rot(n_heads=4, hd=64):
    idx = []
    for h in range(n_heads):
        idx += [h * hd + 2 * i for i in range(hd // 2)] + [h * hd + 2 * i + 1 for i in range(hd // 2)]
    return np.array(idx)


def pack_layer(L, w_in, rw_mu, rw_w0, rw_w2, rw_a0, rw_a2, rw_kk, rw_ka, rw_rk, rw_ln_w, rw_ln_b, pl_w, pl_scale,
               ml_conv, ml_ib, ml_fb, ml_ln_w, ln_g, ln_b):
    W = np.asarray(w_in[L], np.float32)
    o = np.cumsum([0, 896, 256, 256, 256, 512, 256, 4, 4, 256, 256, 256, 256, 256, 256])
    rw_in, rw_g, pl_u, pl_g, ml_qk, ml_v, ml_i, ml_f, ml_o, ml_g, rt_q, rt_k, rt_v, rt_g = [W[:, o[i]:o[i + 1]] for i in range(14)]
    pr = _perm_rot()
    wfm = np.concatenate([rw_in[:, 0:256], rw_in[:, 256:512], rw_in[:, 768:896], pl_u, ml_qk[:, 0:256], ml_qk[:, 256:512]], axis=1)
    wtm = np.concatenate([rw_in[:, 512:768], rw_g, pl_g, ml_g, rt_g, ml_o, rt_q[:, pr], rt_k[:, pr], ml_v, rt_v, ml_i, ml_f], axis=1)
    assert wfm.shape[1] == FMC and wtm.shape[1] == TMC
    hT = lambda v: np.asarray(v, np.float32).reshape(4, 64).T
    mu = np.asarray(rw_mu[L], np.float32)
    rwp = np.concatenate([hT(mu[0:256]), hT(mu[256:512]), mu[768:832, None], mu[832:896, None],
                          hT(rw_w0[L]), hT(rw_a0[L]), hT(rw_kk[L]), hT(rw_ka[L]), hT(rw_rk[L])], axis=1)
    assert rwp.shape == (64, 30)
    rowp = np.zeros(4360, np.float32)
    rowp[0:256] = mu[512:768]
    rowp[256:512] = rw_ln_w[L]
    rowp[512:768] = rw_ln_b[L]
    rowp[768:1024] = pl_scale[L]
    rowp[1024:1280] = ml_ln_w[L]
    rowp[1280:2304] = ln_g[L]
    rowp[2304:3328] = ln_b[L]
    rowp[3328:3332] = ml_ib[L]
    rowp[3332:3336] = ml_fb[L]
    cw = np.asarray(ml_conv[L], np.float32)
    convw = cw.T.reshape(2, 4, 64, 4).transpose(2, 0, 1, 3)
    plw = np.zeros((128, 2, 128), np.float32)
    pw = np.asarray(pl_w[L], np.float32)
    for g in range(4):
        m, hh = g // 2, g % 2
        plw[hh * 64:(hh + 1) * 64, m, hh * 64:(hh + 1) * 64] = pw[g]
    return dict(wfm=np.ascontiguousarray(wfm), wtm=np.ascontiguousarray(wtm), rwp=np.ascontiguousarray(rwp),
                w2=np.asarray(rw_w2[L], np.float32), a2=np.asarray(rw_a2[L], np.float32), rowp=rowp,
                convw=np.ascontiguousarray(convw), plw=plw)


def const_tables(seg, NT):
    f = np.float32
    a = np.arange(128)
    ident = np.eye(128, dtype=f)
    mle = (a[:, None] <= a[None, :]).astype(f)
    mgt = (a[:, None] > a[None, :]).astype(f)
    a64 = np.arange(64)
    mG = np.concatenate([(a64[:, None] < a64[None, :]), (a64[:, None] <= a64[None, :])], axis=1).astype(f)
    gam = 1.0 - np.exp2(-5.0 - np.arange(4))
    lg = np.log(gam)
    rel = a[None, :] - a[:, None]
    dect = np.zeros((128, 4, 128), f)
    for h in range(4):
        dect[:, h, :] = np.where(rel >= 0, np.exp(lg[h] * np.maximum(rel, 0)), 0.0) * 0.125
    zx = np.zeros((128, 8), f)
    for h in range(4):
        zx[:, h] = np.exp(lg[h] * (127 - a)) * 0.125
        zx[:, 4 + h] = np.exp(lg[h] * (a + 1.0))
    gchunk = np.zeros((64, 4, 64), f)
    for h in range(4):
        gchunk[:, h, :] = np.exp(lg[h] * 128)
    pos = (seg * SEG + np.arange(NT * TT)).astype(np.float32)
    inv_freq = (np.float32(10000.0) ** (-np.arange(0, 64, 2, dtype=np.float32) / np.float32(64))).astype(np.float32)
    ang = pos[:, None] * inv_freq[None, :]
    c, s = np.cos(ang).astype(f), np.sin(ang).astype(f)
    cs2 = np.zeros((NT * TT, 2, 2, 32), f)
    cs2[:, 0, 0], cs2[:, 0, 1] = c, s
    cs2[:, 1, 0], cs2[:, 1, 1] = -s, c
    cnt0 = np.zeros((128, 2, 128), f)
    wins = (2, 4, 8, 16)
    for g in range(4):
        m, hh = g // 2, g % 2
        if seg == 0:
            cnt0[hh * 64:(hh + 1) * 64, m, :] = 1.0 / np.minimum(a + 1.0, float(wins[g]))[None, :]
        else:
            cnt0[hh * 64:(hh + 1) * 64, m, :] = 1.0 / wins[g]
    return dict(ident=ident, mask_le=mle, mask_gt=mgt, mask_g=mG, dect=dect, zetaxi=zx, gchunk=gchunk,
                cs2=np.ascontiguousarray(cs2.reshape(NT * TT, 128)), cnt0=cnt0)


def seg_consts(seg):
    f = np.float32
    gam = 1.0 - np.exp2(-5.0 - np.arange(4))
    gseg = np.zeros((64, 4, 64), f)
    for h in range(4):
        gseg[:, h, :] = np.exp(np.log(gam[h]) * SEG)
    msk = np.zeros((64, NSEG), f)
    msk[:, :seg] = 1.0
    return dict(gseg=gseg, seg_mask=msk)


SUMW = 1032
WNAMES = ("wfm", "wtm", "rwp", "w2", "a2", "rowp", "convw", "plw")
WSHAPES = dict(wfm=[D, FMC], wtm=[D, TMC], rwp=[64, 30], w2=[64, G], a2=[64, G], rowp=[4360], convw=[64, 2, 4, 4], plw=[128, 2, 128])
CSHAPES = lambda NT: dict(ident=[128, 128], mask_le=[128, 128], mask_gt=[128, 128], mask_g=[64, 128], dect=[128, 4, 128],
                          zetaxi=[128, 8], gchunk=[64, 4, 64], cs2=[NT * TT, 128], cnt0=[128, 2, 128])


def _declare(nc, NT, suffixes, NR, fused):
    def din(name, shape):
        return nc.dram_tensor(name, list(shape), F32, kind="ExternalInput").ap()
    dr = {}
    for k, shp in CSHAPES(NT).items():
        dr[k] = din(k, shp)
    dr["gseg"] = din("gseg", [64, 4, 64])
    dr["seg_mask"] = din("seg_mask", [64, NR])
    per = {}
    for sfx in suffixes:
        d = {}
        for k in WNAMES:
            d[k] = din(k + sfx, WSHAPES[k])
        d["w_out"] = din("w_out" + sfx, [D, D])
        d["ple_w"] = din("ple_w" + sfx, [256, D])
        d["ple_gate"] = din("ple_gate" + sfx, [D, D])
        d["p"] = din("p" + sfx, [NT * TT, 256])
        per[sfx] = d
    return dr, per


def build_fused(NT, NL=2):
    nc = bass.Bass("TRN2", target_bir_lowering=False)
    NR = 8
    groups = [list(range(NR))]
    dr, per = _declare(nc, NT, [f"_{L}" for L in range(NL)], NR, True)
    xh0 = nc.dram_tensor("xh", [HALO + NT * TT, D], F32, kind="ExternalInput").ap()
    pred_oh = nc.dram_tensor("pred_oh", [HALO, NR], F32, kind="ExternalInput").ap()
    xout = nc.dram_tensor("xout", [NT * TT, D], F32, kind="ExternalOutput").ap()
    x1 = nc.dram_tensor("x1_scratch", [HALO + NT * TT, D], F32, kind="Internal").ap()
    dr["t_mix"] = T(nc.dram_tensor("mix_scratch", [NT, 128, 1024], BF16, kind="Internal").ap(), "mix_scratch")
    dr["t_xTs"] = T(nc.dram_tensor("xT_scratch", [NT, 128, 1024], BF16, kind="Internal").ap(), "xT_scratch")
    dr["t_rp"] = T(nc.dram_tensor("h_rp", [NT, 64, 1024], BF16, kind="Internal").ap(), "h_rp")
    dr["t_yh"] = T(nc.dram_tensor("h_yh", [NT, 64, 512], F32, kind="Internal").ap(), "h_yh")
    dr["t_bv"] = T(nc.dram_tensor("h_bv", [NT, 64, 512], F32, kind="Internal").ap(), "h_bv")
    dr["t_ps"] = T(nc.dram_tensor("h_ps", [NT, 64, 512], F32, kind="Internal").ap(), "h_ps")
    dr["t_pl"] = T(nc.dram_tensor("h_pl", [NT, 64, 8], F32, kind="Internal").ap(), "h_pl")
    sum_src = [nc.dram_tensor(f"sum_src{L}", [64, SUMW], F32) for L in range(NL)]
    sum_all = [nc.dram_tensor(f"sum_all{L}", [NR * 64, SUMW], F32) for L in range(NL)]
    tail_src = nc.dram_tensor("tail_src", [HALO, D], F32)
    tail_all = nc.dram_tensor("tail_all", [NR * HALO, D], F32)
    with contextlib.ExitStack() as st:
        Pg = Prog(nc, st)
        S = Pg.S
        ident = S.sbuf([128, 128], F32, "ident")
        identb = S.sbuf([128, 128], BF16, "identb")
        Pg.load(None, ident[:], dr["ident"])
        Pg.cp("dve", identb[:], ident[:])
        for L in range(NL):
            d = dict(dr)
            d.update(per[f"_{L}"])
            d["xh"] = xh0 if L == 0 else x1
            d["o_sum"] = sum_src[L].ap()
            emit_pass(Pg, ident, identb, "P1", NT, d, NR)
            S.new_phase()
            tsrc, tdst = T(None, "ccsrc"), T(None, "ccdst")
            S.collective_allgather(tsrc, tdst, sum_src[L].ap().opt(), sum_all[L].ap().opt(), groups)
            S.barrier()
            S.emit()
            S.new_phase()
            d["sum_all"] = sum_all[L].ap()
            if L == 0 and NL > 1:
                d["xout"] = x1[HALO:, :]
                d["tail"] = tail_src.ap()
            else:
                d["xout"] = xout
                d["tail"] = None
            emit_pass(Pg, ident, identb, "P2", NT, d, NR)
            S.new_phase()
            if L == 0 and NL > 1:
                tsrc, tdst = T(None, "ccsrc"), T(None, "ccdst")
                S.collective_allgather(tsrc, tdst, tail_src.ap().opt(), tail_all.ap().opt(), groups)
                with contextlib.ExitStack() as stH:
                    S.stack = stH
                    S.barrier()
                    tl = S.sbuf([HALO, NR, D], F32, "tl")
                    oh = S.sbuf([HALO, NR], F32, "oh")
                    hl = S.sbuf([HALO, D], F32, "hl")
                    Pg.load(None, tl[:], tail_all.ap().rearrange("(j p) d -> p j d", p=HALO))
                    Pg.load(None, oh[:], pred_oh)
                    Pg.ts("dve", hl[:], tl[:, 0, :], oh[:, 0:1], None, ALU.mult)
                    for j in range(1, NR):
                        Pg.stt(hl[:], tl[:, j, :], oh[:, j:j + 1], hl[:], ALU.mult, ALU.add)
                    Pg.store(None, x1[0:HALO, :], hl[:])
                    S.barrier()
                    S.emit()
                S.stack = S.base
                S.new_phase()
    return nc


_PROGS = {}


def _prog(mode, NT):
    key = (mode, NT)
    if key not in _PROGS:
        _PROGS[key] = build_fused(NT, 2 if mode == "F" else 1)
    return _PROGS[key]


def _layer_inputs(L, args):
    (w_in, rw_mu, rw_w0, rw_w2, rw_a0, rw_a2, rw_kk, rw_ka, rw_rk, rw_ln_w, rw_ln_b, pl_w, pl_scale,
     ml_conv, ml_ib, ml_fb, ml_ln_w, w_out, ple_w, ple_gate, ln_g, ln_b) = args
    pk = pack_layer(L, w_in, rw_mu, rw_w0, rw_w2, rw_a0, rw_a2, rw_kk, rw_ka, rw_rk, rw_ln_w, rw_ln_b, pl_w, pl_scale,
                    ml_conv, ml_ib, ml_fb, ml_ln_w, ln_g, ln_b)
    pk["w_out"] = np.ascontiguousarray(np.asarray(w_out[L], np.float32))
    pk["ple_w"] = np.ascontiguousarray(np.asarray(ple_w[L], np.float32))
    pk["ple_gate"] = np.ascontiguousarray(np.asarray(ple_gate[L], np.float32))
    return pk


def _xh(cur, c):
    b, sg = c // NSEG, c % NSEG
    xh = np.zeros((HALO + SEG, D), np.float32)
    xh[HALO:] = cur[b, sg * SEG:(sg + 1) * SEG]
    if sg > 0:
        xh[:HALO] = cur[b, sg * SEG - HALO: sg * SEG]
    return xh


def _core_inputs(c, NT, pks, p, xcur, layers):
    b, sg = c // NSEG, c % NSEG
    ncores = 2 * NSEG
    im = {}
    im.update(const_tables(sg, NT))
    im["gseg"] = seg_consts(sg)["gseg"]
    msk = np.zeros((64, ncores), np.float32)
    msk[:, b * NSEG: c] = 1.0
    im["seg_mask"] = msk
    oh = np.zeros((HALO, ncores), np.float32)
    if sg > 0:
        oh[:, c - 1] = 1.0
    im["pred_oh"] = oh
    for i, L in enumerate(layers):
        for k, v in pks[L].items():
            im[f"{k}_{i}"] = v
        im[f"p_{i}"] = np.ascontiguousarray(p[L, b, sg * SEG:(sg + 1) * SEG])
    im["xh"] = _xh(xcur, c)
    return im


def kernel_two_launch(x, p, *args):
    x = np.asarray(x, np.float32)
    p = np.asarray(p, np.float32)
    NT = SEG // TT
    ncores = 2 * NSEG
    pks = [_layer_inputs(L, args) for L in range(2)]
    cur = x
    for L in range(2):
        ims = [_core_inputs(c, NT, pks, p, cur, [L]) for c in range(ncores)]
        res = run_bass_kernel_spmd(_prog("L", NT), ims, core_ids=list(range(ncores))).results
        nxt = np.empty_like(cur)
        for c in range(ncores):
            nxt[c // NSEG, (c % NSEG) * SEG:(c % NSEG + 1) * SEG] = res[c]["xout"]
        cur = nxt
    return cur


def kernel(x, p, w_in, rw_mu, rw_w0, rw_w2, rw_a0, rw_a2, rw_kk, rw_ka, rw_rk, rw_ln_w, rw_ln_b, pl_w, pl_scale,
           ml_conv, ml_ib, ml_fb, ml_ln_w, w_out, ple_w, ple_gate, ln_g, ln_b):
    args = (w_in, rw_mu, rw_w0, rw_w2, rw_a0, rw_a2, rw_kk, rw_ka, rw_rk, rw_ln_w, rw_ln_b, pl_w, pl_scale,
            ml_conv, ml_ib, ml_fb, ml_ln_w, w_out, ple_w, ple_gate, ln_g, ln_b)
    x = np.asarray(x, np.float32)
    p = np.asarray(p, np.float32)
    NT = SEG // TT
    ncores = 2 * NSEG
    pks = [_layer_inputs(L, args) for L in range(2)]
    ims = [_core_inputs(c, NT, pks, p, x, [0, 1]) for c in range(ncores)]
    res = run_bass_kernel_spmd(_prog("F", NT), ims, core_ids=list(range(ncores))).results
    out = np.empty_like(x)
    for c in range(ncores):
        out[c // NSEG, (c % NSEG) * SEG:(c % NSEG + 1) * SEG] = res[c]["xout"]
    return out
```

```python
import contextlib
import math
import numpy as np
import concourse.bass as bass
import concourse.mybir as mybir
from concourse.bass_utils import run_bass_kernel_spmd

F32 = mybir.dt.float32
BF16 = mybir.dt.bfloat16
AF = mybir.ActivationFunctionType
ALU = mybir.AluOpType
AX = mybir.AxisListType

D = 1024
G = 256
NSEG = 4
SEG = 4096
HALO = 16
TT = 128
FMC = 1408
TMC = 2568
C0 = math.exp(-0.5)
ALPHA = (2.0 * 2) ** 0.25
RW_EPS = 64e-5
HN_EPS = 1e-6
LN_EPS = 1e-5
SAME_ENGINE_SYNC = ("act", "dve", "pool")
PIPE = True
RDT = BF16


class Ref:
    __slots__ = ("ts", "ap")

    def __init__(self, ts, ap):
        self.ts = ts
        self.ap = ap

    def re(self, pat, **kw):
        return Ref(self.ts, self.ap.rearrange(pat, **kw))

    def bc(self, shape):
        return Ref(self.ts, self.ap.broadcast_to(list(shape)))

    def __getitem__(self, idx):
        return Ref(self.ts, self.ap[idx])


class T:
    __slots__ = ("h", "hb", "last_w", "readers", "name")

    def __init__(self, h, name="", hb=None):
        self.h = h
        self.hb = hb
        self.last_w = None
        self.readers = []
        self.name = name

    def __getitem__(self, idx):
        return Ref((self,), self.h[idx])

    def b16(self, idx):
        return Ref((self,), self.hb[idx])

    def view(self, ap):
        return View((self,), ap)


class View:
    __slots__ = ("ts", "ap")

    def __init__(self, ts, ap):
        self.ts = ts
        self.ap = ap

    def __getitem__(self, idx):
        return Ref(self.ts, self.ap[idx])


class Sched:
    ENG = ("pe", "act", "dve", "pool", "sp")

    def __init__(self, nc, stack):
        self.nc = nc
        self.base = stack
        self.stack = stack
        self.q = {e: [] for e in self.ENG}
        self.cnt = {}
        self.sem = {}
        self.waited = {e: {} for e in self.ENG}
        self.phase = 0
        self.ekey = {}
        for e in ("pe", "act", "dve", "pool"):
            k = f"{e}#0"
            self.ekey[e] = k
            self.sem[k] = stack.enter_context(nc.semaphore("sem_" + e + "_0"))
            self.cnt[k] = 0
        self.ndma_sem = 0
        self.nt = 0
        self.ninst = 0

    def sbuf(self, shape, dt, name=None):
        self.nt += 1
        name = f"{name or 't'}_{self.nt}"
        h = self.stack.enter_context(self.nc.sbuf_tensor(name, list(shape), dt))
        return T(h, name)

    def slot(self, name=None):
        self.nt += 1
        name = f"{name or 's'}_{self.nt}"
        h = self.stack.enter_context(self.nc.sbuf_tensor(name, [128, 512], F32))
        return T(h, name, hb=h.bitcast(BF16))

    def dma_sem(self):
        self.ndma_sem += 1
        key = f"dma{self.ndma_sem}"
        self.sem[key] = self.base.enter_context(self.nc.semaphore("sem_" + key))
        self.cnt[key] = 0
        return key

    def _need(self, eng, tok, waits):
        if tok is None:
            return
        key, val = tok
        if key.split("#")[0] == eng and eng not in SAME_ENGINE_SYNC:
            return
        if self.waited[eng].get(key, 0) >= val:
            return
        waits[key] = max(waits.get(key, 0), val)

    def _deps(self, eng, reads, writes):
        waits = {}
        for t in reads:
            self._need(eng, t.last_w, waits)
        for t in writes:
            self._need(eng, t.last_w, waits)
            for r in t.readers:
                self._need(eng, r, waits)
        for key, val in waits.items():
            self.waited[eng][key] = val
        return list(waits.items())

    def _record(self, tok, reads, writes):
        for t in writes:
            t.last_w = tok
            t.readers = []
        for t in reads:
            if t in writes:
                continue
            t.readers = [r for r in t.readers if r[0] != tok[0]] + [tok]

    def op(self, eng, fn, reads=(), writes=()):
        reads = list(dict.fromkeys(reads))
        writes = list(dict.fromkeys(writes))
        waits = self._deps(eng, reads, writes)
        ek = self.ekey[eng]
        self.cnt[ek] += 1
        self.ninst += 1
        tok = (ek, self.cnt[ek])
        sem = self.sem[ek]
        sems = self.sem

        def run(e, waits=waits, fn=fn, sem=sem):
            for key, val in waits:
                e.wait_ge(sems[key], val)
            fn(e).then_inc(sem, 1)
        self.q[eng].append(run)
        self._record(tok, reads, writes)
        return tok

    def dma(self, qeng, semkey, out_ap, in_ap, reads=(), writes=()):
        reads = list(dict.fromkeys(reads))
        writes = list(dict.fromkeys(writes))
        waits = self._deps(qeng, reads, writes)
        if semkey is None:
            if not hasattr(self, "pool_keys"):
                self.pool_keys = [self.dma_sem() for _ in range(16)]
                self.pool_i = 0
            semkey = self.pool_keys[self.pool_i % len(self.pool_keys)]
            self.pool_i += 1
            prev = self.cnt[semkey]
            if prev > 0 and self.waited[qeng].get(semkey, 0) < prev:
                waits = [w for w in waits if w[0] != semkey] + [(semkey, prev)]
                self.waited[qeng][semkey] = prev
        self.cnt[semkey] += 16
        tok = (semkey, self.cnt[semkey])
        sem = self.sem[semkey]
        sems = self.sem

        def run(e, waits=waits, sem=sem, out_ap=out_ap, in_ap=in_ap):
            for key, val in waits:
                e.wait_ge(sems[key], val)
            e.dma_start(out=out_ap, in_=in_ap).then_inc(sem, 16)
        self.q[qeng].append(run)
        self._record(tok, reads, writes)
        return tok

    def new_phase(self):
        self.barrier()
        self.phase += 1
        for e in ("pe", "act", "dve", "pool"):
            k = f"{e}#{self.phase}"
            self.ekey[e] = k
            self.sem[k] = self.base.enter_context(self.nc.semaphore(f"sem_{e}_{self.phase}"))
            self.cnt[k] = 0

    def collective_allgather(self, src_t, dst_t, src_ap, dst_ap, groups):
        key = self.dma_sem()
        waits = self._deps("pool", [src_t], [dst_t])
        self.cnt[key] += 1
        tok = (key, self.cnt[key])
        sems = self.sem

        def run(e, waits=waits, key=key):
            for k_, v_ in waits:
                e.wait_ge(sems[k_], v_)
            e.collective_compute("AllGather", mybir.AluOpType.bypass, replica_groups=groups,
                                 ins=[src_ap], outs=[dst_ap]).then_inc(sems[key])
        self.q["pool"].append(run)
        self._record(tok, [src_t], [dst_t])
        return tok

    def barrier(self):
        snap = dict(self.cnt)
        sems = self.sem
        for eng in self.ENG:
            waits = []
            for key, val in snap.items():
                if val > 0 and self.waited[eng].get(key, 0) < val:
                    waits.append((key, val))
                    self.waited[eng][key] = val

            def run(e, waits=waits):
                for key, val in waits:
                    e.wait_ge(sems[key], val)
            self.q[eng].append(run)

    def emit(self):
        nc = self.nc
        q = self.q
        self.q = {e: [] for e in self.ENG}
        with nc.Block() as block:
            @block.tensor
            def _(e):
                for f in q["pe"]:
                    f(e)

            @block.scalar
            def _(e):
                for f in q["act"]:
                    f(e)

            @block.vector
            def _(e):
                for f in q["dve"]:
                    f(e)

            @block.gpsimd
            def _(e):
                for f in q["pool"]:
                    f(e)

            @block.sync
            def _(e):
                for f in q["sp"]:
                    f(e)


def _ts(*refs):
    out = []
    for r in refs:
        if isinstance(r, Ref):
            out.extend(r.ts)
    return out


class Prog:
    def __init__(self, nc, stack):
        self.nc = nc
        self.S = Sched(nc, stack)
        ph = stack.enter_context(nc.psum_tensor("psum_all", [128, 8, 512], F32))
        phb = ph.bitcast(BF16)
        self.ph = ph
        self.banks = [T(ph, f"bank{j}", hb=phb) for j in range(8)]
        self.pool = list(range(8))
        self.pool_pos = {}

    def set_pool(self, pool):
        self.pool = pool

    def stream_sems(self):
        if not hasattr(self, "_ss"):
            S = self.S
            two = lambda: [S.dma_sem(), S.dma_sem()]
            self._ss = (two(), two(), two(), two(), two(), two(), S.dma_sem(), S.dma_sem())
        return self._ss

    def handoff_sems(self):
        if not hasattr(self, "_hs"):
            S = self.S
            self._hs = [[S.dma_sem(), S.dma_sem()] for _ in range(5)]
        return self._hs

    def bank(self):
        key = tuple(self.pool)
        i = self.pool_pos.get(key, 0)
        self.pool_pos[key] = i + 1
        j = self.pool[i % len(self.pool)]
        return j, self.banks[j]

    def bank2(self):
        key = tuple(self.pool)
        i = self.pool_pos.get(key, 0)
        if i % 2:
            i += 1
        self.pool_pos[key] = i + 2
        j = self.pool[i % len(self.pool)]
        return j, View((self.banks[j], self.banks[j + 1]), self.ph[:, j:j + 2, :])

    def mm(self, out, lhsT, rhs, start=True, stop=True):
        self.S.op("pe", lambda e: e.matmul(out.ap, lhsT=lhsT.ap, rhs=rhs.ap, start=start, stop=stop),
                  reads=_ts(lhsT, rhs), writes=_ts(out))

    def tr(self, out, in_, ident):
        self.S.op("pe", lambda e: e.transpose(out=out.ap, in_=in_.ap, identity=ident.ap),
                  reads=_ts(in_, ident), writes=_ts(out))

    def act(self, out, in_, func, bias=None, scale=None):
        kw = {}
        if bias is not None:
            kw["bias"] = bias.ap if isinstance(bias, Ref) else bias
        if scale is not None:
            kw["scale"] = scale.ap if isinstance(scale, Ref) else scale
        self.S.op("act", lambda e: e.activation(out=out.ap, in_=in_.ap, func=func, **kw),
                  reads=_ts(in_, bias, scale), writes=_ts(out))

    def cp(self, eng, out, in_):
        if eng == "act":
            self.S.op("act", lambda e: e.copy(out=out.ap, in_=in_.ap), reads=_ts(in_), writes=_ts(out))
        else:
            self.S.op(eng, lambda e: e.tensor_copy(out=out.ap, in_=in_.ap), reads=_ts(in_), writes=_ts(out))

    def tt(self, eng, out, a, b, op):
        self.S.op(eng, lambda e: e.tensor_tensor(out=out.ap, in0=a.ap, in1=b.ap, op=op),
                  reads=_ts(a, b), writes=_ts(out))

    def ts(self, eng, out, a, s1, s2, op0, op1=None):
        v1 = s1.ap if isinstance(s1, Ref) else s1
        v2 = s2.ap if isinstance(s2, Ref) else s2
        if op1 is None:
            self.S.op(eng, lambda e: e.tensor_scalar(out=out.ap, in0=a.ap, scalar1=v1, scalar2=None, op0=op0),
                      reads=_ts(a, s1), writes=_ts(out))
        else:
            self.S.op(eng, lambda e: e.tensor_scalar(out=out.ap, in0=a.ap, scalar1=v1, scalar2=v2, op0=op0, op1=op1),
                      reads=_ts(a, s1, s2), writes=_ts(out))

    def stt(self, out, a, s, b, op0, op1):
        v = s.ap if isinstance(s, Ref) else s
        self.S.op("dve", lambda e: e.scalar_tensor_tensor(out=out.ap, in0=a.ap, scalar=v, in1=b.ap, op0=op0, op1=op1),
                  reads=_ts(a, s, b), writes=_ts(out))

    def red(self, out, in_, op=ALU.add):
        self.S.op("dve", lambda e: e.tensor_reduce(out=out.ap, in_=in_.ap, axis=AX.X, op=op),
                  reads=_ts(in_), writes=_ts(out))

    def scan_add(self, out, ones, data):
        self.S.op("dve", lambda e: e.tensor_tensor_scan(out=out.ap, data0=ones.ap, data1=data.ap, initial=0.0,
                                                        op0=ALU.mult, op1=ALU.add),
                  reads=_ts(ones, data), writes=_ts(out))

    def bn_stats(self, out, in_):
        self.S.op("dve", lambda e: e.bn_stats(out=out.ap, in_=in_.ap), reads=_ts(in_), writes=_ts(out))

    def bn_aggr(self, out, in_):
        self.S.op("dve", lambda e: e.bn_aggr(out=out.ap, in_=in_.ap), reads=_ts(in_), writes=_ts(out))

    def recip(self, out, in_):
        self.S.op("dve", lambda e: e.reciprocal(out=out.ap, in_=in_.ap), reads=_ts(in_), writes=_ts(out))

    def memset(self, eng, out, val):
        self.S.op(eng, lambda e: e.memset(out.ap, val), reads=[], writes=_ts(out))

    def load(self, semkey, out, src, q="sp"):
        if isinstance(src, Ref):
            return self.S.dma(q, semkey, out.ap, src.ap, reads=_ts(src), writes=_ts(out))
        return self.S.dma(q, semkey, out.ap, src, writes=_ts(out))

    def store(self, semkey, dst, in_, q="sp"):
        if isinstance(dst, Ref):
            return self.S.dma(q, semkey, dst.ap, in_.ap, reads=_ts(in_), writes=_ts(dst))
        return self.S.dma(q, semkey, dst, in_.ap, reads=_ts(in_))


def emit_pass(Pg, ident, identb, mode, NT, dr, NR):
    P2 = mode == "P2"
    RW = not P2
    nc = Pg.nc
    t_rp, t_yh, t_bv, t_ps, t_pl = dr["t_rp"], dr["t_yh"], dr["t_bv"], dr["t_ps"], dr["t_pl"]
    NTOK = NT * TT
    d_xh = dr["xh"]
    d_wfm, d_wtm, d_rwp, d_w2, d_a2, d_row = dr["wfm"], dr["wtm"], dr["rwp"], dr["w2"], dr["a2"], dr["rowp"]
    d_conv, d_plw = dr["convw"], dr["plw"]
    d_mle, d_mgt, d_mG, d_dec, d_zx, d_gb = dr["mask_le"], dr["mask_gt"], dr["mask_g"], dr["dect"], dr["zetaxi"], dr["gchunk"]
    d_cs, d_cnt0 = dr["cs2"], dr["cnt0"]
    if P2:
        d_sall, d_msk, d_gseg = dr["sum_all"], dr["seg_mask"], dr["gseg"]
        d_p, d_wout, d_plew, d_pleg, d_out = dr["p"], dr["w_out"], dr["ple_w"], dr["ple_gate"], dr["xout"]
        t_mix, t_xTs = dr["t_mix"], dr["t_xTs"]
    else:
        d_osum = dr["o_sum"]
    R_MUV, R_LNW, R_LNB, R_PLS, R_MLLN, R_LNG, R_LNBB, R_IB, R_OMUV = 0, 256, 512, 768, 1024, 1280, 2304, 3328, 3336
    out_toks = []
    S = Pg.S
    mm, tr, act, cp, tt, ts, stt, red = Pg.mm, Pg.tr, Pg.act, Pg.cp, Pg.tt, Pg.ts, Pg.stt, Pg.red
    sb = S.sbuf
    ld_x, ld_p, ld_m, ld_t, ld_cs, st_o, st_sx, st_sm = Pg.stream_sems()
    ld_h = Pg.handoff_sems()
    if True:
        with contextlib.ExitStack() as stM:
            S.stack = stM
            mle = sb([128, 128], F32, "mle")
            mgt = sb([128, 128], F32, "mgt")
            mG = sb([64, 128], F32, "mG")
            dect = sb([128, 4, 128], F32, "dect")
            zx = sb([128, 8], F32, "zx")
            gb = sb([64, 4, 64], F32, "gb")
            cnt0 = sb([128, 2, 128], F32, "cnt0")
            rwp = sb([64, 48], F32, "rwp")
            w2 = sb([64, G], F32, "w2")
            a2 = sb([64, G], F32, "a2")
            rowp = sb([128, 1280 + 8 + 256 + 8], F32, "rowp")
            convw = sb([64, 2, 4, 4], F32, "convw")
            plwb = sb([128, 2, 128], BF16, "plwb")
            ones = sb([128, 128], F32, "ones")
            sm = sb([128, 96], F32, "sm")
            sm2 = sb([64, 32], F32, "sm2")
            RM_IB, RM_OMUV = 1280, 1288
            Pg.load(None, mle[:], d_mle)
            Pg.load(None, mgt[:], d_mgt)
            Pg.load(None, mG[:], d_mG)
            Pg.load(None, dect[:], d_dec)
            Pg.load(None, zx[:], d_zx)
            Pg.load(None, gb[:], d_gb)
            Pg.load(None, cnt0[:], d_cnt0)
            Pg.load(None, rwp[:, 0:30], d_rwp)
            Pg.load(None, w2[:], d_w2)
            Pg.load(None, a2[:], d_a2)
            Pg.load(None, rowp[:, 0:1280], d_row[0:1280].partition_broadcast(128))
            Pg.load(None, rowp[:, RM_IB:RM_IB + 8], d_row[R_IB:R_IB + 8].partition_broadcast(128))
            Pg.load(None, convw[:], d_conv)
            Pg.memset("dve", ones[:], 1.0)
            ts("dve", rwp[:, 30:40], rwp[:, 0:10], -1.0, 1.0, ALU.mult, ALU.add)
            ts("dve", rwp[:, 40:44], rwp[:, 22:26], -1.0, 1.0, ALU.mult, ALU.add)
            ts("dve", rowp[:, RM_OMUV:RM_OMUV + 256], rowp[:, R_MUV:R_MUV + 256], -1.0, 1.0, ALU.mult, ALU.add)
            ts("dve", rowp[:, RM_IB:RM_IB + 4], rowp[:, RM_IB:RM_IB + 4], math.log(0.125), None, ALU.add)

            NSL = 47
            A = [S.slot(f"A{i}") for i in range(NSL)]

            def f64(sl, h=4):
                return sl.view(sl.h[0:64, :].rearrange("p (h t) -> p h t", h=h))

            def b64(sl, lo, n, c=8):
                hh = sl.hb if RDT == BF16 else sl.h
                return sl.view(hh[0:64, lo:lo + n].rearrange("p (c d) -> p c d", c=c))
            zs_r = f64(A[0]); rk = zs_r
            zs_k = f64(A[1]); kmod = zs_k
            sig = f64(A[2]); esc = sig; kq = sig; kk = sig
            aa = f64(A[3]); bbv = aa
            gsc = f64(A[4]); cumI = gsc; pinv = gsc
            tA = f64(A[5]); cE = tA; pprev = tA
            tB = f64(A[6]); dl = tB; pLp = tB
            pp = f64(A[7])
            tC = f64(A[8])
            zs_wa = A[9].view(A[9].h[0:64, 0:256].rearrange("p (h t) -> p h t", h=2))
            th = A[9].view(A[9].h[0:64, 256:384])
            Vc = A[10].view(A[10].h[0:64, :].rearrange("p (c n) -> p c n", c=2))
            tV = A[11].view(A[11].h[0:64, :].rearrange("p (c n) -> p c n", c=2)); grw = tV
            if RDT == BF16:
                AR = A[12].view(A[12].hb[0:64, :].rearrange("p (c w l) -> p c w l", c=8, w=2))
                Bt, Kt = b64(A[13], 0, 512), b64(A[13], 512, 512)
                Bpf, Kpf = b64(A[14], 0, 512), b64(A[14], 512, 512)
                Bptm, Kptm = b64(A[15], 0, 512), b64(A[15], 512, 512)
                Gb = b64(A[16], 0, 1024)
                Gk = b64(A[17], 0, 1024)
                Pn = [b64(A[18], 0, 512), b64(A[18], 512, 512)]
                Ptb = [b64(A[19], 0, 512), b64(A[19], 512, 512)]
                X = [b64(A[20], 0, 1024), b64(A[21], 0, 1024)]
                Phi, Rhat = b64(A[22], 0, 512), b64(A[22], 512, 512)
                Vcr = A[24].view(A[24].hb[0:64, 0:512].rearrange("p (c n) -> p c n", c=2))
            else:
                raise NotImplementedError
            Yrw = A[23].view(A[23].h[0:64, :].rearrange("p (c d) -> p c d", c=8))
            mixa = A[24].view(A[24].hb[0:64, 512:1024].rearrange("p (c n) -> p c n", c=2))
            BVt = A[28].view(A[28].h[0:64, :].rearrange("p (c d) -> p c d", c=8))
            psi = A[29].view(A[29].h[0:64, :].rearrange("p (c n) -> p c n", c=2))
            pLt = sm2.view(sm2.h[0:64, 8:16])
            RPl = [A[0].view(A[0].hb[0:64, :]), A[7].view(A[7].hb[0:64, :])]
            YHl = [A[9].view(A[9].h[0:64, :]), A[10].view(A[10].h[0:64, :])]
            BVl = [A[12].view(A[12].h[0:64, :]), A[15].view(A[15].h[0:64, :])]
            PSl = [A[36].view(A[36].h[0:64, :]), A[37].view(A[37].h[0:64, :])]
            PLl = [sm2.view(sm2.h[0:64, 16:24]), sm2.view(sm2.h[0:64, 24:32])]
            IFB = [(AR, Bt, Kt, Bpf, Kpf, Vc, Vcr, pp, zs_r)]
            if RW:
                IFB.append((A[38].view(A[38].hb[0:64, :].rearrange("p (c w l) -> p c w l", c=8, w=2)),
                            b64(A[39], 0, 512), b64(A[39], 512, 512), b64(A[40], 0, 512), b64(A[40], 512, 512),
                            A[41].view(A[41].h[0:64, :].rearrange("p (c n) -> p c n", c=2)),
                            A[42].view(A[42].hb[0:64, 0:512].rearrange("p (c n) -> p c n", c=2)),
                            f64(A[43]), f64(A[44])))
            else:
                IFB.append(IFB[0])
            if RW:
                cq, ck, ctmp = tA, f64(A[32]), f64(A[34])
            else:
                cq, ck, ctmp = tA, tB, tC
            qT = A[25].view(A[25].hb[0:64, 0:512].rearrange("p (h t) -> p h t", h=4))
            kT = A[25].view(A[25].hb[0:64, 512:1024].rearrange("p (h t) -> p h t", h=4))
            f128 = lambda sl: sl.view(sl.h[:, :].rearrange("p (h t) -> p h t", h=4))
            Ah, Eh, Em = (f128(A[35]), f128(A[36]), f128(A[4])) if RW else (f128(A[2]), f128(A[3]), f128(A[4]))
            PTm = A[26].view(A[26].hb[:, 0:512].rearrange("p (h t) -> p h t", h=4))
            ktw = A[26].view(A[26].hb[:, 512:768].rearrange("p (h d) -> p h d", h=4))
            Vaug = A[27].view(A[27].hb[:, 0:260].rearrange("p (h d) -> p h d", h=4))
            vrt = A[46].view(A[46].hb[:, 0:256].rearrange("p (h d) -> p h d", h=4))
            vz = A[46].view(A[46].hb[:, 256:512].rearrange("p (h d) -> p h d", h=4))
            numt = A[28].view(A[28].h[:, 0:260].rearrange("p (h d) -> p h d", h=4))
            numi = A[29].view(A[29].h[:, 0:260].rearrange("p (h d) -> p h d", h=4))
            osg = A[30].view(A[30].h[:, 0:256])
            Yml = A[30].view(A[30].h[:, 256:512].rearrange("p (h d) -> p h d", h=4))
            ct2s = A[37] if RW else A[1]
            ctmp2 = ct2s.view(ct2s.h[0:64, 0:260].rearrange("p (h d) -> p h d", h=4))
            rtmp = A[45].view(A[45].h[0:64, 0:256].rearrange("p (h d) -> p h d", h=4))
            rAs, rBs = (A[33], A[30]) if RW else (A[13], A[14])
            rA = rAs.view(rAs.h[:, :].rearrange("p (q w i) -> p q w i", q=8, w=2))
            rB = rBs.view(rBs.h[:, :].rearrange("p (q w i) -> p q w i", q=8, w=2))
            qkr = A[31].view(A[31].hb[:, 0:512].rearrange("p (q d) -> p q d", q=8))
            PTr = A[31].view(A[31].hb[:, 512:1024].rearrange("p (h t) -> p h t", h=4))
            qkT = A[32].view(A[32].hb[0:64, :].rearrange("p (q t) -> p q t", q=8))
            yint = A[33].view(A[33].h[:, 0:256].rearrange("p (h d) -> p h d", h=4))
            Yrt = A[33].view(A[33].h[:, 256:512].rearrange("p (h d) -> p h d", h=4))
            s2 = A[16].view(A[16].h[:, 0:288].rearrange("p (m t) -> p m t", m=2))
            s4 = A[17].view(A[17].h[:, 0:288].rearrange("p (m t) -> p m t", m=2))
            s8 = A[18].view(A[18].h[:, 0:144])
            s16 = A[18].view(A[18].h[:, 144:288])
            pm = A[19].view(A[19].h[:, 0:256].rearrange("p (m t) -> p m t", m=2))
            pd = A[19].view(A[19].hb[:, 512:768].rearrange("p (m t) -> p m t", m=2))
            g2 = A[20].view(A[20].h[:, :])
            g3 = A[21].view(A[21].h[:, 0:256])
            hbig = A[22].view(A[22].h[:, :].rearrange("p (g d) -> p g d", g=8))
            mixb = A[34].view(A[34].hb[:, 0:768])
            mixT = A[35].view(A[35].hb[:, :].rearrange("p (k t) -> p k t", k=8))
            smv = lambda lo, n, P_=128: sm.view(sm.h[0:P_, lo:lo + n])
            hsum, hsq, hm, hv = smv(0, 8), smv(8, 8), smv(16, 8), smv(24, 8)
            gat, nlf, ibs, ebl = smv(32, 8), smv(40, 4), smv(44, 4), smv(48, 4)
            etot, dden, bon = smv(52, 4, 64), smv(56, 4), sm2.view(sm2.h[0:64, 0:8])
            NBT = smv(68, 4, 64)

            wfm = sb([128, 8, FMC], BF16, "wfm")
            wtm = sb([128, 8, TMC], BF16, "wtm")
            cast_engs = ["dve", "pool", "act"]
            si = 0
            for dst, dram, ncols in ((wfm, d_wfm, FMC), (wtm, d_wtm, TMC)):
                for k in range(8):
                    for c0 in range(0, ncols, 512):
                        c1 = min(ncols, c0 + 512)
                        sl = A[si % 12 + 20] if si % 12 + 20 < NSL else A[si % 12]
                        Pg.load(None, sl[:, 0:c1 - c0], dram[k * 128:(k + 1) * 128, c0:c1])
                        cp(cast_engs[si % 3], dst[:, k, c0:c1], sl[:, 0:c1 - c0])
                        si += 1
            sl = A[0]
            Pg.load(None, sl[:, 0:256], d_plw.rearrange("p m n -> p (m n)"))
            cp("dve", plwb[:].re("p m n -> p (m n)"), sl[:, 0:256])

            xt = sb([128, D], F32, "xt")
            xTb = sb([128, 8, 144], BF16, "xTb")
            Zr = sb([64, 4, 144], F32, "Zr")
            Zk = sb([64, 4, 144], F32, "Zk")
            Zwa = sb([64, 2, 144], F32, "Zwa")
            Zu = sb([128, 2, 144], F32, "Zu")
            Zq = sb([64, 4, 144], F32, "Zq")
            Zkm = sb([64, 4, 144], F32, "Zkm")
            SW = 64 if P2 else 128
            S32 = sb([64, 4, SW], F32, "S32")
            Sr = sb([64, 4, SW], RDT, "Sr")
            C32 = sb([64, 4, 65], F32, "C32")
            Cb = sb([64, 4, 65], BF16, "Cb")
            R32 = sb([64, 4, 64], F32, "R32")
            Rb = sb([64, 4, 64], BF16, "Rb")
            cst = [sb([128, 128], F32, "cs0"), sb([128, 128], F32, "cs1")]
            Pg.memset("dve", S32[:], 0.0)
            Pg.memset("dve", C32[:], 0.0)
            Pg.memset("dve", R32[:], 0.0)
            if not P2:
                for h in range(4):
                    cp("pool", S32[:, h, 64:128], ident[0:64, 0:64])
            else:
                msk = sb([64, NR], F32, "msk")
                gseg = sb([64, 4, 64], F32, "gseg")
                Pg.load(None, msk[:], d_msk)
                Pg.load(None, gseg[:], d_gseg)
                c_sg = A[0].view(A[0].h[0:64, :].rearrange("p (h n) -> p h n", h=4))
                c_ml = A[1].view(A[1].h[0:64, 0:260].rearrange("p (h n) -> p h n", h=4))
                c_bt = A[1].view(A[1].h[0:64, 300:304])
                c_rt = A[2].view(A[2].h[0:64, 0:256].rearrange("p (h n) -> p h n", h=4))
                c_gt = A[3].view(A[3].h[0:64, 0:256].rearrange("p (h n) -> p h n", h=4))
                c_cd = A[4].view(A[4].h[0:64, 0:260].rearrange("p (h n) -> p h n", h=4))
                for js in range(NR - 1):
                    rs = slice(js * 64, (js + 1) * 64)
                    Pg.load(None, c_sg[:], d_sall[rs, 0:512].rearrange("p (h n) -> p h n", h=4))
                    Pg.load(None, c_ml[:], d_sall[rs, 512:772].rearrange("p (h n) -> p h n", h=4))
                    Pg.load(None, c_bt[:], d_sall[rs, 772:776])
                    Pg.load(None, c_rt[:], d_sall[rs, 776:1032].rearrange("p (h n) -> p h n", h=4))
                    mj = msk[:, js:js + 1]
                    j, bk = Pg.bank()
                    for h in range(4):
                        tr(bk[0:64, j, h * 64:(h + 1) * 64], c_sg[:, h, 64:128], ident[0:64, 0:64])
                    cp("act", c_gt[:], bk[0:64, j, 0:256].re("p (h n) -> p h n", h=4))
                    j, bk = Pg.bank()
                    for h in range(4):
                        mm(bk[0:64, j, h * 64:(h + 1) * 64], c_gt[:, h, :], S32[:, h, :])
                    tt("dve", c_cd[:, :, 0:64], bk[0:64, j, 0:256].re("p (h n) -> p h n", h=4), c_sg[:, :, 0:64], ALU.add)
                    tt("dve", c_cd[:, :, 0:64], c_cd[:, :, 0:64], S32[:], ALU.subtract)
                    stt(S32[:], c_cd[:, :, 0:64], mj, S32[:], ALU.mult, ALU.add)
                    act(c_bt[:], c_bt[:], AF.Exp, scale=-1.0)
                    tt("dve", c_cd[:], C32[:], c_bt[:].re("p (h o) -> p h o", o=1).bc([64, 4, 65]), ALU.mult)
                    tt("dve", c_cd[:], c_cd[:], c_ml[:], ALU.add)
                    tt("dve", c_cd[:], c_cd[:], C32[:], ALU.subtract)
                    stt(C32[:], c_cd[:], mj, C32[:], ALU.mult, ALU.add)
                    tt("dve", c_cd[:, :, 0:64], R32[:], gseg[:], ALU.mult)
                    tt("dve", c_cd[:, :, 0:64], c_cd[:, :, 0:64], c_rt[:], ALU.add)
                    tt("dve", c_cd[:, :, 0:64], c_cd[:, :, 0:64], R32[:], ALU.subtract)
                    stt(R32[:], c_cd[:, :, 0:64], mj, R32[:], ALU.mult, ALU.add)
            cp("dve", Sr[:], S32[:])
            cp("dve", Cb[:], C32[:])
            cp("dve", Rb[:], R32[:])
            Pg.memset("dve", NBT[:], 0.0)

            FM_TILES = [("r", Zr, 0), ("k", Zk, 256), ("wa", Zwa, 512), ("u", Zu, 640), ("q", Zq, 896), ("km", Zkm, 1152)]
            if not P2:
                FM_TILES = [f for f in FM_TILES if f[0] in ("r", "k", "wa", "km")]
            else:
                FM_TILES = [f for f in FM_TILES if f[0] in ("u", "q", "km")]

            def proj_fm(c_lo, c_hi, d_lo):
                n = c_hi - c_lo
                for name, Z, off in FM_TILES:
                    j, bk = Pg.bank()
                    if name == "u":
                        for m in range(2):
                            for k in range(8):
                                mm(bk[:, j, m * 128:m * 128 + n], wfm[:, k, off + m * 128: off + (m + 1) * 128], xTb[:, k, c_lo:c_hi],
                                   start=(k == 0), stop=(k == 7))
                        cp("act", Z[:, :, d_lo:d_lo + n], bk[:, j, 0:256].re("p (m t) -> p m t", m=2)[:, :, 0:n])
                    else:
                        nm = 2 if name == "wa" else 4
                        for m in range(nm):
                            for k in range(8):
                                mm(bk[0:64, j, m * 128:m * 128 + n], wfm[:, k, off + m * 64: off + (m + 1) * 64], xTb[:, k, c_lo:c_hi],
                                   start=(k == 0), stop=(k == 7))
                        cp("act", Z[:, :, d_lo:d_lo + n], bk[0:64, j, 0:nm * 128].re("p (m t) -> p m t", m=nm)[:, :, 0:n])

            Pg.load(ld_x[0], xt[0:16, :], d_xh[0:HALO, :])
            j, bk = Pg.bank()
            for k in range(8):
                tr(bk[:, j, k * 16:(k + 1) * 16], xt[0:16, k * 128:(k + 1) * 128], ident[0:16, 0:16])
            cp("act", xTb[:, :, 128:144], bk[:, j, 0:128].re("p (k t) -> p k t", k=8))
            proj_fm(128, 144, 128)
            Pg.memset("dve", Vaug[:], 1.0)
            if not P2:
                Pg.memset("pool", AR[:, :, 1, :], 0.0)

            def b4(col, w=128):
                return rwp[:, col:col + 4].re("p (h o) -> p h o", o=1).bc([64, 4, w])

            def b2(col, w=128):
                return rwp[:, col:col + 2].re("p (h o) -> p h o", o=1).bc([64, 2, w])

            def v8(t_):
                return t_[:].re("p h (c l) -> p (h c) l", c=2)

            def tmproj(col_lo, ncols):
                j, bk = Pg.bank()
                for k in range(8):
                    mm(bk[:, j, 0:ncols], xTb[:, k, 16:144], wtm[:, k, col_lo:col_lo + ncols], start=(k == 0), stop=(k == 7))
                return j, bk

            def headnorm(Y, P_, ng, eps):
                red(hsum[0:P_, 0:ng], Y)
                tt("pool", hbig[0:P_, 0:ng, :], Y, Y, ALU.mult)
                red(hsq[0:P_, 0:ng], hbig[0:P_, 0:ng, :])
                ts("dve", hm[0:P_, 0:ng], hsum[0:P_, 0:ng], 1.0 / 64, None, ALU.mult)
                tt("dve", hv[0:P_, 0:ng], hm[0:P_, 0:ng], hm[0:P_, 0:ng], ALU.mult)
                stt(hv[0:P_, 0:ng], hsq[0:P_, 0:ng], 1.0 / 64, hv[0:P_, 0:ng], ALU.mult, ALU.subtract)
                ts("dve", hv[0:P_, 0:ng], hv[0:P_, 0:ng], eps, None, ALU.add)
                act(hv[0:P_, 0:ng], hv[0:P_, 0:ng], AF.Ln)
                act(hv[0:P_, 0:ng], hv[0:P_, 0:ng], AF.Exp, scale=-0.5)
                tt("dve", Y, Y, hm[0:P_, 0:ng].re("p (g o) -> p g o", o=1).bc([P_, ng, 64]), ALU.subtract)
                tt("dve", Y, Y, hv[0:P_, 0:ng].re("p (g o) -> p g o", o=1).bc([P_, ng, 64]), ALU.mult)

            def tile_body(ti, part):
                AR, Bt, Kt, Bpf, Kpf, Vc, Vcr, pp, zs_r = IFB[ti % 2]
                rk = zs_r
                if part in ("all", "A0"):
                    cp("pool", xTb[:, :, 0:16], xTb[:, :, 128:144])
                    yield
                    for name, Z, off in FM_TILES:
                        cp("pool", Z[:, :, 0:16], Z[:, :, 128:144])
                    yield
                    Pg.load(ld_x[0], xt[:], d_xh[HALO + ti * TT: HALO + (ti + 1) * TT, :])
                    Pg.load(ld_cs[ti % 2], cst[ti % 2][:], d_cs[ti * TT:(ti + 1) * TT, :])
                    yield
                    j, b2v = Pg.bank2()
                    for k in range(8):
                        tr(b2v[:, k // 4, (k % 4) * 128:(k % 4 + 1) * 128], xt[:, k * 128:(k + 1) * 128], ident[:])
                    cp("act", xTb[:, 0:4, 16:144], b2v[:, 0, :].re("p (k t) -> p k t", k=4))
                    yield
                    cp("dve", xTb[:, 4:8, 16:144], b2v[:, 1, :].re("p (k t) -> p k t", k=4))
                    yield
                    if P2:
                        Pg.store(st_sx, t_xTs[ti].re("p (k t) -> p k t", k=8), xTb[:, :, 16:144])
                    proj_fm(16, 144, 16)

                    yield
                    if RW:
                        jv, bv = Pg.bank2()
                        for c in range(2):
                            for w_, sh in ((0, 16), (1, 15)):
                                for k in range(8):
                                    mm(bv[0:64, c, w_ * 256:(w_ + 1) * 256], xTb[:, k, sh + c * 64: sh + (c + 1) * 64], wtm[:, k, 0:256],
                                       start=(k == 0), stop=(k == 7))
                        bvv = bv[0:64, :, :].re("p c (w n) -> p c w n", w=2)
                        tt("dve", tV[:], bvv[:, :, 1, :], rowp[0:64, R_MUV:R_MUV + 256].re("p (o n) -> p o n", o=1).bc([64, 2, 256]), ALU.mult)
                        yield
                        tt("dve", Vc[:], bvv[:, :, 0, :], rowp[0:64, RM_OMUV:RM_OMUV + 256].re("p (o n) -> p o n", o=1).bc([64, 2, 256]), ALU.mult)
                        yield
                        tt("dve", Vc[:], Vc[:], tV[:], ALU.add)
                        yield
                        cp("pool", Vcr[:], Vc[:])
                        yield
                    if P2:
                        jg, bg = Pg.bank()
                        for c in range(2):
                            for k in range(8):
                                mm(bg[0:64, jg, c * 256:(c + 1) * 256], xTb[:, k, 16 + c * 64: 16 + (c + 1) * 64], wtm[:, k, 256:512],
                                   start=(k == 0), stop=(k == 7))
                        act(grw[:], bg[0:64, jg, :].re("p (c n) -> p c n", c=2), AF.Silu)
                        yield

                    yield
                if part in ("all", "A1"):
                    if RW:
                        if True:
                            tt("dve", tA[:], Zr[:, :, 16:144], b4(30), ALU.mult)
                            yield
                            tt("pool", tB[:], Zr[:, :, 15:143], b4(0), ALU.mult)
                            yield
                            tt("dve", zs_r[:], tA[:], tB[:], ALU.add)
                            yield
                        tt("dve", tA[:], Zk[:, :, 16:144], b4(34), ALU.mult)
                        yield
                        tt("pool", tB[:], Zk[:, :, 15:143], b4(4), ALU.mult)
                        yield
                        tt("dve", zs_k[:], tA[:], tB[:], ALU.add)
                        yield
                        tt("dve", tA[:, 0:2, :], Zwa[:, :, 16:144], b2(38), ALU.mult)
                        yield
                        tt("pool", tB[:, 0:2, :], Zwa[:, :, 15:143], b2(8), ALU.mult)
                        yield
                        tt("dve", zs_wa[:], tA[:, 0:2, :], tB[:, 0:2, :], ALU.add)
                        yield
                        act(th[:], zs_wa[:, 0, :], AF.Tanh)
                        yield
                        j, bk = Pg.bank()
                        for h in range(4):
                            mm(bk[0:64, j, h * 128:(h + 1) * 128], w2[:, h * 64:(h + 1) * 64], th[:])
                        tt("dve", tA[:], bk[0:64, j, :].re("p (h t) -> p h t", h=4), b4(10), ALU.add)
                        yield
                        act(sig[:], tA[:], AF.Sigmoid)
                        yield
                        j, bk = Pg.bank()
                        for h in range(4):
                            mm(bk[0:64, j, h * 128:(h + 1) * 128], a2[:, h * 64:(h + 1) * 64], zs_wa[:, 1, :])
                        tt("dve", tB[:], bk[0:64, j, :].re("p (h t) -> p h t", h=4), b4(14), ALU.add)
                        yield
                        act(aa[:], tB[:], AF.Sigmoid)
                        yield
                        Pg.scan_add(gsc[:].re("p h t -> p (h t)"), ones[0:64, 0:1].bc([64, 512]), sig[:].re("p h t -> p (h t)"))
                        yield
                        tt("dve", esc[:], gsc[:], sig[:], ALU.subtract)
                        yield
                        base8 = v8(esc)[:, :, 0:1].bc([64, 8, 64])
                        tt("dve", v8(cE), v8(esc), base8, ALU.subtract)
                        yield
                        tt("dve", v8(cumI), v8(gsc), base8, ALU.subtract)
                        yield
                        tt("dve", v8(dl), v8(cumI), v8(cumI)[:, :, 63:64].bc([64, 8, 64]), ALU.subtract)
                        yield
                        act(pp[:], cumI[:], AF.Exp, scale=-C0)
                        yield
                        act(pinv[:], cumI[:], AF.Exp, scale=C0)
                        yield
                        act(pprev[:], cE[:], AF.Exp, scale=-C0)
                        yield
                        act(pLp[:], dl[:], AF.Exp, scale=C0)
                        yield
                        tt("dve", kq[:], zs_k[:], b4(18), ALU.mult)
                        yield
                        tt("pool", tC[:], kq[:], kq[:], ALU.mult)
                        yield
                        j, bk = Pg.bank()
                        mm(bk[0:64, j, :], ones[0:64, 0:64], tC[:].re("p h t -> p (h t)"))
                        ts("dve", tC[:].re("p h t -> p (h t)"), bk[0:64, j, :], 1e-18, None, ALU.max)
                        yield
                        act(tC[:], tC[:], AF.Ln)
                        yield
                        act(tC[:], tC[:], AF.Exp, scale=-0.5)
                        yield
                        tt("dve", kk[:], kq[:], tC[:], ALU.mult)
                        yield
                        tt("dve", tC[:], aa[:], b4(22), ALU.mult)
                        yield
                        tt("dve", tC[:], tC[:], b4(40), ALU.add)
                        yield
                        tt("dve", kmod[:], zs_k[:], tC[:], ALU.mult)
                        yield
                        tt("pool", bbv[:], kk[:], aa[:], ALU.mult)
                        yield
                        stt(AR[:, :, 0, :], v8(kk), -1.0, v8(pprev), ALU.mult, ALU.mult)
                        yield
                        if True:
                            tt("dve", AR[:, :, 1, :], v8(zs_r), v8(pp), ALU.mult)
                            yield
                            tt("dve", rk[:], zs_r[:], b4(26), ALU.mult)
                            yield
                            tt("dve", rk[:], rk[:], kmod[:], ALU.mult)
                            yield
                        tt("dve", Bt[:], v8(bbv), v8(pinv), ALU.mult)
                        yield
                        tt("dve", Kt[:], v8(kmod), v8(pinv), ALU.mult)
                        yield
                        tt("pool", Bpf[:], v8(bbv), v8(pLp), ALU.mult)
                        yield
                        tt("pool", Kpf[:], v8(kmod), v8(pLp), ALU.mult)
                        yield
                if part in ("all", "B"):
                    if RW:
                        Xc = X[0]
                        for src, dst in ((AR, None), (Bpf, Bptm), (Kpf, Kptm)):
                            j, bk = Pg.bank()
                            for ch in range(8):
                                in_ = src[:, ch, 0, :] if src is AR else src[:, ch, :]
                                tr(bk.b16((slice(0, 64), j, slice(ch * 64, (ch + 1) * 64))), in_, identb[0:64, 0:64])
                            pview = bk.b16((slice(0, 64), j, slice(0, 512))).re("p (c d) -> p c d", c=8)
                            cp("act", Xc[:, :, 0:64] if dst is None else dst[:], pview)
                        yield
                        for lhs, dstG in ((Bt, Gb), (Kt, Gk)):
                            j, b2v = Pg.bank2()
                            for ch in range(8):
                                mm(b2v[0:64, ch // 4, (ch % 4) * 128:(ch % 4 + 1) * 128], lhs[:, ch, :], AR[:, ch, :, :].re("p w l -> p (w l)"))
                            tt("dve", dstG[:], b2v[0:64, :, :].re("p b (c n) -> p (b c) n", c=4),
                               mG[:].re("p (o n) -> p o n", o=1).bc([64, 8, 128]), ALU.mult)
                        yield
                        j, bk = Pg.bank()
                        for ch in range(8):
                            mm(bk[0:64, j, ch * 64:(ch + 1) * 64], AR[:, ch, 0, :], Bt[:, ch, :])
                        tt("dve", Pn[0][:], bk[0:64, j, :].re("p (c s) -> p c s", c=8),
                           mgt[0:64, 0:64].re("p (o s) -> p o s", o=1).bc([64, 8, 64]), ALU.mult)
                        yield
                        yield
                        j, bk = Pg.bank()
                        for ch in range(8):
                            h_, c_ = ch // 2, ch % 2
                            mm(bk[0:64, j, ch * 64:(ch + 1) * 64], Gk[:, ch, 0:64], Vcr[:, c_, h_ * 64:(h_ + 1) * 64])
                        cp("act", Xc[:, :, 64:128], bk[0:64, j, :].re("p (c d) -> p c d", c=8))
                        yield
                        Ptc = None
                        for lvl in range(6):
                            Xn = X[(lvl + 1) % 2]
                            yield
                            j, b2v = Pg.bank2()
                            for ch in range(8):
                                lh = Gb[:, ch, 0:64] if lvl == 0 else Ptc[:, ch, :]
                                o_ = b2v[0:64, ch // 4, (ch % 4) * 128:(ch % 4 + 1) * 128]
                                mm(o_, lh, Xc[:, ch, :], start=True, stop=False)
                                mm(o_, identb[0:64, 0:64], Xc[:, ch, :], start=False, stop=True)
                            cp("act", Xn[:], b2v[0:64, :, :].re("p b (c n) -> p (b c) n", c=4))
                            if lvl < 5:
                                Pnc, Pnn, Ptn = Pn[lvl % 2], Pn[(lvl + 1) % 2], Ptb[(lvl + 1) % 2]
                                yield
                                j, bk = Pg.bank()
                                for ch in range(8):
                                    lhT = Gb[:, ch, 0:64] if lvl == 0 else Ptc[:, ch, :]
                                    mm(bk[0:64, j, ch * 64:(ch + 1) * 64], Pnc[:, ch, :], lhT)
                                cp("act", Ptn[:], bk[0:64, j, :].re("p (c d) -> p c d", c=8))
                                yield
                                j, bk = Pg.bank()
                                for ch in range(8):
                                    lhT = Gb[:, ch, 0:64] if lvl == 0 else Ptc[:, ch, :]
                                    mm(bk[0:64, j, ch * 64:(ch + 1) * 64], lhT, Pnc[:, ch, :])
                                cp("act", Pnn[:], bk[0:64, j, :].re("p (c d) -> p c d", c=8))
                                Ptc = Ptn
                            Xc = Xn
                        yield
                        yield
                        j, bk = Pg.bank()
                        for ch in range(8):
                            mm(bk[0:64, j, ch * 64:(ch + 1) * 64], Xc[:, ch, 0:64], Bptm[:, ch, :])
                        cp("act", Phi[:], bk[0:64, j, :].re("p (c d) -> p c d", c=8))
                        yield
                        j, bk = Pg.bank()
                        for ch in range(8):
                            mm(bk[0:64, j, ch * 64:(ch + 1) * 64], Xc[:, ch, 0:64], Gb[:, ch, 64:128])
                        tt("dve", Rhat[:], bk[0:64, j, :].re("p (c d) -> p c d", c=8), AR[:, :, 1, :], ALU.add)
                        Pg.store(None, t_rp[ti], A[22].b16((slice(0, 64), slice(0, 1024))))
                        yield
                        yield
                        j, bk = Pg.bank()
                        for ch in range(8):
                            h_, c_ = ch // 2, ch % 2
                            o = bk[0:64, j, ch * 64:(ch + 1) * 64]
                            mm(o, Gb[:, ch, 64:128], Xc[:, ch, 64:128], start=True, stop=False)
                            mm(o, Gk[:, ch, 64:128], Vcr[:, c_, h_ * 64:(h_ + 1) * 64], start=False, stop=True)
                        cp("act", Yrw[:], bk[0:64, j, :].re("p (c d) -> p c d", c=8))
                        Pg.store(None, t_yh[ti], Yrw[:].re("p c d -> p (c d)"))
                        yield
                        yield
                        j, bk = Pg.bank()
                        for ch in range(8):
                            mm(bk[0:64, j, ch * 2:ch * 2 + 1], v8(rk)[:, ch, :], ones[0:64, 0:1])
                        cp("act", bon[:], bk[0:64, j, 0:16].re("p (c o) -> p c o", o=2)[:, :, 0])
                        tt("dve", BVt[:].re("p (h c) d -> p h c d", c=2), Vc[:].re("p c (h d) -> p h c d", h=4),
                           bon[:].re("p (h c o) -> p h c o", c=2, o=1).bc([64, 4, 2, 64]), ALU.mult)
                        Pg.store(None, t_bv[ti], BVt[:].re("p c d -> p (c d)"))
                        yield
                        for c in range(2):
                            j, bk = Pg.bank()
                            for h in range(4):
                                ch = h * 2 + c
                                o = bk[0:64, j, h * 64:(h + 1) * 64]
                                mm(o, Bptm[:, ch, :], Xc[:, ch, 64:128], start=True, stop=False)
                                mm(o, Kptm[:, ch, :], Vcr[:, c, h * 64:(h + 1) * 64], start=False, stop=True)
                            cp("act", psi[:, c, :], bk[0:64, j, 0:256])
                            cp("pool", pLt[:, c * 4:(c + 1) * 4], pp[:, :, c * 64 + 63])
                        Pg.store(None, t_ps[ti], psi[:].re("p c n -> p (c n)"))
                        Pg.store(None, t_pl[ti], pLt[:])
                        Rhat_u, Phi_u, yhat_u, BV_u, psi_u, pL_u = Rhat, Phi, Yrw, BVt, psi, pLt
                    else:
                        par_ = ti % 2
                        Pg.load(ld_h[0][par_], RPl[par_][:], t_rp[ti])
                        Pg.load(ld_h[1][par_], YHl[par_][:], t_yh[ti])
                        Pg.load(ld_h[2][par_], BVl[par_][:], t_bv[ti])
                        Pg.load(ld_h[3][par_], PSl[par_][:], t_ps[ti])
                        Pg.load(ld_h[4][par_], PLl[par_][:], t_pl[ti])
                        Phi_u = View(RPl[par_].ts, RPl[par_].ap[:, 0:512].rearrange("p (c d) -> p c d", c=8))
                        Rhat_u = View(RPl[par_].ts, RPl[par_].ap[:, 512:1024].rearrange("p (c d) -> p c d", c=8))
                        yhat_u = View(YHl[par_].ts, YHl[par_].ap.rearrange("p (c d) -> p c d", c=8))
                        BV_u = View(BVl[par_].ts, BVl[par_].ap.rearrange("p (c d) -> p c d", c=8))
                        psi_u = View(PSl[par_].ts, PSl[par_].ap.rearrange("p (c n) -> p c n", c=2))
                        pL_u = PLl[par_]
                    XS["BV_u"] = BV_u
                    yield
                    if P2:
                        jy, by = Pg.bank()
                    for c in range(2):
                        if P2:
                            for h in range(4):
                                ch = h * 2 + c
                                mm(by[0:64, jy, ch * 64:(ch + 1) * 64], Rhat_u[:, ch, :], Sr[:, h, 0:64])
                        yield
                        js, bs = Pg.bank()
                        for h in range(4):
                            ch = h * 2 + c
                            mm(bs[0:64, js, h * 128:h * 128 + SW], Phi_u[:, ch, :], Sr[:, h, :])
                        tt("dve", S32[:], S32[:], pL_u[:, c * 4:(c + 1) * 4].re("p (h o) -> p h o", o=1).bc([64, 4, SW]), ALU.mult)
                        tt("dve", S32[:], S32[:], bs[0:64, js, :].re("p (h n) -> p h n", h=4)[:, :, 0:SW], ALU.add)
                        tt("dve", S32[:, :, 0:64], S32[:, :, 0:64], psi_u[:, c, :].re("p (h d) -> p h d", h=4), ALU.add)
                        cp("act", Sr[:], S32[:])
                    if P2:
                        tt("dve", Yrw[:], by[0:64, jy, :].re("p (c d) -> p c d", c=8), yhat_u[:], ALU.add)

                if part in ("all", "M"):
                    yield
                    j, bk = tmproj(2560, 8)
                    tt("dve", gat[:], bk[:, j, 0:8], rowp[:, RM_IB:RM_IB + 8], ALU.add)
                    yield
                    cp("pool", ibs[:], gat[:, 0:4])
                    yield
                    act(nlf[:], gat[:, 4:8], AF.Exp, scale=-1.0)
                    yield
                    act(nlf[:], nlf[:], AF.Ln, bias=1.0)
                    yield
                    j, bk = Pg.bank()
                    mm(bk[:, j, 0:4], mle[:], nlf[:])
                    mm(bk[0:64, j, 8:12], ones[:, 0:64], nlf[:])
                    act(ebl[:], bk[:, j, 0:4], AF.Exp, scale=-1.0)
                    yield
                    act(etot[:], bk[0:64, j, 8:12], AF.Exp, scale=-1.0)
                    yield
                    tt("dve", NBT[:], NBT[:], bk[0:64, j, 8:12], ALU.add)
                    yield
                    for h in range(4):
                        ts("dve", Ah[:, h, :], mgt[:], nlf[:, h:h + 1], None, ALU.mult)
                    yield
                    j, bk = Pg.bank()
                    for h in range(4):
                        mm(bk[:, j, h * 128:(h + 1) * 128], Ah[:, h, :], mle[:])
                    for h in range(4):
                        act(Eh[:, h, :], bk[:, j, h * 128:(h + 1) * 128], AF.Exp, scale=-1.0, bias=ibs[:, h:h + 1])
                    yield
                    j, bk = tmproj(2048, 256)
                    cp("act", Vaug[:, :, 0:64], bk[:, j, 0:256].re("p (h d) -> p h d", h=4))
                    yield
                    for Zs, cdst, qi, outT in ((Zq, cq, 0, qT), (Zkm, ck, 1, kT)):
                        if not P2 and qi == 0:
                            continue
                        for tap in range(4):
                            wv = convw[:, qi, :, tap:tap + 1].bc([64, 4, 128])
                            src = Zs[:, :, 13 + tap:141 + tap]
                            if tap == 0:
                                tt("dve", cdst[:], src, wv, ALU.mult)
                                yield
                            else:
                                tt("pool", ctmp[:], src, wv, ALU.mult)
                                yield
                                tt("dve", cdst[:], cdst[:], ctmp[:], ALU.add)
                                yield
                        act(outT[:], cdst[:], AF.Silu)
                    yield
                    j, bk = Pg.bank()
                    for h in range(4):
                        tr(bk.b16((slice(0, 128), j, slice(h * 64, (h + 1) * 64))), kT[:, h, :], identb[0:64, 0:64])
                    tt("dve", ktw[:], bk.b16((slice(0, 128), j, slice(0, 256))).re("p (h d) -> p h d", h=4), Eh[:, :, 127:128].bc([128, 4, 64]), ALU.mult)
                    yield
                    if P2:
                        tt("pool", Em[:], Eh[:], mle[:].re("p (o n) -> p o n", o=1).bc([128, 4, 128]), ALU.mult)
                        yield
                        j, bk = Pg.bank()
                        for h in range(4):
                            mm(bk[:, j, h * 128:(h + 1) * 128], kT[:, h, :], qT[:, h, :])
                        tt("dve", PTm[:], bk[:, j, :].re("p (h l) -> p h l", h=4), Em[:], ALU.mult)
                        yield
                        ji, bki = Pg.bank()
                        yield
                        jn, bkn = Pg.bank()
                        for h in range(4):
                            mm(bki[:, ji, h * 65:(h + 1) * 65], qT[:, h, :], Cb[:, h, :])
                        for h in range(4):
                            mm(bkn[:, jn, h * 65:(h + 1) * 65], PTm[:, h, :], Vaug[:, h, :])
                        tt("dve", numi[:], bki[:, ji, 0:260].re("p (h d) -> p h d", h=4), ebl[:].re("p (h o) -> p h o", o=1).bc([128, 4, 65]), ALU.mult)
                        yield
                        tt("dve", numt[:], numi[:], bkn[:, jn, 0:260].re("p (h d) -> p h d", h=4), ALU.add)
                        yield
                        act(dden[:], numt[:, :, 64], AF.Abs)
                        yield
                        ts("dve", dden[:], dden[:], 1.0, None, ALU.max)
                        yield
                        Pg.recip(dden[:], dden[:])
                        yield
                        tt("dve", Yml[:], numt[:, :, 0:64], dden[:].re("p (h o) -> p h o", o=1).bc([128, 4, 64]), ALU.mult)
                    yield
                    j, bk = Pg.bank()
                    for h in range(4):
                        mm(bk[0:64, j, h * 65:(h + 1) * 65], ktw[:, h, :], Vaug[:, h, :])
                    tt("dve", ctmp2[:], C32[:], etot[:].re("p (h o) -> p h o", o=1).bc([64, 4, 65]), ALU.mult)
                    yield
                    tt("dve", C32[:], ctmp2[:], bk[0:64, j, 0:260].re("p (h d) -> p h d", h=4), ALU.add)
                    yield
                    cp("act", Cb[:], C32[:])
                    yield

                    yield
                if part in ("all", "R"):
                    yield
                    j, bk = tmproj(2304, 256)
                    if P2:
                        cp("act", vrt[:], bk[:, j, 0:256].re("p (h d) -> p h d", h=4))
                    tt("dve", vz[:], bk[:, j, 0:256].re("p (h d) -> p h d", h=4), zx[:, 0:4].re("p (h o) -> p h o", o=1).bc([128, 4, 64]), ALU.mult)
                    yield
                    j, bk = tmproj(1536, 512)
                    zv = bk[:, j, :].re("p (q w i) -> p q w i", q=8, w=2)
                    csv = cst[ti % 2]
                    csA = csv[:, 0:64].re("p (o w i) -> p o w i", o=1, w=2).bc([128, 8, 2, 32])
                    csB = csv[:, 64:128].re("p (o w i) -> p o w i", o=1, w=2).bc([128, 8, 2, 32])
                    tt("dve", rA[:], zv[:, :, 0:1, :].bc([128, 8, 2, 32]), csA, ALU.mult)
                    yield
                    tt("dve", rB[:], zv[:, :, 1:2, :].bc([128, 8, 2, 32]), csB, ALU.mult)
                    yield
                    tt("dve", qkr[:].re("p q (w i) -> p q w i", w=2), rA[:], rB[:], ALU.add)
                    yield
                    if P2:
                        j, bk = Pg.bank()
                        for q_ in range(8):
                            tr(bk.b16((slice(0, 64), j, slice(q_ * 128, (q_ + 1) * 128))), qkr[:, q_, :], identb[:])
                        cp("act", qkT[:], bk.b16((slice(0, 64), j, slice(0, 1024))).re("p (q t) -> p q t", q=8))
                        yield
                        j, bk = Pg.bank()
                        for h in range(4):
                            mm(bk[:, j, h * 128:(h + 1) * 128], qkT[:, 4 + h, :], qkT[:, h, :])
                        tt("dve", PTr[:], bk[:, j, :].re("p (h l) -> p h l", h=4), dect[:], ALU.mult)
                        yield
                        ji, bki = Pg.bank()
                        yield
                        jn, bkn = Pg.bank()
                        for h in range(4):
                            mm(bki[:, ji, h * 64:(h + 1) * 64], qkT[:, h, :], Rb[:, h, :])
                        for h in range(4):
                            mm(bkn[:, jn, h * 64:(h + 1) * 64], PTr[:, h, :], vrt[:, h, :])
                        tt("dve", yint[:], bki[:, ji, 0:256].re("p (h d) -> p h d", h=4), zx[:, 4:8].re("p (h o) -> p h o", o=1).bc([128, 4, 64]), ALU.mult)
                        yield
                        tt("dve", Yrt[:], yint[:], bkn[:, jn, 0:256].re("p (h d) -> p h d", h=4), ALU.add)
                    yield
                    j, bk = Pg.bank()
                    for h in range(4):
                        mm(bk[0:64, j, h * 64:(h + 1) * 64], qkr[:, 4 + h, :], vz[:, h, :])
                    tt("dve", rtmp[:], R32[:], gb[:], ALU.mult)
                    yield
                    tt("dve", R32[:], rtmp[:], bk[0:64, j, 0:256].re("p (h d) -> p h d", h=4), ALU.add)
                    yield
                    cp("act", Rb[:], R32[:])
                    yield


                    yield
                if P2 and part in ("all", "P"):
                    tt("dve", s2[:, :, 1:144], Zu[:, :, 1:144], Zu[:, :, 0:143], ALU.add)
                    yield
                    tt("dve", s4[:, :, 3:144], s2[:, :, 3:144], s2[:, :, 1:142], ALU.add)
                    yield
                    tt("dve", s8[:, 7:144], s4[:, 1, 7:144], s4[:, 1, 3:140], ALU.add)
                    yield
                    tt("dve", s16[64:128, 15:144], s8[64:128, 15:144], s8[64:128, 7:136], ALU.add)
                    yield
                    sels = ((s2[0:64, 0, 16:144], 0, 0, 0.5), (s4[64:128, 0, 16:144], 64, 0, 0.25),
                            (s8[0:64, 16:144], 0, 1, 0.125), (s16[64:128, 16:144], 64, 1, 0.0625))
                    for sw_, p0, m_, iw in sels:
                        if ti == 0:
                            tt("dve", pm[p0:p0 + 64, m_, :], sw_, cnt0[p0:p0 + 64, m_, :], ALU.mult)
                            yield
                            tt("dve", pd[p0:p0 + 64, m_, :], pm[p0:p0 + 64, m_, :], Zu[p0:p0 + 64, m_, 16:144], ALU.subtract)
                            yield
                        else:
                            stt(pd[p0:p0 + 64, m_, :], sw_, iw, Zu[p0:p0 + 64, m_, 16:144], ALU.mult, ALU.subtract)
                    yield
                    jpl, bpl = Pg.bank()
                    for m_ in range(2):
                        mm(bpl[:, jpl, m_ * 128:(m_ + 1) * 128], pd[:, m_, :], plwb[:, m_, :])

                    yield
                    XS["bpl"] = (jpl, bpl)
                if P2 and part in ("all", "T"):
                    jpl, bpl = XS["bpl"]
                    BV_u = XS["BV_u"]
                    headnorm(Yrw[:], 64, 8, RW_EPS)
                    lnw = rowp[0:64, R_LNW:R_LNW + 256].re("p (h o d) -> p h o d", h=4, o=1).bc([64, 4, 2, 64])
                    lnb = rowp[0:64, R_LNB:R_LNB + 256].re("p (h o d) -> p h o d", h=4, o=1).bc([64, 4, 2, 64])
                    Y4 = Yrw[:].re("p (h c) d -> p h c d", c=2)
                    tt("dve", Y4, Y4, lnw, ALU.mult)
                    yield
                    tt("dve", Y4, Y4, lnb, ALU.add)
                    yield
                    tt("dve", Yrw[:], Yrw[:], BV_u[:], ALU.add)
                    yield
                    tt("dve", mixa[:].re("p c (h d) -> p h c d", h=4), Y4, grw[:].re("p c (h d) -> p h c d", h=4), ALU.mult)
                    yield
                    j, bk = tmproj(512, 512)
                    act(g2[:], bk[:, j, :], AF.Silu)
                    yield
                    tt("dve", g2[:, 0:256], g2[:, 0:256], rowp[:, R_PLS:R_PLS + 256], ALU.mult)
                    yield
                    tt("dve", mixb[:, 0:256], bpl[:, jpl, 0:256], g2[:, 0:256], ALU.mult)
                    yield
                    j, bk = tmproj(1024, 512)
                    act(g3[:], bk[:, j, 0:256], AF.Silu)
                    yield
                    act(osg[:], bk[:, j, 256:512], AF.Sigmoid)
                    yield
                    tt("dve", Yml[:], Yml[:], osg[:].re("p (h d) -> p h d", h=4), ALU.mult)
                    yield
                    headnorm(Yml[:], 128, 4, HN_EPS)
                    tt("dve", g2[:, 256:512], g2[:, 256:512], rowp[:, R_MLLN:R_MLLN + 256], ALU.mult)
                    yield
                    tt("dve", mixb[:, 256:512], Yml[:].re("p h d -> p (h d)"), g2[:, 256:512], ALU.mult)
                    yield
                    headnorm(Yrt[:], 128, 4, HN_EPS)
                    tt("dve", mixb[:, 512:768], Yrt[:].re("p h d -> p (h d)"), g3[:], ALU.mult)
                    yield
                    yield
                    j, bk = Pg.bank()
                    for c in range(2):
                        for k in range(2):
                            tr(bk.b16((slice(0, 128), j, slice(k * 128 + c * 64, k * 128 + (c + 1) * 64))), mixa[:, c, k * 128:(k + 1) * 128], identb[0:64, 0:64])
                    for k in range(2, 8):
                        tr(bk.b16((slice(0, 128), j, slice(k * 128, (k + 1) * 128))), mixb[:, (k - 2) * 128:(k - 1) * 128], identb[:])
                    cp("act", mixT[:], bk.b16((slice(0, 128), j, slice(0, 1024))).re("p (k t) -> p k t", k=8))
                    yield
                    Pg.store(st_sm, t_mix[ti].re("p (k t) -> p k t", k=8), mixT[:])


            ALLB = list(range(8))
            XS = {}

            def run_all(g):
                for _ in g:
                    pass

            def par(items):
                items = list(items)
                while items:
                    for it in list(items):
                        try:
                            for _ in range(it[2]):
                                Pg.set_pool(it[0])
                                next(it[1])
                        except StopIteration:
                            items.remove(it)
                        yield

            def seq(pool, *gens):
                for g in gens:
                    for _ in g:
                        Pg.set_pool(pool)
                        yield

            if not PIPE:
                for ti in range(NT):
                    Pg.set_pool(ALLB)
                    run_all(tile_body(ti, "all"))
            elif not RW:
                for ti in range(NT):
                    Pg.set_pool(ALLB)
                    run_all(tile_body(ti, "A0"))
                    run_all(par([([0, 1, 2], tile_body(ti, "B"), 1), ([3, 4], tile_body(ti, "M"), 2),
                                 ([5, 6], tile_body(ti, "R"), 1), ([7], tile_body(ti, "P"), 1)]))
                    Pg.set_pool([0, 1, 2, 3, 4, 5, 6])
                    run_all(tile_body(ti, "T"))
                Pg.set_pool(ALLB)
            else:
                PA, PM, PR, PB = [0, 1], [2], [3], [4, 5, 6, 7]

                def stageA(ti):
                    Pg.set_pool(PA)
                    for _ in tile_body(ti, "A0"):
                        Pg.set_pool(PA)
                        yield
                    for _ in par([(PA, tile_body(ti, "A1"), 2), (PM, tile_body(ti, "M"), 1), (PR, tile_body(ti, "R"), 1)]):
                        yield
                run_all(stageA(0))
                for ti in range(NT):
                    items = [(PB, tile_body(ti, "B"), 1)]
                    if ti + 1 < NT:
                        items.append((PA, stageA(ti + 1), 4))
                    run_all(par(items))
                Pg.set_pool(ALLB)

            if not P2:
                Pg.store(None, d_osum[:, 0:512].rearrange("p (h n) -> p h n", h=4), S32[:])
                Pg.store(None, d_osum[:, 512:772].rearrange("p (h n) -> p h n", h=4), C32[:])
                Pg.store(None, d_osum[:, 772:776], NBT[:])
                Pg.store(None, d_osum[:, 776:1032].rearrange("p (h n) -> p h n", h=4), R32[:])
            S.barrier()
            S.emit()

        if P2:
            with contextlib.ExitStack() as stE:
                S.stack = stE
                S.barrier()
                wout = sb([128, 8, D], BF16, "wout")
                pleg = sb([128, 8, D], BF16, "pleg")
                plew = sb([128, 2, D], BF16, "plew")
                lng = sb([128, D], F32, "lng")
                lnb_ = sb([128, D], F32, "lnb")
                stg = [sb([128, D], F32, "stg0"), sb([128, D], F32, "stg1"), sb([128, D], F32, "stg2")]
                Pg.load(None, lng[:], d_row[R_LNG:R_LNG + D].partition_broadcast(128))
                Pg.load(None, lnb_[:], d_row[R_LNBB:R_LNBB + D].partition_broadcast(128))
                si = 0
                cast_engs = ["dve", "pool", "act"]
                for dst, dram, kc in ((wout, d_wout, 8), (pleg, d_pleg, 8), (plew, d_plew, 2)):
                    for k in range(kc):
                        sg = stg[si % 3]
                        Pg.load(None, sg[:], dram[k * 128:(k + 1) * 128, :])
                        cp(cast_engs[si % 3], dst[:, k, :], sg[:])
                        si += 1
                xte = [sb([128, D], F32, "xte0"), sb([128, D], F32, "xte1")]
                xTe = [sb([128, 8, 128], BF16, "xTe0"), sb([128, 8, 128], BF16, "xTe1")]
                mxe = [sb([128, 8, 128], BF16, "mxe0"), sb([128, 8, 128], BF16, "mxe1")]
                pte = [sb([128, 256], F32, "pte0"), sb([128, 256], F32, "pte1")]
                pT = sb([128, 2, 128], BF16, "pT")
                sgp = sb([128, D], F32, "sgp")
                acc = [sb([128, D], F32, "acc0"), sb([128, D], F32, "acc1")]
                bst = sb([128, 2, 6], F32, "bst")
                mv = sb([128, 2], F32, "mv")
                rstd = sb([128, 1], F32, "rstd")
                for ti in range(NT):
                    par = ti % 2
                    Pg.load(ld_x[par], xte[par][:], d_xh[HALO + ti * TT: HALO + (ti + 1) * TT, :])
                    Pg.load(ld_t[par], xTe[par][:], t_xTs[ti].re("p (k t) -> p k t", k=8), q="act")
                    Pg.load(ld_m[par], mxe[par][:], t_mix[ti].re("p (k t) -> p k t", k=8), q="pool")
                    Pg.load(ld_p[par], pte[par][:], d_p[ti * TT:(ti + 1) * TT, :])
                    j, bk = Pg.bank()
                    for k in range(2):
                        tr(bk[:, j, k * 128:(k + 1) * 128], pte[par][:, k * 128:(k + 1) * 128], ident[:])
                    cp("act", pT[:], bk[:, j, 0:256].re("p (k t) -> p k t", k=2))
                    ac = acc[par]
                    for half in range(2):
                        cs_ = slice(half * 512, (half + 1) * 512)
                        jg, bg = Pg.bank()
                        for k in range(8):
                            mm(bg[:, jg, :], xTe[par][:, k, :], pleg[:, k, cs_], start=(k == 0), stop=(k == 7))
                        act(sgp[:, cs_], bg[:, jg, :], AF.Sigmoid)
                        jw, bw = Pg.bank()
                        for k in range(2):
                            mm(bw[:, jw, :], pT[:, k, :], plew[:, k, cs_], start=(k == 0), stop=(k == 1))
                        tt("dve", sgp[:, cs_], sgp[:, cs_], bw[:, jw, :], ALU.mult)
                        jo, bo = Pg.bank()
                        for k in range(8):
                            mm(bo[:, jo, :], mxe[par][:, k, :], wout[:, k, cs_], start=(k == 0), stop=(k == 7))
                        stt(ac[:, cs_], xte[par][:, cs_], ALPHA, bo[:, jo, :], ALU.mult, ALU.add)
                        tt("pool", ac[:, cs_], ac[:, cs_], sgp[:, cs_], ALU.add)
                        Pg.bn_stats(bst[:, half, :], ac[:, cs_])
                    Pg.bn_aggr(mv[:], bst[:].re("p a b -> p (a b)"))
                    ts("dve", rstd[:], mv[:, 1:2], LN_EPS, None, ALU.add)
                    act(rstd[:], rstd[:], AF.Ln)
                    act(rstd[:], rstd[:], AF.Exp, scale=-0.5)
                    ts("dve", ac[:], ac[:], mv[:, 0:1], rstd[:, 0:1], ALU.subtract, ALU.mult)
                    tt("pool", ac[:], ac[:], lng[:], ALU.mult)
                    tt("dve", ac[:], ac[:], lnb_[:], ALU.add)
                    out_toks.append(Pg.store(st_o[par], d_out[ti * TT:(ti + 1) * TT, :], ac[:]))
                    if dr.get("tail") is not None and ti == NT - 1:
                        Pg.store(None, dr["tail"], ac[TT - HALO:TT, :])
                S.barrier()
                S.emit()
        S.stack = S.base


def _perm_rot(n_heads=4, hd=64):
    idx = []
    for h in range(n_heads):
        idx += [h * hd + 2 * i for i in range(hd // 2)] + [h * hd + 2 * i + 1 for i in range(hd // 2)]
    return np.array(idx)


def pack_layer(L, w_in, rw_mu, rw_w0, rw_w2, rw_a0, rw_a2, rw_kk, rw_ka, rw_rk, rw_ln_w, rw_ln_b, pl_w, pl_scale,
               ml_conv, ml_ib, ml_fb, ml_ln_w, ln_g, ln_b):
    W = np.asarray(w_in[L], np.float32)
    o = np.cumsum([0, 896, 256, 256, 256, 512, 256, 4, 4, 256, 256, 256, 256, 256, 256])
    rw_in, rw_g, pl_u, pl_g, ml_qk, ml_v, ml_i, ml_f, ml_o, ml_g, rt_q, rt_k, rt_v, rt_g = [W[:, o[i]:o[i + 1]] for i in range(14)]
    pr = _perm_rot()
    wfm = np.concatenate([rw_in[:, 0:256], rw_in[:, 256:512], rw_in[:, 768:896], pl_u, ml_qk[:, 0:256], ml_qk[:, 256:512]], axis=1)
    wtm = np.concatenate([rw_in[:, 512:768], rw_g, pl_g, ml_g, rt_g, ml_o, rt_q[:, pr], rt_k[:, pr], ml_v, rt_v, ml_i, ml_f], axis=1)
    assert wfm.shape[1] == FMC and wtm.shape[1] == TMC
    hT = lambda v: np.asarray(v, np.float32).reshape(4, 64).T
    mu = np.asarray(rw_mu[L], np.float32)
    rwp = np.concatenate([hT(mu[0:256]), hT(mu[256:512]), mu[768:832, None], mu[832:896, None],
                          hT(rw_w0[L]), hT(rw_a0[L]), hT(rw_kk[L]), hT(rw_ka[L]), hT(rw_rk[L])], axis=1)
    assert rwp.shape == (64, 30)
    rowp = np.zeros(4360, np.float32)
    rowp[0:256] = mu[512:768]
    rowp[256:512] = rw_ln_w[L]
    rowp[512:768] = rw_ln_b[L]
    rowp[768:1024] = pl_scale[L]
    rowp[1024:1280] = ml_ln_w[L]
    rowp[1280:2304] = ln_g[L]
    rowp[2304:3328] = ln_b[L]
    rowp[3328:3332] = ml_ib[L]
    rowp[3332:3336] = ml_fb[L]
    cw = np.asarray(ml_conv[L], np.float32)
    convw = cw.T.reshape(2, 4, 64, 4).transpose(2, 0, 1, 3)
    plw = np.zeros((128, 2, 128), np.float32)
    pw = np.asarray(pl_w[L], np.float32)
    for g in range(4):
        m, hh = g // 2, g % 2
        plw[hh * 64:(hh + 1) * 64, m, hh * 64:(hh + 1) * 64] = pw[g]
    return dict(wfm=np.ascontiguousarray(wfm), wtm=np.ascontiguousarray(wtm), rwp=np.ascontiguousarray(rwp),
                w2=np.asarray(rw_w2[L], np.float32), a2=np.asarray(rw_a2[L], np.float32), rowp=rowp,
                convw=np.ascontiguousarray(convw), plw=plw)


def const_tables(seg, NT):
    f = np.float32
    a = np.arange(128)
    ident = np.eye(128, dtype=f)
    mle = (a[:, None] <= a[None, :]).astype(f)
    mgt = (a[:, None] > a[None, :]).astype(f)
    a64 = np.arange(64)
    mG = np.concatenate([(a64[:, None] < a64[None, :]), (a64[:, None] <= a64[None, :])], axis=1).astype(f)
    gam = 1.0 - np.exp2(-5.0 - np.arange(4))
    lg = np.log(gam)
    rel = a[None, :] - a[:, None]
    dect = np.zeros((128, 4, 128), f)
    for h in range(4):
        dect[:, h, :] = np.where(rel >= 0, np.exp(lg[h] * np.maximum(rel, 0)), 0.0) * 0.125
    zx = np.zeros((128, 8), f)
    for h in range(4):
        zx[:, h] = np.exp(lg[h] * (127 - a)) * 0.125
        zx[:, 4 + h] = np.exp(lg[h] * (a + 1.0))
    gchunk = np.zeros((64, 4, 64), f)
    for h in range(4):
        gchunk[:, h, :] = np.exp(lg[h] * 128)
    pos = (seg * SEG + np.arange(NT * TT)).astype(np.float32)
    inv_freq = (np.float32(10000.0) ** (-np.arange(0, 64, 2, dtype=np.float32) / np.float32(64))).astype(np.float32)
    ang = pos[:, None] * inv_freq[None, :]
    c, s = np.cos(ang).astype(f), np.sin(ang).astype(f)
    cs2 = np.zeros((NT * TT, 2, 2, 32), f)
    cs2[:, 0, 0], cs2[:, 0, 1] = c, s
    cs2[:, 1, 0], cs2[:, 1, 1] = -s, c
    cnt0 = np.zeros((128, 2, 128), f)
    wins = (2, 4, 8, 16)
    for g in range(4):
        m, hh = g // 2, g % 2
        if seg == 0:
            cnt0[hh * 64:(hh + 1) * 64, m, :] = 1.0 / np.minimum(a + 1.0, float(wins[g]))[None, :]
        else:
            cnt0[hh * 64:(hh + 1) * 64, m, :] = 1.0 / wins[g]
    return dict(ident=ident, mask_le=mle, mask_gt=mgt, mask_g=mG, dect=dect, zetaxi=zx, gchunk=gchunk,
                cs2=np.ascontiguousarray(cs2.reshape(NT * TT, 128)), cnt0=cnt0)


def seg_consts(seg):
    f = np.float32
    gam = 1.0 - np.exp2(-5.0 - np.arange(4))
    gseg = np.zeros((64, 4, 64), f)
    for h in range(4):
        gseg[:, h, :] = np.exp(np.log(gam[h]) * SEG)
    msk = np.zeros((64, NSEG), f)
    msk[:, :seg] = 1.0
    return dict(gseg=gseg, seg_mask=msk)


SUMW = 1032
WNAMES = ("wfm", "wtm", "rwp", "w2", "a2", "rowp", "convw", "plw")
WSHAPES = dict(wfm=[D, FMC], wtm=[D, TMC], rwp=[64, 30], w2=[64, G], a2=[64, G], rowp=[4360], convw=[64, 2, 4, 4], plw=[128, 2, 128])
CSHAPES = lambda NT: dict(ident=[128, 128], mask_le=[128, 128], mask_gt=[128, 128], mask_g=[64, 128], dect=[128, 4, 128],
                          zetaxi=[128, 8], gchunk=[64, 4, 64], cs2=[NT * TT, 128], cnt0=[128, 2, 128])


def _declare(nc, NT, suffixes, NR, fused):
    def din(name, shape):
        return nc.dram_tensor(name, list(shape), F32, kind="ExternalInput").ap()
    dr = {}
    for k, shp in CSHAPES(NT).items():
        dr[k] = din(k, shp)
    dr["gseg"] = din("gseg", [64, 4, 64])
    dr["seg_mask"] = din("seg_mask", [64, NR])
    per = {}
    for sfx in suffixes:
        d = {}
        for k in WNAMES:
            d[k] = din(k + sfx, WSHAPES[k])
        d["w_out"] = din("w_out" + sfx, [D, D])
        d["ple_w"] = din("ple_w" + sfx, [256, D])
        d["ple_gate"] = din("ple_gate" + sfx, [D, D])
        d["p"] = din("p" + sfx, [NT * TT, 256])
        per[sfx] = d
    return dr, per


def build_fused(NT, NL=2):
    nc = bass.Bass("TRN2", target_bir_lowering=False)
    NR = 8
    groups = [list(range(NR))]
    dr, per = _declare(nc, NT, [f"_{L}" for L in range(NL)], NR, True)
    xh0 = nc.dram_tensor("xh", [HALO + NT * TT, D], F32, kind="ExternalInput").ap()
    pred_oh = nc.dram_tensor("pred_oh", [HALO, NR], F32, kind="ExternalInput").ap()
    xout = nc.dram_tensor("xout", [NT * TT, D], F32, kind="ExternalOutput").ap()
    x1 = nc.dram_tensor("x1_scratch", [HALO + NT * TT, D], F32, kind="Internal").ap()
    dr["t_mix"] = T(nc.dram_tensor("mix_scratch", [NT, 128, 1024], BF16, kind="Internal").ap(), "mix_scratch")
    dr["t_xTs"] = T(nc.dram_tensor("xT_scratch", [NT, 128, 1024], BF16, kind="Internal").ap(), "xT_scratch")
    dr["t_rp"] = T(nc.dram_tensor("h_rp", [NT, 64, 1024], BF16, kind="Internal").ap(), "h_rp")
    dr["t_yh"] = T(nc.dram_tensor("h_yh", [NT, 64, 512], F32, kind="Internal").ap(), "h_yh")
    dr["t_bv"] = T(nc.dram_tensor("h_bv", [NT, 64, 512], F32, kind="Internal").ap(), "h_bv")
    dr["t_ps"] = T(nc.dram_tensor("h_ps", [NT, 64, 512], F32, kind="Internal").ap(), "h_ps")
    dr["t_pl"] = T(nc.dram_tensor("h_pl", [NT, 64, 8], F32, kind="Internal").ap(), "h_pl")
    sum_src = [nc.dram_tensor(f"sum_src{L}", [64, SUMW], F32) for L in range(NL)]
    sum_all = [nc.dram_tensor(f"sum_all{L}", [NR * 64, SUMW], F32) for L in range(NL)]
    tail_src = nc.dram_tensor("tail_src", [HALO, D], F32)
    tail_all = nc.dram_tensor("tail_all", [NR * HALO, D], F32)
    with contextlib.ExitStack() as st:
        Pg = Prog(nc, st)
        S = Pg.S
        ident = S.sbuf([128, 128], F32, "ident")
        identb = S.sbuf([128, 128], BF16, "identb")
        Pg.load(None, ident[:], dr["ident"])
        Pg.cp("dve", identb[:], ident[:])
        for L in range(NL):
            d = dict(dr)
            d.update(per[f"_{L}"])
            d["xh"] = xh0 if L == 0 else x1
            d["o_sum"] = sum_src[L].ap()
            emit_pass(Pg, ident, identb, "P1", NT, d, NR)
            S.new_phase()
            tsrc, tdst = T(None, "ccsrc"), T(None, "ccdst")
            S.collective_allgather(tsrc, tdst, sum_src[L].ap().opt(), sum_all[L].ap().opt(), groups)
            S.barrier()
            S.emit()
            S.new_phase()
            d["sum_all"] = sum_all[L].ap()
            if L == 0 and NL > 1:
                d["xout"] = x1[HALO:, :]
                d["tail"] = tail_src.ap()
            else:
                d["xout"] = xout
                d["tail"] = None
            emit_pass(Pg, ident, identb, "P2", NT, d, NR)
            S.new_phase()
            if L == 0 and NL > 1:
                tsrc, tdst = T(None, "ccsrc"), T(None, "ccdst")
                S.collective_allgather(tsrc, tdst, tail_src.ap().opt(), tail_all.ap().opt(), groups)
                with contextlib.ExitStack() as stH:
                    S.stack = stH
                    S.barrier()
                    tl = S.sbuf([HALO, NR, D], F32, "tl")
                    oh = S.sbuf([HALO, NR], F32, "oh")
                    hl = S.sbuf([HALO, D], F32, "hl")
                    Pg.load(None, tl[:], tail_all.ap().rearrange("(j p) d -> p j d", p=HALO))
                    Pg.load(None, oh[:], pred_oh)
                    Pg.ts("dve", hl[:], tl[:, 0, :], oh[:, 0:1], None, ALU.mult)
                    for j in range(1, NR):
                        Pg.stt(hl[:], tl[:, j, :], oh[:, j:j + 1], hl[:], ALU.mult, ALU.add)
                    Pg.store(None, x1[0:HALO, :], hl[:])
                    S.barrier()
                    S.emit()
                S.stack = S.base
                S.new_phase()
    return nc


_PROGS = {}


def _prog(mode, NT):
    key = (mode, NT)
    if key not in _PROGS:
        _PROGS[key] = build_fused(NT, 2 if mode == "F" else 1)
    return _PROGS[key]


def _layer_inputs(L, args):
    (w_in, rw_mu, rw_w0, rw_w2, rw_a0, rw_a2, rw_kk, rw_ka, rw_rk, rw_ln_w, rw_ln_b, pl_w, pl_scale,
     ml_conv, ml_ib, ml_fb, ml_ln_w, w_out, ple_w, ple_gate, ln_g, ln_b) = args
    pk = pack_layer(L, w_in, rw_mu, rw_w0, rw_w2, rw_a0, rw_a2, rw_kk, rw_ka, rw_rk, rw_ln_w, rw_ln_b, pl_w, pl_scale,
                    ml_conv, ml_ib, ml_fb, ml_ln_w, ln_g, ln_b)
    pk["w_out"] = np.ascontiguousarray(np.asarray(w_out[L], np.float32))
    pk["ple_w"] = np.ascontiguousarray(np.asarray(ple_w[L], np.float32))
    pk["ple_gate"] = np.ascontiguousarray(np.asarray(ple_gate[L], np.float32))
    return pk


def _xh(cur, c):
    b, sg = c // NSEG, c % NSEG
    xh = np.zeros((HALO + SEG, D), np.float32)
    xh[HALO:] = cur[b, sg * SEG:(sg + 1) * SEG]
    if sg > 0:
        xh[:HALO] = cur[b, sg * SEG - HALO: sg * SEG]
    return xh


def _core_inputs(c, NT, pks, p, xcur, layers):
    b, sg = c // NSEG, c % NSEG
    ncores = 2 * NSEG
    im = {}
    im.update(const_tables(sg, NT))
    im["gseg"] = seg_consts(sg)["gseg"]
    msk = np.zeros((64, ncores), np.float32)
    msk[:, b * NSEG: c] = 1.0
    im["seg_mask"] = msk
    oh = np.zeros((HALO, ncores), np.float32)
    if sg > 0:
        oh[:, c - 1] = 1.0
    im["pred_oh"] = oh
    for i, L in enumerate(layers):
        for k, v in pks[L].items():
            im[f"{k}_{i}"] = v
        im[f"p_{i}"] = np.ascontiguousarray(p[L, b, sg * SEG:(sg + 1) * SEG])
    im["xh"] = _xh(xcur, c)
    return im


def kernel_two_launch(x, p, *args):
    x = np.asarray(x, np.float32)
    p = np.asarray(p, np.float32)
    NT = SEG // TT
    ncores = 2 * NSEG
    pks = [_layer_inputs(L, args) for L in range(2)]
    cur = x
    for L in range(2):
        ims = [_core_inputs(c, NT, pks, p, cur, [L]) for c in range(ncores)]
        res = run_bass_kernel_spmd(_prog("L", NT), ims, core_ids=list(range(ncores))).results
        nxt = np.empty_like(cur)
        for c in range(ncores):
            nxt[c // NSEG, (c % NSEG) * SEG:(c % NSEG + 1) * SEG] = res[c]["xout"]
        cur = nxt
    return cur


def kernel(x, p, w_in, rw_mu, rw_w0, rw_w2, rw_a0, rw_a2, rw_kk, rw_ka, rw_rk, rw_ln_w, rw_ln_b, pl_w, pl_scale,
           ml_conv, ml_ib, ml_fb, ml_ln_w, w_out, ple_w, ple_gate, ln_g, ln_b):
    args = (w_in, rw_mu, rw_w0, rw_w2, rw_a0, rw_a2, rw_kk, rw_ka, rw_rk, rw_ln_w, rw_ln_b, pl_w, pl_scale,
            ml_conv, ml_ib, ml_fb, ml_ln_w, w_out, ple_w, ple_gate, ln_g, ln_b)
    x = np.asarray(x, np.float32)
    p = np.asarray(p, np.float32)
    NT = SEG // TT
    ncores = 2 * NSEG
    pks = [_layer_inputs(L, args) for L in range(2)]
    ims = [_core_inputs(c, NT, pks, p, x, [0, 1]) for c in range(ncores)]
    res = run_bass_kernel_spmd(_prog("F", NT), ims, core_ids=list(range(ncores))).results
    out = np.empty_like(x)
    for c in range(ncores):
        out[c // NSEG, (c % NSEG) * SEG:(c % NSEG + 1) * SEG] = res[c]["xout"]
    return out
```
